# Optimizing a Trainium2 kernel written in Bass

```python
import math
import jax, jax.numpy as jnp
from jax import lax
import numpy as np

D_MODEL = 4096
BATCH = 1
SEQ = 16384
DEPTH = 2

N_A = DEPTH // 2
N_B = DEPTH - N_A

POOL_WINDOWS = (2, 4, 8, 16)
N_POOL_GROUPS = len(POOL_WINDOWS)
POOL_GROUP_DIM = D_MODEL // N_POOL_GROUPS

QK_NOPE_DIM = 128
QK_ROPE_DIM = 64
V_HEAD_DIM = 128
N_HEADS = D_MODEL // V_HEAD_DIM
QK_HEAD_DIM = QK_NOPE_DIM + QK_ROPE_DIM
Q_LORA_RANK = 1024
KV_LORA_RANK = 512
ROPE_THETA = 10000.0
Q_BLOCK = 128

D_FF = 11008
CONV_WIDTH = 3

EPS = 1e-6

kernel_name = "yoco_pool_mla_convffn_trunk"


def rmsnorm(x, g):
    xf = x.astype(jnp.float32)
    y = xf * lax.rsqrt(jnp.mean(xf * xf, axis=-1, keepdims=True) + EPS)
    return (y * g.astype(jnp.float32)).astype(x.dtype)


def rope_tables(seq_len, dim, dtype):
    inv_freq = ROPE_THETA ** (-jnp.arange(0, dim, 2, dtype=jnp.float32) / dim)
    ang = jnp.arange(seq_len, dtype=jnp.float32)[:, None] * inv_freq[None, :]
    return jnp.cos(ang).astype(dtype), jnp.sin(ang).astype(dtype)


def apply_rope(x, cos, sin):
    x1, x2 = jnp.split(x, 2, axis=-1)
    return jnp.concatenate([x1 * cos - x2 * sin, x2 * cos + x1 * sin], axis=-1)


def multiscale_pool_mixer(xn, w_groups, scale):
    B, S, D = xn.shape
    xg = xn.astype(jnp.float32).reshape(B, S, N_POOL_GROUPS, POOL_GROUP_DIM)
    csum = jnp.concatenate(
        [jnp.zeros((B, 1, N_POOL_GROUPS, POOL_GROUP_DIM), jnp.float32), jnp.cumsum(xg, axis=1)],
        axis=1)
    t = jnp.arange(S)
    outs = []
    for gi, w in enumerate(POOL_WINDOWS):
        c = csum[:, :, gi]
        hi = c[:, 1:]
        lo = jnp.concatenate([jnp.zeros((B, w - 1, POOL_GROUP_DIM), jnp.float32),
                              c[:, :S - w + 1]], axis=1)
        count = jnp.minimum(t + 1, w).astype(jnp.float32)[None, :, None]
        outs.append((hi - lo) / count - xg[:, :, gi])
    pooled = jnp.stack(outs, axis=2).astype(xn.dtype)
    mixed = jnp.einsum('bsgc,gcd->bsgd', pooled, w_groups).reshape(B, S, D)
    return mixed * scale


def conv_ffn(xn, w_in, conv_w, conv_b, w_out):
    S = xn.shape[1]
    u = xn @ w_in
    up = jnp.pad(u, ((0, 0), (CONV_WIDTH - 1, 0), (0, 0)))
    uc = sum(conv_w[k] * up[:, k:k + S] for k in range(CONV_WIDTH)) + conv_b
    gate, val = jnp.split(uc, 2, axis=-1)
    return (jax.nn.silu(gate) * val) @ w_out


def shared_latent_kv(h, kv_norm, w_dkv, kv_lat_norm, w_ukv, cos, sin):
    hn = rmsnorm(h, kv_norm)
    ckv = hn @ w_dkv
    c_kv = rmsnorm(ckv[..., :KV_LORA_RANK], kv_lat_norm)
    k_rope = apply_rope(ckv[..., KV_LORA_RANK:], cos, sin)
    kv = jnp.einsum('bsc,chd->bshd', c_kv, w_ukv)
    return kv[..., :QK_NOPE_DIM], k_rope, kv[..., QK_NOPE_DIM:]


def mla_mixer(xn, w_dq, q_lat_norm, w_uq, w_o, k_nope, k_rope, v, cos, sin):
    B, S, _ = xn.shape
    c_q = rmsnorm(xn @ w_dq, q_lat_norm)
    q = jnp.einsum('bsc,chd->bshd', c_q, w_uq)
    q_nope = q[..., :QK_NOPE_DIM]
    q_rope = apply_rope(q[..., QK_NOPE_DIM:], cos[:, None, :], sin[:, None, :])
    nblk = S // Q_BLOCK
    qn_b = q_nope.reshape(B, nblk, Q_BLOCK, N_HEADS, QK_NOPE_DIM).transpose(1, 0, 2, 3, 4)
    qr_b = q_rope.reshape(B, nblk, Q_BLOCK, N_HEADS, QK_ROPE_DIM).transpose(1, 0, 2, 3, 4)
    kpos = jnp.arange(S)
    sm_scale = QK_HEAD_DIM ** -0.5

    def attend_block(args):
        qn, qr, bi = args
        s = (jnp.einsum('bqhd,bkhd->bhqk', qn, k_nope)
             + jnp.einsum('bqhr,bkr->bhqk', qr, k_rope))
        s = s.astype(jnp.float32) * sm_scale
        qpos = bi * Q_BLOCK + jnp.arange(Q_BLOCK)
        s = jnp.where(qpos[:, None] >= kpos[None, :], s, -jnp.inf)
        p = jax.nn.softmax(s, axis=-1).astype(v.dtype)
        return jnp.einsum('bhqk,bkhd->bqhd', p, v)

    o = lax.map(attend_block, (qn_b, qr_b, jnp.arange(nblk)))
    o = o.transpose(1, 0, 2, 3, 4).reshape(B, S, N_HEADS * V_HEAD_DIM)
    return o @ w_o


def setup_inputs(seed: int = 0) -> dict:
    key = jax.random.key(seed)
    ks = jax.random.split(key, 24)
    f32 = jnp.float32
    nrm = lambda k, shape, s: jax.random.normal(k, shape, f32) * s
    gain = lambda k, shape: 1.0 + 0.02 * jax.random.normal(k, shape, f32)
    return {
        "x": jax.random.normal(ks[0], (BATCH, SEQ, D_MODEL), f32),
        "a_norm": gain(ks[1], (N_A, D_MODEL)),
        "a_pool_w": nrm(ks[2], (N_A, N_POOL_GROUPS, POOL_GROUP_DIM, POOL_GROUP_DIM), POOL_GROUP_DIM ** -0.5),
        "a_scale": 1.0 + 0.1 * jax.random.normal(ks[3], (N_A, D_MODEL), f32),
        "kv_norm": gain(ks[4], (D_MODEL,)),
        "w_dkv": nrm(ks[5], (D_MODEL, KV_LORA_RANK + QK_ROPE_DIM), D_MODEL ** -0.5),
        "kv_lat_norm": gain(ks[6], (KV_LORA_RANK,)),
        "w_ukv": nrm(ks[7], (KV_LORA_RANK, N_HEADS, QK_NOPE_DIM + V_HEAD_DIM), KV_LORA_RANK ** -0.5),
        "b_norm": gain(ks[8], (N_B, D_MODEL)),
        "w_dq": nrm(ks[9], (N_B, D_MODEL, Q_LORA_RANK), D_MODEL ** -0.5),
        "q_lat_norm": gain(ks[10], (N_B, Q_LORA_RANK)),
        "w_uq": nrm(ks[11], (N_B, Q_LORA_RANK, N_HEADS, QK_HEAD_DIM), Q_LORA_RANK ** -0.5),
        "w_o": nrm(ks[12], (N_B, N_HEADS * V_HEAD_DIM, D_MODEL), (N_HEADS * V_HEAD_DIM) ** -0.5),
        "ffn_norm": gain(ks[13], (DEPTH, D_MODEL)),
        "ffn_w_in": nrm(ks[14], (DEPTH, D_MODEL, 2 * D_FF), D_MODEL ** -0.5),
        "ffn_conv_w": nrm(ks[15], (DEPTH, CONV_WIDTH, 2 * D_FF), CONV_WIDTH ** -0.5),
        "ffn_conv_b": nrm(ks[16], (DEPTH, 2 * D_FF), 0.01),
        "ffn_w_out": nrm(ks[17], (DEPTH, D_FF, D_MODEL), D_FF ** -0.5),
        "final_norm": gain(ks[18], (D_MODEL,)),
    }


def reference(x, a_norm, a_pool_w, a_scale, kv_norm, w_dkv, kv_lat_norm, w_ukv,
              b_norm, w_dq, q_lat_norm, w_uq, w_o, ffn_norm, ffn_w_in, ffn_conv_w,
              ffn_conv_b, ffn_w_out, final_norm):
    S = x.shape[1]
    cos, sin = rope_tables(S, QK_ROPE_DIM, x.dtype)
    h = x
    shared = None
    for layer in range(DEPTH):
        if layer < N_A:
            h = h + multiscale_pool_mixer(rmsnorm(h, a_norm[layer]), a_pool_w[layer], a_scale[layer])
        else:
            j = layer - N_A
            k_nope, k_rope, v = shared
            h = h + mla_mixer(rmsnorm(h, b_norm[j]), w_dq[j], q_lat_norm[j], w_uq[j], w_o[j],
                              k_nope, k_rope, v, cos, sin)
        h = h + conv_ffn(rmsnorm(h, ffn_norm[layer]), ffn_w_in[layer], ffn_conv_w[layer],
                         ffn_conv_b[layer], ffn_w_out[layer])
        if layer == N_A - 1:
            shared = shared_latent_kv(h, kv_norm, w_dkv, kv_lat_norm, w_ukv, cos, sin)
    return rmsnorm(h, final_norm)
```

```python
from contextlib import ExitStack
import numpy as np
import concourse.bass as bass
import concourse.mybir as mybir

F32 = mybir.dt.float32
BF16 = mybir.dt.bfloat16
AF = mybir.ActivationFunctionType
ALU = mybir.AluOpType
EPS = 1e-6


class KB:
    def __init__(self):
        self.nc = bass.Bass("TRN2", target_bir_lowering=False)
        self.es = ExitStack()
        nc = self.nc
        self.engs = {"pe": nc.tensor, "act": nc.scalar, "dve": nc.vector, "pool": nc.gpsimd, "sp": nc.sync}
        self.sem = {}
        self.cnt = {}
        for e in self.engs:
            self.sem[e] = self.es.enter_context(nc.semaphore("s_" + e))
            self.cnt[e] = 0
        self.waited = {}
        self.nsem = 0
        self.banks = [nc.alloc_psum_tensor(f"bank{i}", [128, 512], F32) for i in range(8)]
        self.phase_es = None
        self.uid = 0
        self.sem_pool = []
        self.phase_sems = []

    def sig(self, e, ins):
        ins.then_inc(self.sem[e], 1)
        self.cnt[e] += 1
        return ("e", e, self.cnt[e])

    def wait(self, consumer, t):
        if t is None:
            return
        if isinstance(t, (list, tuple)) and t and isinstance(t[0], (list, tuple)):
            for x in t:
                self.wait(consumer, x)
            return
        kind, key, val = t
        k = (consumer, kind, key if kind == "e" else key.uid)
        if self.waited.get(k, 0) >= val:
            return
        s = self.sem[key] if kind == "e" else key.sem
        self.engs[consumer].wait_ge(s, val)
        self.waited[k] = val

    def newsem(self, name):
        self.nsem += 1
        return self.es.enter_context(self.nc.semaphore(f"{name}_{self.nsem}"))

    class DSem:
        def __init__(self, kb, name):
            if kb.sem_pool:
                self.sem, self.uid, self.cnt = kb.sem_pool.pop()
            else:
                self.sem = kb.newsem(name)
                self.uid = kb.nsem
                self.cnt = 0
            kb.phase_sems.append(self)

    def dma(self, q, dsem, out, in_):
        ins = self.engs[q].dma_start(out=out, in_=in_)
        ins.then_inc(dsem.sem, 16)
        dsem.cnt += 16
        return ("d", dsem, dsem.cnt)

    def phase_begin(self):
        self.phase_es = ExitStack()

    def sb(self, name, shape, dt):
        self.uid += 1
        return self.phase_es.enter_context(self.nc.sbuf_tensor(f"{name}_{self.uid}", list(shape), dt))

    def persist(self, name, shape, dt):
        self.uid += 1
        return self.es.enter_context(self.nc.sbuf_tensor(f"{name}_{self.uid}", list(shape), dt))

    def phase_end(self, final_tickets):
        for t in final_tickets:
            self.wait("sp", t)
        ins = self.nc.sync.sem_inc(self.sem["sp"], 1)
        self.cnt["sp"] += 1
        t = ("e", "sp", self.cnt["sp"])
        for e in ("pe", "act", "dve", "pool"):
            self.wait(e, t)
        self.phase_es.close()
        self.phase_es = None
        for d in self.phase_sems:
            self.sem_pool.append((d.sem, d.uid, d.cnt))
        self.phase_sems = []
        return t


class Ring:
    def __init__(self, bufs):
        self.bufs = bufs
        self.free = [None] * len(bufs)
        self.i = -1

    def next(self):
        self.i = (self.i + 1) % len(self.bufs)
        return self.i


class WStream:
    def __init__(self, kb, loads, ns=4, pf=3, q="pool"):
        self.kb = kb
        self.loads = loads
        self.ns, self.pf, self.q = ns, pf, q
        self.slots = [kb.sb("wslot", [128, 32, 128], BF16) for _ in range(ns)]
        self.dsem = [KB.DSem(kb, "wld") for _ in range(ns)]
        self.free = [None] * ns
        self.ticket = {}
        self.issued = 0
        for _ in range(min(pf, len(loads))):
            self._issue()

    def _issue(self):
        l = self.issued
        if l >= len(self.loads):
            return
        s = l % self.ns
        ap, nk = self.loads[l]
        self.kb.wait(self.q, self.free[s])
        self.ticket[l] = self.kb.dma(self.q, self.dsem[s], self.slots[s][:, 0:nk, :], ap)
        self.issued += 1

    def get(self, l):
        return self.slots[l % self.ns], self.ticket[l]

    def release(self, l, t):
        self.free[l % self.ns] = t
        self._issue()


class BankRing:
    def __init__(self, kb, idxs):
        self.kb = kb
        self.idxs = idxs
        self.free = {i: None for i in idxs}
        self.p = -1

    def next(self):
        self.p = (self.p + 1) % len(self.idxs)
        b = self.idxs[self.p]
        return b, self.kb.banks[b], self.free[b]

    def release(self, b, t):
        self.free[b] = t


def col_tiles(ncol, halo):
    tiles = []
    if halo:
        tiles.append((0, halo))
    c = halo
    while c < ncol:
        n = min(512, ncol - c)
        tiles.append((c, n))
        c += n
    return tiles


def rms_stats(kb, src_fn, nchunk, ncol, rstd, ones_f32, epsT, start_t=None):
    nc = kb.nc
    tiles = col_tiles(ncol, 0)
    assert len(tiles) <= 6
    xb = [kb.sb("rs_x", [128, ncol], F32) for _ in range(3)]
    xs = [KB.DSem(kb, "rs_xs") for _ in range(3)]
    xfree = [None] * 3
    sq = [kb.sb("rs_sq", [128, ncol], F32) for _ in range(2)]
    sqfree = [None] * 2
    lt = {}

    def load(k):
        i = k % 3
        kb.wait("sp", xfree[i])
        lt[k] = kb.dma("sp", xs[i], xb[i][:, :], src_fn(k))
    for k in range(min(2, nchunk)):
        load(k)
    last_pe = None
    for k in range(nchunk):
        if k + 2 < nchunk:
            load(k + 2)
        i, j = k % 3, k % 2
        kb.wait("act", lt[k])
        kb.wait("act", sqfree[j])
        if k == 0:
            kb.wait("act", start_t)
        ta = kb.sig("act", nc.scalar.activation(sq[j][:, :], xb[i][:, :], AF.Square))
        xfree[i] = ta
        kb.wait("pe", ta)
        if k == 0:
            kb.wait("pe", start_t)
        for ti, (c0, n) in enumerate(tiles):
            ins = nc.tensor.matmul(kb.banks[ti][:, 0:n], ones_f32[:, :], sq[j][:, c0:c0 + n],
                                   start=(k == 0), stop=(k == nchunk - 1))
        last_pe = kb.sig("pe", ins)
        sqfree[j] = last_pe
    kb.wait("act", last_pe)
    for ti, (c0, n) in enumerate(tiles):
        ins = nc.scalar.activation(rstd[:, c0:c0 + n], kb.banks[ti][:, 0:n], AF.Sqrt, bias=epsT[:, 0:1], scale=1.0)
    t1 = kb.sig("act", ins)
    kb.wait("dve", t1)
    t2 = kb.sig("dve", nc.vector.reciprocal(rstd[:, :], rstd[:, :]))
    return t2


def norm_apply(kb, src_fn, nchunk, ncol, gamma, rstd, rstd_t, dst_fn, dst_free=None, after=None):
    nc = kb.nc
    xb = [kb.sb("na_x", [128, ncol], F32) for _ in range(3)]
    xs = [KB.DSem(kb, "na_xs") for _ in range(3)]
    xfree = [None] * 3
    lt = {}

    def load(k):
        i = k % 3
        kb.wait("sp", xfree[i])
        lt[k] = kb.dma("sp", xs[i], xb[i][:, :], src_fn(k))
    for k in range(min(2, nchunk)):
        load(k)
    out_t = []
    kb.wait("dve", rstd_t)
    for k in range(nchunk):
        if k + 2 < nchunk:
            load(k + 2)
        i = k % 3
        kb.wait("dve", lt[k])
        if dst_free is not None:
            kb.wait("dve", dst_free(k))
        t = kb.sig("dve", nc.vector.scalar_tensor_tensor(dst_fn(k), xb[i][:, :], gamma[:, k:k + 1], rstd[:, :],
                                                          op0=ALU.mult, op1=ALU.mult))
        xfree[i] = t
        out_t.append(t)
        if after is not None:
            after(k, t)
    return out_t


def proj(kb, ws, x_fn, x_ready, douts, tiles, brng, evac, chunk_done=None, ms=None):
    nc = kb.nc
    first = True
    for di, parts in enumerate(douts):
        nk_tot = sum(p[1] for p in parts)
        m = 128 if ms is None else ms[di]
        last_t = None
        for ti, (c0, n) in enumerate(tiles):
            b, bank, bfree = brng.next()
            kb.wait("pe", bfree)
            if first:
                kb.wait("pe", x_ready)
                first = False
            kk = 0
            for (l, nk, k0) in parts:
                slot, lt = ws.get(l)
                kb.wait("pe", lt)
                for kc in range(nk):
                    ins = nc.tensor.matmul(bank[0:m, 0:n], slot[:, kc, 0:m], x_fn(k0 + kc, c0, n),
                                           start=(kk == 0), stop=(kk == nk_tot - 1))
                    kk += 1
            last_t = kb.sig("pe", ins)
            ft = evac(di, ti, c0, n, bank, last_t)
            brng.release(b, ft)
        for (l, nk, k0) in parts:
            ws.release(l, last_t)
        if chunk_done is not None:
            chunk_done(di)


def ffn_phase(kb, cfg, hin, hout, gscr, w_in_d, w_out_d, cw_d, gam_d, TS=1024):
    nc = kb.nc
    D, DFF, NT = cfg["D"], cfg["DFF"], cfg["NT"]
    KC, NFC = D // 128, DFF // 128
    final = []
    for st in range(NT // TS):
        kb.phase_begin()
        ncol = 2 + TS
        cbase = st * TS
        ones = kb.sb("ones", [128, 128], F32)
        gam = kb.sb("gam", [128, KC], F32)
        cw = kb.sb("cw", [128, 2 * NFC, 4], F32)
        rstd = kb.sb("rstd", [128, ncol], F32)
        xnT = kb.sb("xnT", [128, KC, ncol], BF16)
        cs = KB.DSem(kb, "const")
        epsT = kb.sb("epsT", [128, 1], F32)
        nc.vector.memset(epsT[:, :], EPS)
        t0 = kb.sig("dve", nc.vector.memset(ones[:, :], 1.0 / D))
        kb.dma("sp", cs, gam[:, :], gam_d[:, :])
        tc = kb.dma("sp", cs, cw[:, :, :], cw_d[:, :, :])
        src = lambda k: hin[k * 128:(k + 1) * 128, cbase:cbase + ncol]
        rt = rms_stats(kb, src, KC, ncol, rstd, ones, epsT, start_t=t0)
        kb.wait("dve", tc)
        xt = norm_apply(kb, src, KC, ncol, gam, rstd, rt, lambda k: xnT[:, k, :])
        loads = [(w_in_d[i], KC) for i in range(2 * NFC)]
        ws = WStream(kb, loads)
        tiles = col_tiles(ncol, 2)
        brng = BankRing(kb, list(range(8)))
        ub = [kb.sb("ubuf", [128, ncol], F32) for _ in range(2)]
        ubfree = [None, None]
        ab = {0: [kb.sb("ag", [128, TS], F32) for _ in range(2)], 1: [kb.sb("av", [128, TS], F32) for _ in range(2)]}
        abfree = {0: [None, None], 1: [None, None]}
        go = [kb.sb("gout", [128, TS], BF16) for _ in range(2)]
        gos = [KB.DSem(kb, "gst") for _ in range(2)]
        gofree = [None, None]
        state = {"evt": [], "conv": {}}

        def evac(di, ti, c0, n, bank, pt):
            u = di % 2
            kb.wait("act", pt)
            if ti == 0:
                kb.wait("act", ubfree[u])
            t = kb.sig("act", nc.scalar.copy(ub[u][:, c0:c0 + n], bank[:, 0:n]))
            state["evt"] = t
            return t

        def chunk_done(di):
            fc, gv = di // 2, di % 2
            u = di % 2
            r = fc % 2
            ci = fc if gv == 0 else NFC + fc
            a = ab[gv][r]
            kb.wait("dve", state["evt"])
            kb.wait("dve", abfree[gv][r])
            t = kb.sig("dve", nc.vector.tensor_scalar(a[:, :], ub[u][:, 2:2 + TS], cw[:, ci, 2:3], cw[:, ci, 3:4],
                                                      op0=ALU.mult, op1=ALU.add))
            kb.wait("dve", t)
            t = kb.sig("dve", nc.vector.scalar_tensor_tensor(a[:, :], ub[u][:, 1:1 + TS], cw[:, ci, 1:2], a[:, :],
                                                              op0=ALU.mult, op1=ALU.add))
            kb.wait("dve", t)
            t = kb.sig("dve", nc.vector.scalar_tensor_tensor(a[:, :], ub[u][:, 0:TS], cw[:, ci, 0:1], a[:, :],
                                                              op0=ALU.mult, op1=ALU.add))
            ubfree[u] = t
            if gv == 0:
                kb.wait("act", t)
                state["conv"][0] = kb.sig("act", nc.scalar.activation(a[:, :], a[:, :], AF.Silu))
            else:
                state["conv"][1] = t
                kb.wait("dve", state["conv"][0])
                kb.wait("dve", state["conv"][1])
                kb.wait("dve", gofree[r])
                tm = kb.sig("dve", nc.vector.tensor_tensor(go[r][:, :], ab[0][r][:, :], a[:, :], op=ALU.mult))
                abfree[0][r] = tm
                abfree[1][r] = tm
                kb.wait("sp", tm)
                for j in range(TS // 512):
                    tj = kb.dma("sp", gos[r], gscr[(cbase // 512) + j, :, fc, :], go[r][:, j * 512:(j + 1) * 512])
                gofree[r] = tj
                state["last_store"] = tj

        douts = [[(i, KC, 0)] for i in range(2 * NFC)]
        proj(kb, ws, lambda kc, c0, n: xnT[:, kc, c0:c0 + n], xt[-1], douts, tiles, brng, evac, chunk_done)
        kb.phase_end([gofree[0], gofree[1]])

        for tt in range(TS // 512):
            tile = cbase // 512 + tt
            kb.phase_begin()
            gT = kb.sb("gT", [128, NFC, 512], BF16)
            gs = KB.DSem(kb, "gld")
            nsplit = 4 if NFC >= 4 else 1
            step = (NFC + nsplit - 1) // nsplit
            for a0 in range(0, NFC, step):
                a1 = min(NFC, a0 + step)
                tg = kb.dma("sp", gs, gT[:, a0:a1, :], gscr[tile, :, a0:a1, :])
            parts_k = []
            k0 = 0
            while k0 < NFC:
                nk = min(32, NFC - k0)
                parts_k.append((k0, nk))
                k0 += nk
            loads = []
            douts = []
            for dc in range(KC):
                ps = []
                for (k0, nk) in parts_k:
                    ps.append((len(loads), nk, k0))
                    loads.append((w_out_d[dc, :, k0:k0 + nk, :], nk))
                douts.append(ps)
            ws = WStream(kb, loads)
            brng = BankRing(kb, list(range(8)))
            rb = [kb.sb("rbuf", [128, 512], F32) for _ in range(3)]
            rs = [KB.DSem(kb, "rld") for _ in range(3)]
            rfree = [None] * 3
            ob = [kb.sb("obuf", [128, 512], F32) for _ in range(2)]
            osm = [KB.DSem(kb, "ost") for _ in range(2)]
            ofree = [None, None]
            rt_ = {}

            def rload(dc):
                i = dc % 3
                kb.wait("sp", rfree[i])
                rt_[dc] = kb.dma("sp", rs[i], rb[i][:, :], hin[dc * 128:(dc + 1) * 128, 2 + tile * 512:2 + (tile + 1) * 512])
            rload(0)
            if KC > 1:
                rload(1)

            def evac2(di, ti, c0, n, bank, pt):
                i, o = di % 3, di % 2
                kb.wait("dve", pt)
                kb.wait("dve", rt_[di])
                kb.wait("dve", ofree[o])
                t = kb.sig("dve", nc.vector.tensor_tensor(ob[o][:, :], bank[:, 0:512], rb[i][:, :], op=ALU.add))
                rfree[i] = t
                if di + 2 < KC:
                    rload(di + 2)
                kb.wait("sp", t)
                ofree[o] = kb.dma("sp", osm[o], hout[di * 128:(di + 1) * 128, tile * 512:(tile + 1) * 512], ob[o][:, :])
                return t

            proj(kb, ws, lambda kc, c0, n: gT[:, kc, c0:c0 + n], tg, douts, [(0, 512)], brng, evac2)
            final = [ofree[0], ofree[1]]
            kb.phase_end(final)
    return final


POOL_WINDOWS = (2, 4, 8, 16)
HALO_A = 17


def consts(kb, D):
    nc = kb.nc
    c = {}
    c["epsT"] = kb.persist("epsT", [128, 1], F32)
    nc.vector.memset(c["epsT"][:, :], EPS)
    t = None
    for nm, val in (("onesD", 1.0 / D), ("ones512", 1.0 / 512), ("ones1024", 1.0 / 1024)):
        c[nm] = kb.persist(nm, [128, 128], F32)
        t = kb.sig("dve", nc.vector.memset(c[nm][:, :], val))
    c["t"] = t
    return c


def stats_phase(kb, src_fn, nchunk, ncol, rstd, ones, epsT, t0):
    kb.phase_begin()
    rt = rms_stats(kb, src_fn, nchunk, ncol, rstd, ones, epsT, start_t=t0)
    kb.phase_end([rt])


def pool_phase(kb, cfg, C, xT, hmid, w_pool_d, gam_d, asc_d, invc_d, rstd):
    nc = kb.nc
    D, NT = cfg["D"], cfg["NT"]
    KC = D // 128
    G = 4
    CG = KC // G
    ncol = HALO_A + NT
    nmid = 2 + NT
    OFF = HALO_A - 2
    stats_phase(kb, lambda k: xT[k * 128:(k + 1) * 128, :], KC, ncol, rstd, C["onesD"], C["epsT"], C["t"])
    kb.phase_begin()
    gam = kb.sb("gam", [128, KC], F32)
    asc = kb.sb("asc", [128, KC], F32)
    cs = KB.DSem(kb, "const")
    kb.dma("sp", cs, gam[:, :], gam_d[:, :])
    tc = kb.dma("sp", cs, asc[:, :], asc_d[:, :])
    pooled = kb.sb("pooled", [128, CG, nmid], BF16)
    pooled_free = None
    xl = [kb.sb("xl", [128, ncol], F32) for _ in range(2)]
    xls = [KB.DSem(kb, "xls") for _ in range(2)]
    xlfree = [None, None]
    xnb = kb.sb("xnb", [128, ncol], F32)
    tmp = [kb.sb("ptmp", [128, ncol], F32) for _ in range(2)]
    invc = kb.sb("invc", [128, ncol], F32)
    ivs = KB.DSem(kb, "ivs")
    loads = [(w_pool_d[i], CG) for i in range(G * CG)]
    ws = WStream(kb, loads)
    brng = BankRing(kb, list(range(8)))
    tiles = col_tiles(nmid, 2)
    rb = [kb.sb("presid", [128, nmid], F32) for _ in range(2)]
    rbs = [KB.DSem(kb, "prs") for _ in range(2)]
    rbfree = [None, None]
    ob = [kb.sb("pobuf", [128, nmid], F32) for _ in range(2)]
    obs = [KB.DSem(kb, "pos") for _ in range(2)]
    obfree = [None, None]
    kb.wait("dve", tc)
    last_dve = None
    nload = 0
    for g in range(G):
        kb.wait("sp", last_dve)
        ti = kb.dma("sp", ivs, invc[:, :], invc_d[g, :, :])
        for c in range(CG):
            k = g * CG + c
            i = nload % 2
            nload += 1
            kb.wait("sp", xlfree[i])
            tl = kb.dma("sp", xls[i], xl[i][:, :], xT[k * 128:(k + 1) * 128, :])
            kb.wait("dve", tl)
            kb.wait("dve", last_dve)
            t = kb.sig("dve", nc.vector.scalar_tensor_tensor(xnb[:, :], xl[i][:, :], gam[:, k:k + 1], rstd[:, 0:ncol],
                                                              op0=ALU.mult, op1=ALU.mult))
            xlfree[i] = t
            cur = xnb
            sh = 1
            for step in range(g + 1):
                dst = tmp[step % 2]
                kb.wait("dve", t)
                t = kb.sig("dve", nc.vector.tensor_tensor(dst[:, sh:ncol], cur[:, sh:ncol], cur[:, 0:ncol - sh], op=ALU.add))
                cur = dst
                sh *= 2
            other = tmp[(g + 1) % 2]
            kb.wait("dve", t)
            kb.wait("dve", ti)
            t = kb.sig("dve", nc.vector.tensor_tensor(other[:, OFF:ncol], cur[:, OFF:ncol], invc[:, OFF:ncol], op=ALU.mult))
            kb.wait("dve", t)
            if c == 0:
                kb.wait("dve", pooled_free)
            t = kb.sig("dve", nc.vector.tensor_tensor(pooled[:, c, :], other[:, OFF:ncol], xnb[:, OFF:ncol], op=ALU.subtract))
            last_dve = t
        st = {}

        def evac(di, ti_, c0, n, bank, pt, g=g):
            ko = g * CG + di
            r = ko % 2
            if ti_ == 0:
                kb.wait("sp", rbfree[r])
                st["rl"] = kb.dma("sp", rbs[r], rb[r][:, :], xT[ko * 128:(ko + 1) * 128, OFF:ncol])
                kb.wait("dve", obfree[r])
            kb.wait("dve", pt)
            kb.wait("dve", st["rl"])
            t = kb.sig("dve", nc.vector.scalar_tensor_tensor(ob[r][:, c0:c0 + n], bank[:, 0:n], asc[:, ko:ko + 1],
                                                              rb[r][:, c0:c0 + n], op0=ALU.mult, op1=ALU.add))
            st["t"] = t
            return t

        def chunk_done(di, g=g):
            ko = g * CG + di
            r = ko % 2
            rbfree[r] = st["t"]
            kb.wait("sp", st["t"])
            obfree[r] = kb.dma("sp", obs[r], hmid[ko * 128:(ko + 1) * 128, :], ob[r][:, :])

        douts = [[(g * CG + dc, CG, 0)] for dc in range(CG)]
        proj(kb, ws, lambda kc, c0, n: pooled[:, kc, c0:c0 + n], last_dve, douts, tiles, brng, evac, chunk_done)
        pooled_free = ("e", "pe", kb.cnt["pe"])
        last_dve = ("e", "dve", kb.cnt["dve"])
    kb.phase_end([obfree[0], obfree[1]])


def rms_sb(kb, buf, nch, ncol, ones, epsT, rstd, sq, buf_ready, banks=(0, 1)):
    nc = kb.nc
    tiles = col_tiles(ncol, 0)
    sqfree = [None, None]
    last = None
    for ch in range(nch):
        j = ch % 2
        kb.wait("act", buf_ready)
        kb.wait("act", sqfree[j])
        ta = kb.sig("act", nc.scalar.activation(sq[j][:, 0:ncol], buf[:, ch, :], AF.Square))
        kb.wait("pe", ta)
        for ti, (c0, n) in enumerate(tiles):
            ins = nc.tensor.matmul(kb.banks[banks[ti]][:, 0:n], ones[:, :], sq[j][:, c0:c0 + n],
                                   start=(ch == 0), stop=(ch == nch - 1))
        last = kb.sig("pe", ins)
        sqfree[j] = last
    kb.wait("act", last)
    for ti, (c0, n) in enumerate(tiles):
        ins = nc.scalar.activation(rstd[:, c0:c0 + n], kb.banks[banks[ti]][:, 0:n], AF.Sqrt, bias=epsT[:, 0:1], scale=1.0)
    t1 = kb.sig("act", ins)
    kb.wait("dve", t1)
    return kb.sig("dve", nc.vector.reciprocal(rstd[:, 0:ncol], rstd[:, 0:ncol]))


def latent_phase(kb, cfg, C, h0T, w_dkv_d, w_dq_d, vec_d, cosT_d, sinT_d, ckvn_o, krope_o, cqn_o, rstd, TS=1024):
    nc = kb.nc
    D, NT, QL = cfg["D"], cfg["NT"], cfg["QL"]
    KC, NQ = D // 128, QL // 128
    stats_phase(kb, lambda k: h0T[k * 128:(k + 1) * 128, :], KC, NT, rstd, C["onesD"], C["epsT"], C["t"])
    onesq = C["ones1024"] if QL == 1024 else C["onesQ"]
    for st in range(NT // TS):
        cb = st * TS
        kb.phase_begin()
        vec = kb.sb("vec", [128, 2 * KC + 4 + NQ], F32)
        cs = KB.DSem(kb, "const")
        tv = kb.dma("sp", cs, vec[:, :], vec_d[:, :])
        cosT = kb.sb("cosT", [64, TS], F32)
        sinT = kb.sb("sinT", [64, TS], F32)
        kb.dma("sp", cs, cosT[:, :], cosT_d[:, cb:cb + TS])
        tcs = kb.dma("sp", cs, sinT[:, :], sinT_d[:, cb:cb + TS])
        xnT = kb.sb("xnT", [128, KC, TS], BF16)
        ckv = kb.sb("ckv", [128, 6, TS], F32)
        sq = [kb.sb("lsq", [128, TS], F32) for _ in range(2)]
        rs2 = kb.sb("rs2", [128, TS], F32)
        ob = [kb.sb("lob", [128, TS], BF16) for _ in range(2)]
        obs = [KB.DSem(kb, "los") for _ in range(2)]
        obfree = [None, None]
        src = lambda k: h0T[k * 128:(k + 1) * 128, cb:cb + TS]
        kb.wait("dve", tv)
        ws = WStream(kb, [(w_dkv_d[i], KC) for i in range(6)])
        brng = BankRing(kb, [2, 3, 4, 5, 6, 7])
        tiles = col_tiles(TS, 0)
        xt = norm_apply(kb, src, KC, TS, vec[:, 0:KC], rstd[:, cb:cb + TS], None, lambda k: xnT[:, k, :])
        stt = {}

        def evac_kv(di, ti, c0, n, bank, pt):
            m = 128 if di < 4 else 64
            kb.wait("act", pt)
            t = kb.sig("act", nc.scalar.copy(ckv[0:m, di, c0:c0 + n], bank[0:m, 0:n]))
            stt["t"] = t
            return t
        proj(kb, ws, lambda kc, c0, n: xnT[:, kc, c0:c0 + n], xt[-1], [[(i, KC, 0)] for i in range(6)], tiles, brng,
             evac_kv, ms=[128] * 4 + [64, 64])
        pe_kv_done = ("e", "pe", kb.cnt["pe"])
        tr = rms_sb(kb, ckv, 4, TS, C["ones512"], C["epsT"], rs2, sq, stt["t"])
        nob = 0
        for ch in range(4):
            r = nob % 2
            nob += 1
            kb.wait("dve", tr)
            kb.wait("dve", obfree[r])
            t = kb.sig("dve", nc.vector.scalar_tensor_tensor(ob[r][:, :], ckv[:, ch, :], vec[:, 2 * KC + ch:2 * KC + ch + 1],
                                                              rs2[:, :], op0=ALU.mult, op1=ALU.mult))
            kb.wait("sp", t)
            obfree[r] = kb.dma("sp", obs[r], ckvn_o[ch * 128:(ch + 1) * 128, cb:cb + TS], ob[r][:, :])
        kb.wait("dve", tcs)
        kb.wait("dve", stt["t"])
        t = kb.sig("dve", nc.vector.tensor_tensor(ckv[0:64, 4, :], ckv[0:64, 4, :], cosT[:, :], op=ALU.mult))
        t = kb.sig("dve", nc.vector.tensor_tensor(ckv[0:64, 5, :], ckv[0:64, 5, :], sinT[:, :], op=ALU.mult))
        kb.wait("dve", t)
        r = nob % 2
        nob += 1
        kb.wait("dve", obfree[r])
        t = kb.sig("dve", nc.vector.tensor_tensor(ob[r][0:64, :], ckv[0:64, 4, :], ckv[0:64, 5, :], op=ALU.add))
        kb.wait("sp", t)
        obfree[r] = kb.dma("sp", obs[r], krope_o[:, cb:cb + TS], ob[r][0:64, :])
        kb.phase_end([obfree[0], obfree[1]])
        kb.phase_begin()
        vec = kb.sb("vec", [128, 2 * KC + 4 + NQ], F32)
        cs = KB.DSem(kb, "const")
        tv = kb.dma("sp", cs, vec[:, :], vec_d[:, :])
        xnT = kb.sb("xnT", [128, KC, TS], BF16)
        cq = kb.sb("cq", [128, NQ, TS], F32)
        sq = [kb.sb("lsq", [128, TS], F32) for _ in range(2)]
        rs2 = kb.sb("rs2", [128, TS], F32)
        ob = [kb.sb("lob", [128, TS], BF16) for _ in range(2)]
        obs = [KB.DSem(kb, "los") for _ in range(2)]
        obfree = [None, None]
        kb.wait("dve", tv)
        ws = WStream(kb, [(w_dq_d[i], KC) for i in range(NQ)])
        brng = BankRing(kb, [2, 3, 4, 5, 6, 7])
        xt = norm_apply(kb, src, KC, TS, vec[:, KC:2 * KC], rstd[:, cb:cb + TS], None, lambda k: xnT[:, k, :])

        def evac_q(di, ti, c0, n, bank, pt):
            kb.wait("act", pt)
            t = kb.sig("act", nc.scalar.copy(cq[:, di, c0:c0 + n], bank[:, 0:n]))
            stt["t"] = t
            return t
        proj(kb, ws, lambda kc, c0, n: xnT[:, kc, c0:c0 + n], xt[-1], [[(i, KC, 0)] for i in range(NQ)], tiles, brng, evac_q)
        tr = rms_sb(kb, cq, NQ, TS, onesq, C["epsT"], rs2, sq, stt["t"])
        for ch in range(NQ):
            r = nob % 2
            nob += 1
            kb.wait("dve", tr)
            kb.wait("dve", obfree[r])
            t = kb.sig("dve", nc.vector.scalar_tensor_tensor(ob[r][:, :], cq[:, ch, :], vec[:, 2 * KC + 4 + ch:2 * KC + 5 + ch],
                                                              rs2[:, :], op0=ALU.mult, op1=ALU.mult))
            kb.wait("sp", t)
            obfree[r] = kb.dma("sp", obs[r], cqn_o[ch * 128:(ch + 1) * 128, cb:cb + TS], ob[r][:, :])
        kb.phase_end([obfree[0], obfree[1]])


def wo_phase(kb, cfg, oT, h0T, h1T, w_o_d, TS=1024):
    nc = kb.nc
    D, NT = cfg["D"], cfg["NT"]
    KC = D // 128
    for st in range(NT // TS):
        c_lo = 0 if st == 0 else 2 + st * TS
        ncol = TS + (2 if st == 0 else 0)
        kb.phase_begin()
        oS = kb.sb("oS", [128, KC, ncol], BF16)
        os_ = KB.DSem(kb, "old")
        for k0 in range(0, KC, 8):
            k1 = min(KC, k0 + 8)
            to = kb.dma("sp", os_, oS[:, k0:k1, :],
                        oT[k0 * 128:k1 * 128, c_lo:c_lo + ncol].rearrange("(k p) c -> p k c", p=128))
        ws = WStream(kb, [(w_o_d[i], KC) for i in range(KC)])
        brng = BankRing(kb, list(range(8)))
        tiles = col_tiles(ncol, 2 if st == 0 else 0)
        rb = [kb.sb("wresid", [128, ncol], F32) for _ in range(2)]
        rbs = [KB.DSem(kb, "wrs") for _ in range(2)]
        rbfree = [None, None]
        ob = [kb.sb("wobuf", [128, ncol], F32) for _ in range(2)]
        obs = [KB.DSem(kb, "wos") for _ in range(2)]
        obfree = [None, None]
        stt = {}

        def evac(di, ti, c0, n, bank, pt):
            r = di % 2
            if ti == 0:
                kb.wait("sp", rbfree[r])
                stt["rl"] = kb.dma("sp", rbs[r], rb[r][:, :], h0T[di * 128:(di + 1) * 128, c_lo:c_lo + ncol])
                kb.wait("dve", obfree[r])
            kb.wait("dve", pt)
            kb.wait("dve", stt["rl"])
            t = kb.sig("dve", nc.vector.tensor_tensor(ob[r][:, c0:c0 + n], bank[:, 0:n], rb[r][:, c0:c0 + n], op=ALU.add))
            stt["t"] = t
            return t

        def chunk_done(di):
            r = di % 2
            rbfree[r] = stt["t"]
            kb.wait("sp", stt["t"])
            obfree[r] = kb.dma("sp", obs[r], h1T[di * 128:(di + 1) * 128, c_lo:c_lo + ncol], ob[r][:, :])
        proj(kb, ws, lambda kc, c0, n: oS[:, kc, c0:c0 + n], to, [[(i, KC, 0)] for i in range(KC)], tiles, brng, evac, chunk_done)
        kb.phase_end([obfree[0], obfree[1]])


def final_norm_phase(kb, cfg, C, hT, outT, gam_d, rstd):
    nc = kb.nc
    D, NT = cfg["D"], cfg["NT"]
    KC = D // 128
    stats_phase(kb, lambda k: hT[k * 128:(k + 1) * 128, :], KC, NT, rstd, C["onesD"], C["epsT"], C["t"])
    kb.phase_begin()
    gam = kb.sb("fgam", [128, KC], F32)
    cs = KB.DSem(kb, "const")
    tg = kb.dma("sp", cs, gam[:, :], gam_d[:, :])
    ob = [kb.sb("fob", [128, NT], F32) for _ in range(2)]
    obs = [KB.DSem(kb, "fos") for _ in range(2)]
    obfree = [None, None]
    kb.wait("dve", tg)
    def after(k, t):
        kb.wait("sp", t)
        obfree[k % 2] = kb.dma("sp", obs[k % 2], outT[k * 128:(k + 1) * 128, :], ob[k % 2][:, :])
    norm_apply(kb, lambda k: hT[k * 128:(k + 1) * 128, :], KC, NT, gam, rstd[:, 0:NT], None,
               lambda k: ob[k % 2][:, :], dst_free=lambda k: obfree[k % 2], after=after)
    kb.phase_end([obfree[0], obfree[1]])


class StageRing:
    def __init__(self, kb, name, n, shape, dt):
        self.kb, self.n = kb, n
        self.bufs = [kb.sb(name, shape, dt) for _ in range(n)]
        self.ds = [KB.DSem(kb, name + "s") for _ in range(n)]
        self.free = [None] * n
        self.i = -1

    def next(self, eng):
        self.i = (self.i + 1) % self.n
        self.kb.wait(eng, self.free[self.i])
        return self.bufs[self.i]

    def store(self, pairs, t):
        self.kb.wait("sp", t)
        for dram_ap, src_ap in pairs:
            self.free[self.i] = self.kb.dma("sp", self.ds[self.i], dram_ap, src_ap)

    def finals(self):
        return [f for f in self.free if f is not None]


class LoadRing:
    def __init__(self, kb, name, n, shapes):
        self.kb, self.n = kb, n
        self.bufs = [[kb.sb(name, sh, dt) for (sh, dt) in shapes] for _ in range(n)]
        self.ds = [KB.DSem(kb, name + "l") for _ in range(n)]
        self.free = [None] * n
        self.i = -1

    def load(self, fn):
        self.i = (self.i + 1) % self.n
        s = self.i
        self.kb.wait("sp", self.free[s])
        t = None
        for dst, src in fn(self.bufs[s]):
            t = self.kb.dma("sp", self.ds[s], dst, src)
        return s, self.bufs[s], t

    def release(self, s, t):
        self.free[s] = t


def attn_pre_phase(kb, cfg, cqnT, ckvnT, wq_d, wk_d, wv_d, cosT_d, sinT_d, QnT, QrT, KnT, Vh):
    nc = kb.nc
    S, NH, QL = cfg["S"], cfg["NH"], cfg["QL"]
    NQ = QL // 128
    sm = float(192 ** -0.5)
    kb.phase_begin()
    wq = kb.sb("wq", [128, NQ, NH * 256], BF16)
    wk = kb.sb("wk", [128, 4, NH * 128], BF16)
    wv = kb.sb("wv", [128, 4, NH * 128], BF16)
    cs = KB.DSem(kb, "wl")
    kb.dma("pool", cs, wq[:, :, :], wq_d[:, :, :])
    kb.dma("pool", cs, wk[:, :, :], wk_d[:, :, :])
    tw = kb.dma("pool", cs, wv[:, :, :], wv_d[:, :, :])
    inr = LoadRing(kb, "pin", 2, [([128, NQ, 512], BF16), ([128, 4, 512], BF16), ([64, 512], F32), ([64, 512], F32)])
    qn_o = StageRing(kb, "qno", 2, [128, 512], BF16)
    qr_o = StageRing(kb, "qro", 2, [64, 512], BF16)
    kn_o = StageRing(kb, "kno", 2, [128, 512], BF16)
    v_o = StageRing(kb, "vo", 2, [128, NH * 128], BF16)
    ta_ = [kb.sb("qra", [64, 512], F32) for _ in range(2)]
    tb_ = [kb.sb("qrb", [64, 512], F32) for _ in range(2)]
    tfree = [None, None]
    brng = BankRing(kb, list(range(8)))
    kb.wait("pe", tw)
    nqr = 0
    pending = None
    for it in range(S // 512):
        c0 = it * 512

        def ld(bufs, c0=c0):
            cq, ckv, co, si = bufs
            return [(cq[:, :, :], cqnT[:, c0:c0 + 512].rearrange("(k p) c -> p k c", p=128)),
                    (ckv[:, :, :], ckvnT[:, c0:c0 + 512].rearrange("(k p) c -> p k c", p=128)),
                    (co[:, :], cosT_d[:, c0:c0 + 512]), (si[:, :], sinT_d[:, c0:c0 + 512])]
        slot, (cq, ckv, co, si), tl = inr.load(ld)
        kb.wait("pe", tl)
        for h in range(NH):
            b, bank, bf = brng.next()
            kb.wait("pe", bf)
            for kc in range(NQ):
                ins = nc.tensor.matmul(bank[:, :], wq[:, kc, h * 256:h * 256 + 128], cq[:, kc, :], start=(kc == 0), stop=(kc == NQ - 1))
            tp = kb.sig("pe", ins)
            ob = qn_o.next("act")
            kb.wait("act", tp)
            t = kb.sig("act", nc.scalar.activation(ob[:, :], bank[:, :], AF.Copy, scale=sm))
            brng.release(b, t)
            qn_o.store([(QnT[h, :, c0:c0 + 512], ob[:, :])], t)
            r = nqr % 2
            nqr += 1
            tt = []
            for which, dst in ((0, ta_[r]), (1, tb_[r])):
                b, bank, bf = brng.next()
                kb.wait("pe", bf)
                o0 = h * 256 + 128 + 64 * which
                for kc in range(NQ):
                    ins = nc.tensor.matmul(bank[0:64, :], wq[:, kc, o0:o0 + 64], cq[:, kc, :], start=(kc == 0), stop=(kc == NQ - 1))
                tp = kb.sig("pe", ins)
                kb.wait("act", tp)
                kb.wait("act", tfree[r])
                t = kb.sig("act", nc.scalar.activation(dst[:, :], bank[0:64, :], AF.Copy, scale=sm))
                brng.release(b, t)
                tt.append(t)
            kb.wait("dve", tt[1])
            kb.wait("dve", tl)
            nc.vector.tensor_tensor(ta_[r][:, :], ta_[r][:, :], co[:, :], op=ALU.mult)
            t = kb.sig("dve", nc.vector.tensor_tensor(tb_[r][:, :], tb_[r][:, :], si[:, :], op=ALU.mult))
            kb.wait("dve", t)
            ob = qr_o.next("dve")
            t = kb.sig("dve", nc.vector.tensor_tensor(ob[:, :], ta_[r][:, :], tb_[r][:, :], op=ALU.add))
            tfree[r] = t
            qr_o.store([(QrT[h, :, c0:c0 + 512], ob[:, :])], t)
            last_dve = t
            b, bank, bf = brng.next()
            kb.wait("pe", bf)
            for kc in range(4):
                ins = nc.tensor.matmul(bank[:, :], wk[:, kc, h * 128:(h + 1) * 128], ckv[:, kc, :], start=(kc == 0), stop=(kc == 3))
            tp = kb.sig("pe", ins)
            ob = kn_o.next("act")
            kb.wait("act", tp)
            t = kb.sig("act", nc.scalar.copy(ob[:, :], bank[:, :]))
            brng.release(b, t)
            kn_o.store([(KnT[h, :, c0:c0 + 512], ob[:, :])], t)
        for j in range(4):
            b, bank, bf = brng.next()
            kb.wait("pe", bf)
            for kc in range(4):
                ins = nc.tensor.matmul(bank[:, 0:NH * 128], ckv[:, kc, j * 128:(j + 1) * 128], wv[:, kc, :], start=(kc == 0), stop=(kc == 3))
            tp = kb.sig("pe", ins)
            ob = v_o.next("act")
            kb.wait("act", tp)
            t = kb.sig("act", nc.scalar.copy(ob[:, :], bank[:, 0:NH * 128]))
            brng.release(b, t)
            v_o.store([(Vh[h, :, it * 4 + j, :], ob[:, h * 128:(h + 1) * 128]) for h in range(NH)], t)
        inr.release(slot, [("e", "pe", kb.cnt["pe"]), last_dve])
    kb.phase_end(qn_o.finals() + qr_o.finals() + kn_o.finals() + v_o.finals())


def attn_phase(kb, cfg, QnT, QrT, KnT, Vh, kropeT, tri_d, oT):
    nc = kb.nc
    S, NH = cfg["S"], cfg["NH"]
    NQB = S // 512
    for h in range(NH):
        kb.phase_begin()
        KT = kb.sb("KT", [128, S], BF16)
        KR = kb.sb("KR", [64, S], BF16)
        VS = kb.sb("VS", [128, S // 128, 128], BF16)
        ones = kb.sb("aones", [128, 128], BF16)
        tri = kb.sb("tri", [128, 128], BF16)
        rec = kb.sb("rec", [128, 512], F32)
        t1 = kb.sig("dve", nc.vector.memset(ones[:, :], 1.0))
        cs = KB.DSem(kb, "kvl")
        kb.dma("sp", cs, tri[:, :], tri_d[:, :])
        npc = 4 if S >= 2048 else 1
        w = S // npc
        for p in range(npc):
            kb.dma("sp", cs, KT[:, p * w:(p + 1) * w], KnT[h, :, p * w:(p + 1) * w])
            kb.dma("sp", cs, KR[:, p * w:(p + 1) * w], kropeT[:, p * w:(p + 1) * w])
            tkv = kb.dma("sp", cs, VS[:, p * (w // 128):(p + 1) * (w // 128), :], Vh[h, :, p * (w // 128):(p + 1) * (w // 128), :])
        kb.wait("pe", tkv)
        kb.wait("pe", t1)
        kb.wait("dve", tkv)
        qring = LoadRing(kb, "qb", 3, [([128, 512], BF16), ([64, 512], BF16)])
        pb = [kb.sb("pb", [128, 512], BF16) for _ in range(4)]
        pfree = [None] * 4
        sfree = [None] * 4
        o_st = StageRing(kb, "ao", 2, [128, 512], BF16)
        olfree = [None, None]
        gi = 0

        def qload(qb):
            return qring.load(lambda bufs: [(bufs[0][:, :], QnT[h, :, qb * 512:(qb + 1) * 512]),
                                            (bufs[1][:, :], QrT[h, :, qb * 512:(qb + 1) * 512])])
        nxt = qload(0)
        for qb in range(NQB):
            qslot, (qn, qr), tq = nxt
            if qb + 1 < NQB:
                nxt = qload(qb + 1)
            O = kb.banks[4 + 2 * (qb % 2)]
            L = kb.banks[5 + 2 * (qb % 2)]
            items = [(kt, None) for kt in range(4 * qb)] + [(4 * qb + j, j) for j in range(4)]
            n = len(items)
            ready = {}
            kb.wait("pe", tq)

            def emitS(i, g):
                kt, j = items[i]
                off = 0 if j is None else 128 * j
                sb_ = g % 4
                bank = kb.banks[sb_]
                kb.wait("pe", sfree[sb_])
                nc.tensor.matmul(bank[:, off:512], KT[:, kt * 128:(kt + 1) * 128], qn[:, off:512], start=True, stop=False)
                ts = kb.sig("pe", nc.tensor.matmul(bank[:, off:512], KR[:, kt * 128:(kt + 1) * 128], qr[:, off:512],
                                                   start=False, stop=True))
                kb.wait("act", ts)
                kb.wait("act", pfree[sb_])
                te = kb.sig("act", nc.scalar.activation(pb[sb_][:, off:512], bank[:, off:512], AF.Exp))
                sfree[sb_] = te
                if j is not None:
                    kb.wait("dve", te)
                    te = kb.sig("dve", nc.vector.tensor_tensor(pb[sb_][:, off:off + 128], pb[sb_][:, off:off + 128],
                                                               tri[:, :], op=ALU.mult))
                ready[i] = (te, sb_, off, kt)

            def emitPV(i):
                te, sb_, off, kt = ready[i]
                kb.wait("pe", te)
                if i == 0:
                    kb.wait("pe", olfree[qb % 2])
                nc.tensor.matmul(O[:, off:512], VS[:, kt, :], pb[sb_][:, off:512], start=(i == 0), stop=(i == n - 1))
                tp = kb.sig("pe", nc.tensor.matmul(L[:, off:512], ones[:, :], pb[sb_][:, off:512], start=(i == 0), stop=(i == n - 1)))
                pfree[sb_] = tp
                return tp
            LA = 2
            for i in range(min(LA, n)):
                emitS(i, gi + i)
            tp = None
            for i in range(n):
                if i + LA < n:
                    emitS(i + LA, gi + i + LA)
                tp = emitPV(i)
            gi += n
            qring.release(qslot, tp)
            kb.wait("dve", tp)
            t = kb.sig("dve", nc.vector.reciprocal(rec[:, :], L[:, :]))
            kb.wait("dve", t)
            ob = o_st.next("dve")
            t = kb.sig("dve", nc.vector.tensor_tensor(ob[:, :], O[:, :], rec[:, :], op=ALU.mult))
            olfree[qb % 2] = t
            o_st.store([(oT[h * 128:(h + 1) * 128, qb * 512:(qb + 1) * 512], ob[:, :])], t)
        kb.phase_end(o_st.finals())


def consts2(kb, cfg):
    C = consts(kb, cfg["D"])
    if cfg["QL"] != 1024:
        C["onesQ"] = kb.persist("onesQ", [128, 128], F32)
        C["t"] = kb.sig("dve", kb.nc.vector.memset(C["onesQ"][:, :], 1.0 / cfg["QL"]))
    return C


def ffn_inputs(nc, cfg, pfx):
    D, DFF = cfg["D"], cfg["DFF"]
    KC, NFC = D // 128, DFF // 128
    w_in_d = nc.dram_tensor(pfx + "w_in", [2 * NFC, 128, KC, 128], F32, kind="ExternalInput").ap()
    w_out_d = nc.dram_tensor(pfx + "w_out", [KC, 128, NFC, 128], F32, kind="ExternalInput").ap()
    cw_d = nc.dram_tensor(pfx + "cw", [128, 2 * NFC, 4], F32, kind="ExternalInput").ap()
    gam_d = nc.dram_tensor(pfx + "gam", [128, KC], F32, kind="ExternalInput").ap()
    return w_in_d, w_out_d, cw_d, gam_d


def build_A(cfg, debug=False):
    kb = KB()
    nc = kb.nc
    D, DFF, NT, QL = cfg["D"], cfg["DFF"], cfg["NT"], cfg["QL"]
    KC, NFC, NQ = D // 128, DFF // 128, QL // 128
    CG = KC // 4
    xT = nc.dram_tensor("xT", [D, HALO_A + NT], F32, kind="ExternalInput").ap()
    invc = nc.dram_tensor("invc", [4, 128, HALO_A + NT], F32, kind="ExternalInput").ap()
    w_pool = nc.dram_tensor("w_pool", [4 * CG, 128, CG, 128], F32, kind="ExternalInput").ap()
    a_gam = nc.dram_tensor("a_gam", [128, KC], F32, kind="ExternalInput").ap()
    a_asc = nc.dram_tensor("a_asc", [128, KC], F32, kind="ExternalInput").ap()
    fw = ffn_inputs(nc, cfg, "f0_")
    w_dkv = nc.dram_tensor("w_dkv", [6, 128, KC, 128], F32, kind="ExternalInput").ap()
    w_dq = nc.dram_tensor("w_dq", [NQ, 128, KC, 128], F32, kind="ExternalInput").ap()
    vec = nc.dram_tensor("vec", [128, 2 * KC + 4 + NQ], F32, kind="ExternalInput").ap()
    cosT = nc.dram_tensor("cosT", [64, NT], F32, kind="ExternalInput").ap()
    sinT = nc.dram_tensor("sinT", [64, NT], F32, kind="ExternalInput").ap()
    h0T = nc.dram_tensor("h0T", [D, NT], F32, kind="ExternalOutput").ap()
    ckvn = nc.dram_tensor("ckvn", [512, NT], BF16, kind="ExternalOutput").ap()
    krope = nc.dram_tensor("krope", [64, NT], BF16, kind="ExternalOutput").ap()
    cqn = nc.dram_tensor("cqn", [QL, NT], BF16, kind="ExternalOutput").ap()
    hmid = nc.dram_tensor("hmid", [D, 2 + NT], F32, kind="ExternalOutput" if debug else "Internal").ap()
    gscr = nc.dram_tensor("gscr", [NT // 512, 128, NFC, 512], BF16).ap()
    C = consts2(kb, cfg)
    rstd = kb.persist("rstd", [128, HALO_A + NT], F32)
    pool_phase(kb, cfg, C, xT, hmid, w_pool, a_gam, a_asc, invc, rstd)
    ffn_phase(kb, cfg, hmid, h0T, gscr, *fw, TS=min(1024, NT))
    latent_phase(kb, cfg, C, h0T, w_dkv, w_dq, vec, cosT, sinT, ckvn, krope, cqn, rstd, TS=min(1024, NT))
    kb.es.close()
    return nc


def build_B(cfg):
    kb = KB()
    nc = kb.nc
    S, NH, QL = cfg["S"], cfg["NH"], cfg["QL"]
    NQ = QL // 128
    cqnT = nc.dram_tensor("cqnT", [QL, S], BF16, kind="ExternalInput").ap()
    ckvnT = nc.dram_tensor("ckvnT", [512, S], BF16, kind="ExternalInput").ap()
    kropeT = nc.dram_tensor("kropeT", [64, S], BF16, kind="ExternalInput").ap()
    wq = nc.dram_tensor("wq", [128, NQ, NH * 256], F32, kind="ExternalInput").ap()
    wk = nc.dram_tensor("wk", [128, 4, NH * 128], F32, kind="ExternalInput").ap()
    wv = nc.dram_tensor("wv", [128, 4, NH * 128], F32, kind="ExternalInput").ap()
    cosT = nc.dram_tensor("cosT", [64, S], F32, kind="ExternalInput").ap()
    sinT = nc.dram_tensor("sinT", [64, S], F32, kind="ExternalInput").ap()
    tri = nc.dram_tensor("tri", [128, 128], BF16, kind="ExternalInput").ap()
    oT = nc.dram_tensor("oT", [NH * 128, S], BF16, kind="ExternalOutput").ap()
    QnT = nc.dram_tensor("QnT", [NH, 128, S], BF16).ap()
    QrT = nc.dram_tensor("QrT", [NH, 64, S], BF16).ap()
    KnT = nc.dram_tensor("KnT", [NH, 128, S], BF16).ap()
    Vh = nc.dram_tensor("Vh", [NH, 128, S // 128, 128], BF16).ap()
    attn_pre_phase(kb, cfg, cqnT, ckvnT, wq, wk, wv, cosT, sinT, QnT, QrT, KnT, Vh)
    attn_phase(kb, cfg, QnT, QrT, KnT, Vh, kropeT, tri, oT)
    kb.es.close()
    return nc


def build_C(cfg):
    kb = KB()
    nc = kb.nc
    D, DFF, NT = cfg["D"], cfg["DFF"], cfg["NT"]
    KC, NFC = D // 128, DFF // 128
    oT = nc.dram_tensor("oT", [D, 2 + NT], BF16, kind="ExternalInput").ap()
    h0T = nc.dram_tensor("h0T", [D, 2 + NT], F32, kind="ExternalInput").ap()
    w_o = nc.dram_tensor("w_o", [KC, 128, KC, 128], F32, kind="ExternalInput").ap()
    fw = ffn_inputs(nc, cfg, "f1_")
    fgam = nc.dram_tensor("fgam", [128, KC], F32, kind="ExternalInput").ap()
    outT = nc.dram_tensor("outT", [D, NT], F32, kind="ExternalOutput").ap()
    h1T = nc.dram_tensor("h1T", [D, 2 + NT], F32).ap()
    h2T = nc.dram_tensor("h2T", [D, NT], F32).ap()
    gscr = nc.dram_tensor("gscr", [NT // 512, 128, NFC, 512], BF16).ap()
    C = consts(kb, D)
    rstd = kb.persist("rstd", [128, NT], F32)
    wo_phase(kb, cfg, oT, h0T, h1T, w_o, TS=min(1024, NT))
    ffn_phase(kb, cfg, h1T, h2T, gscr, *fw, TS=min(1024, NT))
    final_norm_phase(kb, cfg, C, h2T, outT, fgam, rstd)
    kb.es.close()
    return nc


import ml_dtypes
BF = ml_dtypes.bfloat16


def lay_slots(w):
    K, N = w.shape
    a = w.reshape(K // 128, 128, N // 128, 128)
    return np.ascontiguousarray(a.transpose(2, 1, 0, 3))


def lay_vec(g):
    return np.ascontiguousarray(np.asarray(g, np.float32).reshape(-1, 128).T)


def lay_ffn(w_in, w_out, conv_w, conv_b, gam, pfx):
    D, F2 = w_in.shape
    NFC = F2 // 256
    s = lay_slots(w_in)
    order = [fc + gv * NFC for fc in range(NFC) for gv in range(2)]
    cw = np.concatenate([conv_w, conv_b[None]], 0).reshape(4, 2 * NFC, 128)
    return {pfx + "w_in": np.ascontiguousarray(s[order]), pfx + "w_out": lay_slots(w_out),
            pfx + "cw": np.ascontiguousarray(cw.transpose(2, 1, 0)), pfx + "gam": lay_vec(gam)}


def rope_tables(pos):
    inv_freq = (np.float32(10000.0) ** (-(np.arange(0, 64, 2, dtype=np.float32)) / np.float32(64))).astype(np.float32)
    ang = (pos.astype(np.float32)[:, None] * inv_freq[None, :]).astype(np.float32)
    c, s = np.cos(ang).astype(np.float32), np.sin(ang).astype(np.float32)
    cosT = np.ascontiguousarray(np.concatenate([c, c], 1).T)
    sinT = np.ascontiguousarray(np.concatenate([-s, s], 1).T)
    return cosT, sinT


def halo_cols(fullT, c, NT, halo):
    F = fullT.shape[0]
    out = np.zeros((F, halo + NT), fullT.dtype)
    lo = c * NT - halo
    if lo < 0:
        out[:, -lo:] = fullT[:, 0:(c + 1) * NT]
    else:
        out[:] = fullT[:, lo:(c + 1) * NT]
    return out


def inputs_A(inp, c, NT):
    x = np.asarray(inp["x"])[0]
    D = x.shape[1]
    KC = D // 128
    CG = KC // 4
    m = {}
    m["xT"] = halo_cols(np.ascontiguousarray(x.T), c, NT, HALO_A)
    t = c * NT - HALO_A + np.arange(HALO_A + NT)
    iv = np.stack([np.where(t >= 0, 1.0 / np.minimum(np.maximum(t, 0) + 1, w), 1.0 / w) for w in POOL_WINDOWS]).astype(np.float32)
    m["invc"] = np.ascontiguousarray(np.broadcast_to(iv[:, None, :], (4, 128, HALO_A + NT)))
    m["w_pool"] = np.concatenate([lay_slots(np.asarray(inp["a_pool_w"])[0, g]) for g in range(4)], 0)
    m["a_gam"] = lay_vec(np.asarray(inp["a_norm"])[0])
    m["a_asc"] = lay_vec(np.asarray(inp["a_scale"])[0])
    m.update(lay_ffn(np.asarray(inp["ffn_w_in"])[0], np.asarray(inp["ffn_w_out"])[0], np.asarray(inp["ffn_conv_w"])[0],
                     np.asarray(inp["ffn_conv_b"])[0], np.asarray(inp["ffn_norm"])[0], "f0_"))
    wd = np.asarray(inp["w_dkv"])
    z = np.zeros((D, 64), np.float32)
    wd6 = np.concatenate([wd[:, :512], wd[:, 512:576], z, wd[:, 544:576], wd[:, 512:544], z], 1)
    m["w_dkv"] = lay_slots(wd6)
    m["w_dq"] = lay_slots(np.asarray(inp["w_dq"])[0])
    m["vec"] = np.ascontiguousarray(np.concatenate([lay_vec(inp["kv_norm"]), lay_vec(np.asarray(inp["b_norm"])[0]),
                                                    lay_vec(inp["kv_lat_norm"]), lay_vec(np.asarray(inp["q_lat_norm"])[0])], 1))
    m["cosT"], m["sinT"] = rope_tables(c * NT + np.arange(NT))
    return m


def inputs_B(inp, c, NH, cqnT, ckvnT, kropeT, S):
    m = {"cqnT": cqnT, "ckvnT": ckvnT, "kropeT": kropeT}
    wuq = np.asarray(inp["w_uq"])[0]
    QL = wuq.shape[0]
    hs = range(c * NH, (c + 1) * NH)
    cols = []
    for h in hs:
        cols += [wuq[:, h, 0:128], wuq[:, h, 128:192], wuq[:, h, 160:192], wuq[:, h, 128:160]]
    wq = np.concatenate(cols, 1)
    m["wq"] = np.ascontiguousarray(wq.reshape(QL // 128, 128, NH * 256).transpose(1, 0, 2))
    wukv = np.asarray(inp["w_ukv"])
    wk = np.concatenate([wukv[:, h, 0:128] for h in hs], 1)
    wv = np.concatenate([wukv[:, h, 128:256] for h in hs], 1)
    m["wk"] = np.ascontiguousarray(wk.reshape(4, 128, NH * 128).transpose(1, 0, 2))
    m["wv"] = np.ascontiguousarray(wv.reshape(4, 128, NH * 128).transpose(1, 0, 2))
    m["cosT"], m["sinT"] = rope_tables(np.arange(S))
    m["tri"] = (np.arange(128)[None, :] >= np.arange(128)[:, None]).astype(np.float32).astype(BF)
    return m


def inputs_C(inp, c, NT, oT_full, h0T_full):
    m = {"oT": halo_cols(oT_full, c, NT, 2), "h0T": halo_cols(h0T_full, c, NT, 2)}
    m["w_o"] = lay_slots(np.asarray(inp["w_o"])[0])
    m.update(lay_ffn(np.asarray(inp["ffn_w_in"])[1], np.asarray(inp["ffn_w_out"])[1], np.asarray(inp["ffn_conv_w"])[1],
                     np.asarray(inp["ffn_conv_b"])[1], np.asarray(inp["ffn_norm"])[1], "f1_"))
    m["fgam"] = lay_vec(inp["final_norm"])
    return m


from concourse.bass_utils import run_bass_kernel_spmd

CFG = dict(D=4096, DFF=11008, NT=2048, QL=1024, S=16384, NH=4)
NCORES = 8


def shared_A(inp):
    m = inputs_A(inp, 0, CFG["NT"])
    for k in ("xT", "invc", "cosT", "sinT"):
        m.pop(k)
    return m


def percore_A(inp, c, NT):
    x = np.asarray(inp["x"])[0]
    m = {}
    lo = max(0, c * NT - HALO_A)
    slab = np.ascontiguousarray(x[lo:(c + 1) * NT].T)
    xT = np.zeros((x.shape[1], HALO_A + NT), np.float32)
    xT[:, HALO_A + NT - slab.shape[1]:] = slab
    m["xT"] = xT
    t = c * NT - HALO_A + np.arange(HALO_A + NT)
    iv = np.stack([np.where(t >= 0, 1.0 / np.minimum(np.maximum(t, 0) + 1, w), 1.0 / w) for w in POOL_WINDOWS]).astype(np.float32)
    m["invc"] = np.ascontiguousarray(np.broadcast_to(iv[:, None, :], (4, 128, HALO_A + NT)))
    m["cosT"], m["sinT"] = rope_tables(c * NT + np.arange(NT))
    return m


def kernel(**inputs):
    cfg = CFG
    NT, S, NH = cfg["NT"], cfg["S"], cfg["NH"]
    cores = list(range(NCORES))
    shA = shared_A(inputs)
    mapsA = [dict(shA, **percore_A(inputs, c, NT)) for c in cores]
    resA = run_bass_kernel_spmd(build_A(cfg), mapsA, core_ids=cores).results
    del mapsA, shA
    h0T_full = np.concatenate([resA[c]["h0T"] for c in cores], 1)
    cqnT = np.ascontiguousarray(np.concatenate([resA[c]["cqn"] for c in cores], 1))
    ckvnT = np.ascontiguousarray(np.concatenate([resA[c]["ckvn"] for c in cores], 1))
    kropeT = np.ascontiguousarray(np.concatenate([resA[c]["krope"] for c in cores], 1))
    del resA
    mapsB = [inputs_B(inputs, c, NH, cqnT, ckvnT, kropeT, S) for c in cores]
    resB = run_bass_kernel_spmd(build_B(cfg), mapsB, core_ids=cores).results
    oT_full = np.concatenate([resB[c]["oT"] for c in cores], 0)
    del mapsB, resB
    shC = inputs_C(inputs, 0, NT, oT_full, h0T_full)
    mapsC = []
    for c in cores:
        m = dict(shC)
        m["oT"] = halo_cols(oT_full, c, NT, 2)
        m["h0T"] = halo_cols(h0T_full, c, NT, 2)
        mapsC.append(m)
    resC = run_bass_kernel_spmd(build_C(cfg), mapsC, core_ids=cores).results
    outT = np.concatenate([resC[c]["outT"] for c in cores], 1)
    return np.ascontiguousarray(outT.T)[None].astype(np.float32)
```

```python
from contextlib import ExitStack
import numpy as np
import concourse.bass as bass
import concourse.mybir as mybir

F32 = mybir.dt.float32
BF16 = mybir.dt.bfloat16
AF = mybir.ActivationFunctionType
ALU = mybir.AluOpType
EPS = 1e-6


class KB:
    def __init__(self):
        self.nc = bass.Bass("TRN2", target_bir_lowering=False)
        self.es = ExitStack()
        nc = self.nc
        self.engs = {"pe": nc.tensor, "act": nc.scalar, "dve": nc.vector, "pool": nc.gpsimd, "sp": nc.sync}
        self.sem = {}
        self.cnt = {}
        for e in self.engs:
            self.sem[e] = self.es.enter_context(nc.semaphore("s_" + e))
            self.cnt[e] = 0
        self.waited = {}
        self.nsem = 0
        self.banks = [nc.alloc_psum_tensor(f"bank{i}", [128, 512], F32) for i in range(8)]
        self.phase_es = None
        self.uid = 0
        self.sem_pool = []
        self.phase_sems = []

    def sig(self, e, ins):
        ins.then_inc(self.sem[e], 1)
        self.cnt[e] += 1
        return ("e", e, self.cnt[e])

    def wait(self, consumer, t):
        if t is None:
            return
        if isinstance(t, (list, tuple)) and t and isinstance(t[0], (list, tuple)):
            for x in t:
                self.wait(consumer, x)
            return
        kind, key, val = t
        k = (consumer, kind, key if kind == "e" else key.uid)
        if self.waited.get(k, 0) >= val:
            return
        s = self.sem[key] if kind == "e" else key.sem
        self.engs[consumer].wait_ge(s, val)
        self.waited[k] = val

    def newsem(self, name):
        self.nsem += 1
        return self.es.enter_context(self.nc.semaphore(f"{name}_{self.nsem}"))

    class DSem:
        def __init__(self, kb, name):
            if kb.sem_pool:
                self.sem, self.uid, self.cnt = kb.sem_pool.pop()
            else:
                self.sem = kb.newsem(name)
                self.uid = kb.nsem
                self.cnt = 0
            kb.phase_sems.append(self)

    def dma(self, q, dsem, out, in_):
        ins = self.engs[q].dma_start(out=out, in_=in_)
        ins.then_inc(dsem.sem, 16)
        dsem.cnt += 16
        return ("d", dsem, dsem.cnt)

    def phase_begin(self):
        self.phase_es = ExitStack()

    def sb(self, name, shape, dt):
        self.uid += 1
        return self.phase_es.enter_context(self.nc.sbuf_tensor(f"{name}_{self.uid}", list(shape), dt))

    def persist(self, name, shape, dt):
        self.uid += 1
        return self.es.enter_context(self.nc.sbuf_tensor(f"{name}_{self.uid}", list(shape), dt))

    def phase_end(self, final_tickets):
        for t in final_tickets:
            self.wait("sp", t)
        ins = self.nc.sync.sem_inc(self.sem["sp"], 1)
        self.cnt["sp"] += 1
        t = ("e", "sp", self.cnt["sp"])
        for e in ("pe", "act", "dve", "pool"):
            self.wait(e, t)
        self.phase_es.close()
        self.phase_es = None
        for d in self.phase_sems:
            self.sem_pool.append((d.sem, d.uid, d.cnt))
        self.phase_sems = []
        return t


class Ring:
    def __init__(self, bufs):
        self.bufs = bufs
        self.free = [None] * len(bufs)
        self.i = -1

    def next(self):
        self.i = (self.i + 1) % len(self.bufs)
        return self.i


class WStream:
    def __init__(self, kb, loads, ns=4, pf=3, q="pool"):
        self.kb = kb
        self.loads = loads
        self.ns, self.pf, self.q = ns, pf, q
        self.slots = [kb.sb("wslot", [128, 32, 128], BF16) for _ in range(ns)]
        self.dsem = [KB.DSem(kb, "wld") for _ in range(ns)]
        self.free = [None] * ns
        self.ticket = {}
        self.issued = 0
        for _ in range(min(pf, len(loads))):
            self._issue()

    def _issue(self):
        l = self.issued
        if l >= len(self.loads):
            return
        s = l % self.ns
        ap, nk = self.loads[l]
        self.kb.wait(self.q, self.free[s])
        self.ticket[l] = self.kb.dma(self.q, self.dsem[s], self.slots[s][:, 0:nk, :], ap)
        self.issued += 1

    def get(self, l):
        return self.slots[l % self.ns], self.ticket[l]

    def release(self, l, t):
        self.free[l % self.ns] = t
        self._issue()


class BankRing:
    def __init__(self, kb, idxs):
        self.kb = kb
        self.idxs = idxs
        self.free = {i: None for i in idxs}
        self.p = -1

    def next(self):
        self.p = (self.p + 1) % len(self.idxs)
        b = self.idxs[self.p]
        return b, self.kb.banks[b], self.free[b]

    def release(self, b, t):
        self.free[b] = t


def col_tiles(ncol, halo):
    tiles = []
    if halo:
        tiles.append((0, halo))
    c = halo
    while c < ncol:
        n = min(512, ncol - c)
        tiles.append((c, n))
        c += n
    return tiles


def rms_stats(kb, src_fn, nchunk, ncol, rstd, ones_f32, epsT, start_t=None):
    nc = kb.nc
    tiles = col_tiles(ncol, 0)
    assert len(tiles) <= 6
    xb = [kb.sb("rs_x", [128, ncol], F32) for _ in range(3)]
    xs = [KB.DSem(kb, "rs_xs") for _ in range(3)]
    xfree = [None] * 3
    sq = [kb.sb("rs_sq", [128, ncol], F32) for _ in range(2)]
    sqfree = [None] * 2
    lt = {}

    def load(k):
        i = k % 3
        kb.wait("sp", xfree[i])
        lt[k] = kb.dma("sp", xs[i], xb[i][:, :], src_fn(k))
    for k in range(min(2, nchunk)):
        load(k)
    last_pe = None
    for k in range(nchunk):
        if k + 2 < nchunk:
            load(k + 2)
        i, j = k % 3, k % 2
        kb.wait("act", lt[k])
        kb.wait("act", sqfree[j])
        if k == 0:
            kb.wait("act", start_t)
        ta = kb.sig("act", nc.scalar.activation(sq[j][:, :], xb[i][:, :], AF.Square))
        xfree[i] = ta
        kb.wait("pe", ta)
        if k == 0:
            kb.wait("pe", start_t)
        for ti, (c0, n) in enumerate(tiles):
            ins = nc.tensor.matmul(kb.banks[ti][:, 0:n], ones_f32[:, :], sq[j][:, c0:c0 + n],
                                   start=(k == 0), stop=(k == nchunk - 1))
        last_pe = kb.sig("pe", ins)
        sqfree[j] = last_pe
    kb.wait("act", last_pe)
    for ti, (c0, n) in enumerate(tiles):
        ins = nc.scalar.activation(rstd[:, c0:c0 + n], kb.banks[ti][:, 0:n], AF.Sqrt, bias=epsT[:, 0:1], scale=1.0)
    t1 = kb.sig("act", ins)
    kb.wait("dve", t1)
    t2 = kb.sig("dve", nc.vector.reciprocal(rstd[:, :], rstd[:, :]))
    return t2


def norm_apply(kb, src_fn, nchunk, ncol, gamma, rstd, rstd_t, dst_fn, dst_free=None, after=None):
    nc = kb.nc
    xb = [kb.sb("na_x", [128, ncol], F32) for _ in range(3)]
    xs = [KB.DSem(kb, "na_xs") for _ in range(3)]
    xfree = [None] * 3
    lt = {}

    def load(k):
        i = k % 3
        kb.wait("sp", xfree[i])
        lt[k] = kb.dma("sp", xs[i], xb[i][:, :], src_fn(k))
    for k in range(min(2, nchunk)):
        load(k)
    out_t = []
    kb.wait("dve", rstd_t)
    for k in range(nchunk):
        if k + 2 < nchunk:
            load(k + 2)
        i = k % 3
        kb.wait("dve", lt[k])
        if dst_free is not None:
            kb.wait("dve", dst_free(k))
        t = kb.sig("dve", nc.vector.scalar_tensor_tensor(dst_fn(k), xb[i][:, :], gamma[:, k:k + 1], rstd[:, :],
                                                          op0=ALU.mult, op1=ALU.mult))
        xfree[i] = t
        out_t.append(t)
        if after is not None:
            after(k, t)
    return out_t


def proj(kb, ws, x_fn, x_ready, douts, tiles, brng, evac, chunk_done=None, ms=None):
    nc = kb.nc
    first = True
    for di, parts in enumerate(douts):
        nk_tot = sum(p[1] for p in parts)
        m = 128 if ms is None else ms[di]
        last_t = None
        for ti, (c0, n) in enumerate(tiles):
            b, bank, bfree = brng.next()
            kb.wait("pe", bfree)
            if first:
                kb.wait("pe", x_ready)
                first = False
            kk = 0
            for (l, nk, k0) in parts:
                slot, lt = ws.get(l)
                kb.wait("pe", lt)
                for kc in range(nk):
                    ins = nc.tensor.matmul(bank[0:m, 0:n], slot[:, kc, 0:m], x_fn(k0 + kc, c0, n),
                                           start=(kk == 0), stop=(kk == nk_tot - 1))
                    kk += 1
            last_t = kb.sig("pe", ins)
            ft = evac(di, ti, c0, n, bank, last_t)
            brng.release(b, ft)
        for (l, nk, k0) in parts:
            ws.release(l, last_t)
        if chunk_done is not None:
            chunk_done(di)


def ffn_phase(kb, cfg, hin, hout, gscr, w_in_d, w_out_d, cw_d, gam_d, TS=1024):
    nc = kb.nc
    D, DFF, NT = cfg["D"], cfg["DFF"], cfg["NT"]
    KC, NFC = D // 128, DFF // 128
    final = []
    ucarry = kb.persist("ucarry", [128, 2 * NFC, 2], F32)
    for st in range(NT // TS):
        kb.phase_begin()
        ncol = 2 + TS
        cbase = st * TS
        ones = kb.sb("ones", [128, 128], F32)
        gam = kb.sb("gam", [128, KC], F32)
        cw = kb.sb("cw", [128, 2 * NFC, 4], F32)
        rstd = kb.sb("rstd", [128, ncol], F32)
        xnT = kb.sb("xnT", [128, KC, ncol], BF16)
        cs = KB.DSem(kb, "const")
        epsT = kb.sb("epsT", [128, 1], F32)
        nc.vector.memset(epsT[:, :], EPS)
        t0 = kb.sig("dve", nc.vector.memset(ones[:, :], 1.0 / D))
        kb.dma("sp", cs, gam[:, :], gam_d[:, :])
        tc = kb.dma("sp", cs, cw[:, :, :], cw_d[:, :, :])
        src = lambda k: hin[k * 128:(k + 1) * 128, cbase:cbase + ncol]
        rt = rms_stats(kb, src, KC, ncol, rstd, ones, epsT, start_t=t0)
        kb.wait("dve", tc)
        xt = norm_apply(kb, src, KC, ncol, gam, rstd, rt, lambda k: xnT[:, k, :])
        loads = [(w_in_d[i], KC) for i in range(2 * NFC)]
        ws = WStream(kb, loads)
        tiles = col_tiles(ncol, 2)
        if st > 0:
            tiles = tiles[1:]
        brng = BankRing(kb, list(range(8)))
        ub = [kb.sb("ubuf", [128, ncol], F32) for _ in range(2)]
        ubfree = [None, None]
        ab = {0: [kb.sb("ag", [128, TS], F32) for _ in range(2)], 1: [kb.sb("av", [128, TS], F32) for _ in range(2)]}
        abfree = {0: [None, None], 1: [None, None]}
        go = [kb.sb("gout", [128, TS], BF16) for _ in range(2)]
        gos = [KB.DSem(kb, "gst") for _ in range(2)]
        gofree = [None, None]
        state = {"evt": [], "conv": {}}

        def evac(di, ti, c0, n, bank, pt):
            u = di % 2
            kb.wait("act", pt)
            if ti == 0:
                kb.wait("act", ubfree[u])
                if st > 0:
                    nc.scalar.copy(ub[u][:, 0:2], ucarry[:, di, :])
            t = kb.sig("act", nc.scalar.copy(ub[u][:, c0:c0 + n], bank[:, 0:n]))
            if c0 + n == ncol and st + 1 < NT // TS:
                kb.wait("act", t)
                t2 = kb.sig("act", nc.scalar.copy(ucarry[:, di, :], ub[u][:, ncol - 2:ncol]))
                state["evt"] = t2
                return t
            state["evt"] = t
            return t

        def chunk_done(di):
            fc, gv = di // 2, di % 2
            u = di % 2
            r = fc % 2
            ci = fc if gv == 0 else NFC + fc
            a = ab[gv][r]
            kb.wait("dve", state["evt"])
            kb.wait("dve", abfree[gv][r])
            t = kb.sig("dve", nc.vector.tensor_scalar(a[:, :], ub[u][:, 2:2 + TS], cw[:, ci, 2:3], cw[:, ci, 3:4],
                                                      op0=ALU.mult, op1=ALU.add))
            kb.wait("dve", t)
            t = kb.sig("dve", nc.vector.scalar_tensor_tensor(a[:, :], ub[u][:, 1:1 + TS], cw[:, ci, 1:2], a[:, :],
                                                              op0=ALU.mult, op1=ALU.add))
            kb.wait("dve", t)
            t = kb.sig("dve", nc.vector.scalar_tensor_tensor(a[:, :], ub[u][:, 0:TS], cw[:, ci, 0:1], a[:, :],
                                                              op0=ALU.mult, op1=ALU.add))
            ubfree[u] = t
            if gv == 0:
                kb.wait("act", t)
                state["conv"][0] = kb.sig("act", nc.scalar.activation(a[:, :], a[:, :], AF.Silu))
            else:
                state["conv"][1] = t
                kb.wait("dve", state["conv"][0])
                kb.wait("dve", state["conv"][1])
                kb.wait("dve", gofree[r])
                tm = kb.sig("dve", nc.vector.tensor_tensor(go[r][:, :], ab[0][r][:, :], a[:, :], op=ALU.mult))
                abfree[0][r] = tm
                abfree[1][r] = tm
                kb.wait("sp", tm)
                for j in range(TS // 512):
                    tj = kb.dma("sp", gos[r], gscr[(cbase // 512) + j, :, fc, :], go[r][:, j * 512:(j + 1) * 512])
                gofree[r] = tj
                state["last_store"] = tj

        douts = [[(i, KC, 0)] for i in range(2 * NFC)]
        proj(kb, ws, lambda kc, c0, n: xnT[:, kc, c0:c0 + n], xt[-1], douts, tiles, brng, evac, chunk_done)
        kb.phase_end([gofree[0], gofree[1]])

        for tt in range(TS // 512):
            tile = cbase // 512 + tt
            kb.phase_begin()
            gT = kb.sb("gT", [128, NFC, 512], BF16)
            gs = KB.DSem(kb, "gld")
            nsplit = 4 if NFC >= 4 else 1
            step = (NFC + nsplit - 1) // nsplit
            for a0 in range(0, NFC, step):
                a1 = min(NFC, a0 + step)
                tg = kb.dma("sp", gs, gT[:, a0:a1, :], gscr[tile, :, a0:a1, :])
            parts_k = []
            k0 = 0
            while k0 < NFC:
                nk = min(32, NFC - k0)
                parts_k.append((k0, nk))
                k0 += nk
            loads = []
            douts = []
            for dc in range(KC):
                ps = []
                for (k0, nk) in parts_k:
                    ps.append((len(loads), nk, k0))
                    loads.append((w_out_d[dc, :, k0:k0 + nk, :], nk))
                douts.append(ps)
            ws = WStream(kb, loads)
            brng = BankRing(kb, list(range(8)))
            rb = [kb.sb("rbuf", [128, 512], F32) for _ in range(3)]
            rs = [KB.DSem(kb, "rld") for _ in range(3)]
            rfree = [None] * 3
            ob = [kb.sb("obuf", [128, 512], F32) for _ in range(2)]
            osm = [KB.DSem(kb, "ost") for _ in range(2)]
            ofree = [None, None]
            rt_ = {}

            def rload(dc):
                i = dc % 3
                kb.wait("sp", rfree[i])
                rt_[dc] = kb.dma("sp", rs[i], rb[i][:, :], hin[dc * 128:(dc + 1) * 128, 2 + tile * 512:2 + (tile + 1) * 512])
            rload(0)
            if KC > 1:
                rload(1)

            def evac2(di, ti, c0, n, bank, pt):
                i, o = di % 3, di % 2
                kb.wait("dve", pt)
                kb.wait("dve", rt_[di])
                kb.wait("dve", ofree[o])
                t = kb.sig("dve", nc.vector.tensor_tensor(ob[o][:, :], bank[:, 0:512], rb[i][:, :], op=ALU.add))
                rfree[i] = t
                if di + 2 < KC:
                    rload(di + 2)
                kb.wait("sp", t)
                ofree[o] = kb.dma("sp", osm[o], hout[di * 128:(di + 1) * 128, tile * 512:(tile + 1) * 512], ob[o][:, :])
                return t

            proj(kb, ws, lambda kc, c0, n: gT[:, kc, c0:c0 + n], tg, douts, [(0, 512)], brng, evac2)
            final = [ofree[0], ofree[1]]
            kb.phase_end(final)
    return final


POOL_WINDOWS = (2, 4, 8, 16)
HALO_A = 17


def consts(kb, D):
    nc = kb.nc
    c = {}
    c["epsT"] = kb.persist("epsT", [128, 1], F32)
    nc.vector.memset(c["epsT"][:, :], EPS)
    t = None
    for nm, val in (("onesD", 1.0 / D), ("ones512", 1.0 / 512), ("ones1024", 1.0 / 1024)):
        c[nm] = kb.persist(nm, [128, 128], F32)
        t = kb.sig("dve", nc.vector.memset(c[nm][:, :], val))
    c["t"] = t
    return c


def stats_phase(kb, src_fn, nchunk, ncol, rstd, ones, epsT, t0):
    kb.phase_begin()
    rt = rms_stats(kb, src_fn, nchunk, ncol, rstd, ones, epsT, start_t=t0)
    kb.phase_end([rt])


def pool_phase(kb, cfg, C, xT, hmid, w_pool_d, gam_d, asc_d, invc_d, rstd):
    nc = kb.nc
    D, NT = cfg["D"], cfg["NT"]
    KC = D // 128
    G = 4
    CG = KC // G
    ncol = HALO_A + NT
    nmid = 2 + NT
    OFF = HALO_A - 2
    stats_phase(kb, lambda k: xT[k * 128:(k + 1) * 128, :], KC, ncol, rstd, C["onesD"], C["epsT"], C["t"])
    kb.phase_begin()
    gam = kb.sb("gam", [128, KC], F32)
    asc = kb.sb("asc", [128, KC], F32)
    cs = KB.DSem(kb, "const")
    kb.dma("sp", cs, gam[:, :], gam_d[:, :])
    tc = kb.dma("sp", cs, asc[:, :], asc_d[:, :])
    pooled = kb.sb("pooled", [128, CG, nmid], BF16)
    pooled_free = None
    xl = [kb.sb("xl", [128, ncol], F32) for _ in range(2)]
    xls = [KB.DSem(kb, "xls") for _ in range(2)]
    xlfree = [None, None]
    xnb = kb.sb("xnb", [128, ncol], F32)
    tmp = [kb.sb("ptmp", [128, ncol], F32) for _ in range(2)]
    invc = kb.sb("invc", [128, ncol], F32)
    ivs = KB.DSem(kb, "ivs")
    loads = [(w_pool_d[i], CG) for i in range(G * CG)]
    ws = WStream(kb, loads)
    brng = BankRing(kb, list(range(8)))
    tiles = col_tiles(nmid, 2)
    rb = [kb.sb("presid", [128, nmid], F32) for _ in range(2)]
    rbs = [KB.DSem(kb, "prs") for _ in range(2)]
    rbfree = [None, None]
    ob = [kb.sb("pobuf", [128, nmid], F32) for _ in range(2)]
    obs = [KB.DSem(kb, "pos") for _ in range(2)]
    obfree = [None, None]
    kb.wait("dve", tc)
    last_dve = None
    nload = 0
    for g in range(G):
        kb.wait("sp", last_dve)
        ti = kb.dma("sp", ivs, invc[:, :], invc_d[g, :, :])
        for c in range(CG):
            k = g * CG + c
            i = nload % 2
            nload += 1
            kb.wait("sp", xlfree[i])
            tl = kb.dma("sp", xls[i], xl[i][:, :], xT[k * 128:(k + 1) * 128, :])
            kb.wait("dve", tl)
            kb.wait("dve", last_dve)
            t = kb.sig("dve", nc.vector.scalar_tensor_tensor(xnb[:, :], xl[i][:, :], gam[:, k:k + 1], rstd[:, 0:ncol],
                                                              op0=ALU.mult, op1=ALU.mult))
            xlfree[i] = t
            cur = xnb
            sh = 1
            for step in range(g + 1):
                dst = tmp[step % 2]
                kb.wait("dve", t)
                t = kb.sig("dve", nc.vector.tensor_tensor(dst[:, sh:ncol], cur[:, sh:ncol], cur[:, 0:ncol - sh], op=ALU.add))
                cur = dst
                sh *= 2
            other = tmp[(g + 1) % 2]
            kb.wait("dve", t)
            kb.wait("dve", ti)
            t = kb.sig("dve", nc.vector.tensor_tensor(other[:, OFF:ncol], cur[:, OFF:ncol], invc[:, OFF:ncol], op=ALU.mult))
            kb.wait("dve", t)
            if c == 0:
                kb.wait("dve", pooled_free)
            t = kb.sig("dve", nc.vector.tensor_tensor(pooled[:, c, :], other[:, OFF:ncol], xnb[:, OFF:ncol], op=ALU.subtract))
            last_dve = t
        st = {}

        def evac(di, ti_, c0, n, bank, pt, g=g):
            ko = g * CG + di
            r = ko % 2
            if ti_ == 0:
                kb.wait("sp", rbfree[r])
                st["rl"] = kb.dma("sp", rbs[r], rb[r][:, :], xT[ko * 128:(ko + 1) * 128, OFF:ncol])
                kb.wait("dve", obfree[r])
            kb.wait("dve", pt)
            kb.wait("dve", st["rl"])
            t = kb.sig("dve", nc.vector.scalar_tensor_tensor(ob[r][:, c0:c0 + n], bank[:, 0:n], asc[:, ko:ko + 1],
                                                              rb[r][:, c0:c0 + n], op0=ALU.mult, op1=ALU.add))
            st["t"] = t
            return t

        def chunk_done(di, g=g):
            ko = g * CG + di
            r = ko % 2
            rbfree[r] = st["t"]
            kb.wait("sp", st["t"])
            obfree[r] = kb.dma("sp", obs[r], hmid[ko * 128:(ko + 1) * 128, :], ob[r][:, :])

        douts = [[(g * CG + dc, CG, 0)] for dc in range(CG)]
        proj(kb, ws, lambda kc, c0, n: pooled[:, kc, c0:c0 + n], last_dve, douts, tiles, brng, evac, chunk_done)
        pooled_free = ("e", "pe", kb.cnt["pe"])
        last_dve = ("e", "dve", kb.cnt["dve"])
    kb.phase_end([obfree[0], obfree[1]])


def rms_sb(kb, buf, nch, ncol, ones, epsT, rstd, sq, buf_ready, banks=(0, 1)):
    nc = kb.nc
    tiles = col_tiles(ncol, 0)
    sqfree = [None, None]
    last = None
    for ch in range(nch):
        j = ch % 2
        kb.wait("act", buf_ready)
        kb.wait("act", sqfree[j])
        ta = kb.sig("act", nc.scalar.activation(sq[j][:, 0:ncol], buf[:, ch, :], AF.Square))
        kb.wait("pe", ta)
        for ti, (c0, n) in enumerate(tiles):
            ins = nc.tensor.matmul(kb.banks[banks[ti]][:, 0:n], ones[:, :], sq[j][:, c0:c0 + n],
                                   start=(ch == 0), stop=(ch == nch - 1))
        last = kb.sig("pe", ins)
        sqfree[j] = last
    kb.wait("act", last)
    for ti, (c0, n) in enumerate(tiles):
        ins = nc.scalar.activation(rstd[:, c0:c0 + n], kb.banks[banks[ti]][:, 0:n], AF.Sqrt, bias=epsT[:, 0:1], scale=1.0)
    t1 = kb.sig("act", ins)
    kb.wait("dve", t1)
    return kb.sig("dve", nc.vector.reciprocal(rstd[:, 0:ncol], rstd[:, 0:ncol]))


def latent_phase(kb, cfg, C, h0T, w_dkv_d, w_dq_d, vec_d, cosT_d, sinT_d, ckvn_o, krope_o, cqn_o, rstd, TS=1024):
    nc = kb.nc
    D, NT, QL = cfg["D"], cfg["NT"], cfg["QL"]
    KC, NQ = D // 128, QL // 128
    stats_phase(kb, lambda k: h0T[k * 128:(k + 1) * 128, :], KC, NT, rstd, C["onesD"], C["epsT"], C["t"])
    onesq = C["ones1024"] if QL == 1024 else C["onesQ"]
    for st in range(NT // TS):
        cb = st * TS
        kb.phase_begin()
        vec = kb.sb("vec", [128, 2 * KC + 4 + NQ], F32)
        cs = KB.DSem(kb, "const")
        tv = kb.dma("sp", cs, vec[:, :], vec_d[:, :])
        cosT = kb.sb("cosT", [64, TS], F32)
        sinT = kb.sb("sinT", [64, TS], F32)
        kb.dma("sp", cs, cosT[:, :], cosT_d[:, cb:cb + TS])
        tcs = kb.dma("sp", cs, sinT[:, :], sinT_d[:, cb:cb + TS])
        xnT = kb.sb("xnT", [128, KC, TS], BF16)
        ckv = kb.sb("ckv", [128, 6, TS], F32)
        sq = [kb.sb("lsq", [128, TS], F32) for _ in range(2)]
        rs2 = kb.sb("rs2", [128, TS], F32)
        ob = [kb.sb("lob", [128, TS], BF16) for _ in range(2)]
        obs = [KB.DSem(kb, "los") for _ in range(2)]
        obfree = [None, None]
        src = lambda k: h0T[k * 128:(k + 1) * 128, cb:cb + TS]
        kb.wait("dve", tv)
        ws = WStream(kb, [(w_dkv_d[i], KC) for i in range(6)])
        brng = BankRing(kb, [2, 3, 4, 5, 6, 7])
        tiles = col_tiles(TS, 0)
        xt = norm_apply(kb, src, KC, TS, vec[:, 0:KC], rstd[:, cb:cb + TS], None, lambda k: xnT[:, k, :])
        stt = {}

        def evac_kv(di, ti, c0, n, bank, pt):
            m = 128 if di < 4 else 64
            kb.wait("act", pt)
            t = kb.sig("act", nc.scalar.copy(ckv[0:m, di, c0:c0 + n], bank[0:m, 0:n]))
            stt["t"] = t
            return t
        proj(kb, ws, lambda kc, c0, n: xnT[:, kc, c0:c0 + n], xt[-1], [[(i, KC, 0)] for i in range(6)], tiles, brng,
             evac_kv, ms=[128] * 4 + [64, 64])
        pe_kv_done = ("e", "pe", kb.cnt["pe"])
        tr = rms_sb(kb, ckv, 4, TS, C["ones512"], C["epsT"], rs2, sq, stt["t"])
        nob = 0
        for ch in range(4):
            r = nob % 2
            nob += 1
            kb.wait("dve", tr)
            kb.wait("dve", obfree[r])
            t = kb.sig("dve", nc.vector.scalar_tensor_tensor(ob[r][:, :], ckv[:, ch, :], vec[:, 2 * KC + ch:2 * KC + ch + 1],
                                                              rs2[:, :], op0=ALU.mult, op1=ALU.mult))
            kb.wait("sp", t)
            obfree[r] = kb.dma("sp", obs[r], ckvn_o[ch * 128:(ch + 1) * 128, cb:cb + TS], ob[r][:, :])
        kb.wait("dve", tcs)
        kb.wait("dve", stt["t"])
        t = kb.sig("dve", nc.vector.tensor_tensor(ckv[0:64, 4, :], ckv[0:64, 4, :], cosT[:, :], op=ALU.mult))
        t = kb.sig("dve", nc.vector.tensor_tensor(ckv[0:64, 5, :], ckv[0:64, 5, :], sinT[:, :], op=ALU.mult))
        kb.wait("dve", t)
        r = nob % 2
        nob += 1
        kb.wait("dve", obfree[r])
        t = kb.sig("dve", nc.vector.tensor_tensor(ob[r][0:64, :], ckv[0:64, 4, :], ckv[0:64, 5, :], op=ALU.add))
        kb.wait("sp", t)
        obfree[r] = kb.dma("sp", obs[r], krope_o[:, cb:cb + TS], ob[r][0:64, :])
        kb.phase_end([obfree[0], obfree[1]])
        kb.phase_begin()
        vec = kb.sb("vec", [128, 2 * KC + 4 + NQ], F32)
        cs = KB.DSem(kb, "const")
        tv = kb.dma("sp", cs, vec[:, :], vec_d[:, :])
        xnT = kb.sb("xnT", [128, KC, TS], BF16)
        cq = kb.sb("cq", [128, NQ, TS], F32)
        sq = [kb.sb("lsq", [128, TS], F32) for _ in range(2)]
        rs2 = kb.sb("rs2", [128, TS], F32)
        ob = [kb.sb("lob", [128, TS], BF16) for _ in range(2)]
        obs = [KB.DSem(kb, "los") for _ in range(2)]
        obfree = [None, None]
        kb.wait("dve", tv)
        ws = WStream(kb, [(w_dq_d[i], KC) for i in range(NQ)])
        brng = BankRing(kb, [2, 3, 4, 5, 6, 7])
        xt = norm_apply(kb, src, KC, TS, vec[:, KC:2 * KC], rstd[:, cb:cb + TS], None, lambda k: xnT[:, k, :])

        def evac_q(di, ti, c0, n, bank, pt):
            kb.wait("act", pt)
            t = kb.sig("act", nc.scalar.copy(cq[:, di, c0:c0 + n], bank[:, 0:n]))
            stt["t"] = t
            return t
        proj(kb, ws, lambda kc, c0, n: xnT[:, kc, c0:c0 + n], xt[-1], [[(i, KC, 0)] for i in range(NQ)], tiles, brng, evac_q)
        tr = rms_sb(kb, cq, NQ, TS, onesq, C["epsT"], rs2, sq, stt["t"])
        for ch in range(NQ):
            r = nob % 2
            nob += 1
            kb.wait("dve", tr)
            kb.wait("dve", obfree[r])
            t = kb.sig("dve", nc.vector.scalar_tensor_tensor(ob[r][:, :], cq[:, ch, :], vec[:, 2 * KC + 4 + ch:2 * KC + 5 + ch],
                                                              rs2[:, :], op0=ALU.mult, op1=ALU.mult))
            kb.wait("sp", t)
            obfree[r] = kb.dma("sp", obs[r], cqn_o[ch * 128:(ch + 1) * 128, cb:cb + TS], ob[r][:, :])
        kb.phase_end([obfree[0], obfree[1]])


def wo_phase(kb, cfg, o_src, h0T, h1T, w_o_d, TS=1024):
    nc = kb.nc
    D, NT = cfg["D"], cfg["NT"]
    KC = D // 128
    for st in range(NT // TS):
        c_lo = 0 if st == 0 else 2 + st * TS
        ncol = TS + (2 if st == 0 else 0)
        kb.phase_begin()
        oS = kb.sb("oS", [128, KC, ncol], BF16)
        os_ = KB.DSem(kb, "old")
        for k0 in range(0, KC, 8):
            k1 = min(KC, k0 + 8)
            to = kb.dma("sp", os_, oS[:, k0:k1, :], o_src(k0, k1, c_lo, ncol).rearrange("(k p) c -> p k c", p=128))
        ws = WStream(kb, [(w_o_d[i], KC) for i in range(KC)])
        brng = BankRing(kb, list(range(8)))
        tiles = col_tiles(ncol, 2 if st == 0 else 0)
        rb = [kb.sb("wresid", [128, ncol], F32) for _ in range(2)]
        rbs = [KB.DSem(kb, "wrs") for _ in range(2)]
        rbfree = [None, None]
        ob = [kb.sb("wobuf", [128, ncol], F32) for _ in range(2)]
        obs = [KB.DSem(kb, "wos") for _ in range(2)]
        obfree = [None, None]
        stt = {}

        def evac(di, ti, c0, n, bank, pt):
            r = di % 2
            if ti == 0:
                kb.wait("sp", rbfree[r])
                stt["rl"] = kb.dma("sp", rbs[r], rb[r][:, :], h0T[di * 128:(di + 1) * 128, c_lo:c_lo + ncol])
                kb.wait("dve", obfree[r])
            kb.wait("dve", pt)
            kb.wait("dve", stt["rl"])
            t = kb.sig("dve", nc.vector.tensor_tensor(ob[r][:, c0:c0 + n], bank[:, 0:n], rb[r][:, c0:c0 + n], op=ALU.add))
            stt["t"] = t
            return t

        def chunk_done(di):
            r = di % 2
            rbfree[r] = stt["t"]
            kb.wait("sp", stt["t"])
            obfree[r] = kb.dma("sp", obs[r], h1T[di * 128:(di + 1) * 128, c_lo:c_lo + ncol], ob[r][:, :])
        proj(kb, ws, lambda kc, c0, n: oS[:, kc, c0:c0 + n], to, [[(i, KC, 0)] for i in range(KC)], tiles, brng, evac, chunk_done)
        kb.phase_end([obfree[0], obfree[1]])


def final_norm_phase(kb, cfg, C, hT, outT, gam_d, rstd):
    nc = kb.nc
    D, NT = cfg["D"], cfg["NT"]
    KC = D // 128
    stats_phase(kb, lambda k: hT[k * 128:(k + 1) * 128, :], KC, NT, rstd, C["onesD"], C["epsT"], C["t"])
    kb.phase_begin()
    gam = kb.sb("fgam", [128, KC], F32)
    cs = KB.DSem(kb, "const")
    tg = kb.dma("sp", cs, gam[:, :], gam_d[:, :])
    ob = [kb.sb("fob", [128, NT], F32) for _ in range(2)]
    obs = [KB.DSem(kb, "fos") for _ in range(2)]
    obfree = [None, None]
    kb.wait("dve", tg)
    def after(k, t):
        kb.wait("sp", t)
        obfree[k % 2] = kb.dma("sp", obs[k % 2], outT[k * 128:(k + 1) * 128, :], ob[k % 2][:, :])
    norm_apply(kb, lambda k: hT[k * 128:(k + 1) * 128, :], KC, NT, gam, rstd[:, 0:NT], None,
               lambda k: ob[k % 2][:, :], dst_free=lambda k: obfree[k % 2], after=after)
    kb.phase_end([obfree[0], obfree[1]])


class StageRing:
    def __init__(self, kb, name, n, shape, dt):
        self.kb, self.n = kb, n
        self.bufs = [kb.sb(name, shape, dt) for _ in range(n)]
        self.ds = [KB.DSem(kb, name + "s") for _ in range(n)]
        self.free = [None] * n
        self.i = -1

    def next(self, eng):
        self.i = (self.i + 1) % self.n
        self.kb.wait(eng, self.free[self.i])
        return self.bufs[self.i]

    def store(self, pairs, t):
        self.kb.wait("sp", t)
        for dram_ap, src_ap in pairs:
            self.free[self.i] = self.kb.dma("sp", self.ds[self.i], dram_ap, src_ap)

    def finals(self):
        return [f for f in self.free if f is not None]


class LoadRing:
    def __init__(self, kb, name, n, shapes):
        self.kb, self.n = kb, n
        self.bufs = [[kb.sb(name, sh, dt) for (sh, dt) in shapes] for _ in range(n)]
        self.ds = [KB.DSem(kb, name + "l") for _ in range(n)]
        self.free = [None] * n
        self.i = -1

    def load(self, fn):
        self.i = (self.i + 1) % self.n
        s = self.i
        self.kb.wait("sp", self.free[s])
        t = None
        for dst, src in fn(self.bufs[s]):
            t = self.kb.dma("sp", self.ds[s], dst, src)
        return s, self.bufs[s], t

    def release(self, s, t):
        self.free[s] = t


def attn_pre_phase(kb, cfg, cq_src, ckv_src, wq_d, wk_d, wv_d, cosT_d, sinT_d, QnT, QrT, KnT, Vh):
    nc = kb.nc
    S, NH, QL = cfg["S"], cfg["NH"], cfg["QL"]
    NQ = QL // 128
    sm = float(192 ** -0.5)
    kb.phase_begin()
    wq = kb.sb("wq", [128, NQ, NH * 256], BF16)
    wk = kb.sb("wk", [128, 4, NH * 128], BF16)
    wv = kb.sb("wv", [128, 4, NH * 128], BF16)
    cs = KB.DSem(kb, "wl")
    kb.dma("pool", cs, wq[:, :, :], wq_d[:, :, :])
    kb.dma("pool", cs, wk[:, :, :], wk_d[:, :, :])
    tw = kb.dma("pool", cs, wv[:, :, :], wv_d[:, :, :])
    inr = LoadRing(kb, "pin", 2, [([128, NQ, 512], BF16), ([128, 4, 512], BF16), ([64, 512], F32), ([64, 512], F32)])
    qn_o = StageRing(kb, "qno", 2, [128, 512], BF16)
    qr_o = StageRing(kb, "qro", 2, [64, 512], BF16)
    kn_o = StageRing(kb, "kno", 2, [128, 512], BF16)
    v_o = StageRing(kb, "vo", 2, [128, NH * 128], BF16)
    ta_ = [kb.sb("qra", [64, 512], F32) for _ in range(2)]
    tb_ = [kb.sb("qrb", [64, 512], F32) for _ in range(2)]
    tfree = [None, None]
    brng = BankRing(kb, list(range(8)))
    kb.wait("pe", tw)
    nqr = 0
    pending = None
    for it in range(S // 512):
        c0 = it * 512

        def ld(bufs, c0=c0):
            cq, ckv, co, si = bufs
            return [(cq[:, :, :], cq_src(c0).rearrange("(k p) c -> p k c", p=128)),
                    (ckv[:, :, :], ckv_src(c0).rearrange("(k p) c -> p k c", p=128)),
                    (co[:, :], cosT_d[:, c0:c0 + 512]), (si[:, :], sinT_d[:, c0:c0 + 512])]
        slot, (cq, ckv, co, si), tl = inr.load(ld)
        kb.wait("pe", tl)
        for h in range(NH):
            b, bank, bf = brng.next()
            kb.wait("pe", bf)
            for kc in range(NQ):
                ins = nc.tensor.matmul(bank[:, :], wq[:, kc, h * 256:h * 256 + 128], cq[:, kc, :], start=(kc == 0), stop=(kc == NQ - 1))
            tp = kb.sig("pe", ins)
            ob = qn_o.next("act")
            kb.wait("act", tp)
            t = kb.sig("act", nc.scalar.activation(ob[:, :], bank[:, :], AF.Copy, scale=sm))
            brng.release(b, t)
            qn_o.store([(QnT[h, :, c0:c0 + 512], ob[:, :])], t)
            r = nqr % 2
            nqr += 1
            tt = []
            for which, dst in ((0, ta_[r]), (1, tb_[r])):
                b, bank, bf = brng.next()
                kb.wait("pe", bf)
                o0 = h * 256 + 128 + 64 * which
                for kc in range(NQ):
                    ins = nc.tensor.matmul(bank[0:64, :], wq[:, kc, o0:o0 + 64], cq[:, kc, :], start=(kc == 0), stop=(kc == NQ - 1))
                tp = kb.sig("pe", ins)
                kb.wait("act", tp)
                kb.wait("act", tfree[r])
                t = kb.sig("act", nc.scalar.activation(dst[:, :], bank[0:64, :], AF.Copy, scale=sm))
                brng.release(b, t)
                tt.append(t)
            kb.wait("dve", tt[1])
            kb.wait("dve", tl)
            nc.vector.tensor_tensor(ta_[r][:, :], ta_[r][:, :], co[:, :], op=ALU.mult)
            t = kb.sig("dve", nc.vector.tensor_tensor(tb_[r][:, :], tb_[r][:, :], si[:, :], op=ALU.mult))
            kb.wait("dve", t)
            ob = qr_o.next("dve")
            t = kb.sig("dve", nc.vector.tensor_tensor(ob[:, :], ta_[r][:, :], tb_[r][:, :], op=ALU.add))
            tfree[r] = t
            qr_o.store([(QrT[h, :, c0:c0 + 512], ob[:, :])], t)
            last_dve = t
            b, bank, bf = brng.next()
            kb.wait("pe", bf)
            for kc in range(4):
                ins = nc.tensor.matmul(bank[:, :], wk[:, kc, h * 128:(h + 1) * 128], ckv[:, kc, :], start=(kc == 0), stop=(kc == 3))
            tp = kb.sig("pe", ins)
            ob = kn_o.next("act")
            kb.wait("act", tp)
            t = kb.sig("act", nc.scalar.copy(ob[:, :], bank[:, :]))
            brng.release(b, t)
            kn_o.store([(KnT[h, :, c0:c0 + 512], ob[:, :])], t)
        for j in range(4):
            b, bank, bf = brng.next()
            kb.wait("pe", bf)
            for kc in range(4):
                ins = nc.tensor.matmul(bank[:, 0:NH * 128], ckv[:, kc, j * 128:(j + 1) * 128], wv[:, kc, :], start=(kc == 0), stop=(kc == 3))
            tp = kb.sig("pe", ins)
            ob = v_o.next("act")
            kb.wait("act", tp)
            t = kb.sig("act", nc.scalar.copy(ob[:, :], bank[:, 0:NH * 128]))
            brng.release(b, t)
            v_o.store([(Vh[h, :, it * 4 + j, :], ob[:, h * 128:(h + 1) * 128]) for h in range(NH)], t)
        inr.release(slot, [("e", "pe", kb.cnt["pe"]), last_dve])
    kb.phase_end(qn_o.finals() + qr_o.finals() + kn_o.finals() + v_o.finals())


POOL_SHARE = 4


def attn_phase(kb, cfg, QnT, QrT, KnT, Vh, kr_src, tri_d, o_dst):
    nc = kb.nc
    S, NH = cfg["S"], cfg["NH"]
    NQB = S // 512
    for h in range(NH):
        kb.phase_begin()
        KT = kb.sb("KT", [128, S], BF16)
        KR = kb.sb("KR", [128, S], BF16)
        VS = kb.sb("VS", [128, S // 128, 128], BF16)
        ones = kb.sb("aones", [128, 128], BF16)
        tri = kb.sb("tri", [128, 128], BF16)
        rec = kb.sb("rec", [128, 512], F32)
        onesf = kb.sb("aonesf", [128, 128], F32)
        nc.vector.memset(onesf[:, :], 1.0)
        accs = [[kb.sb("lacc", [128, 512], F32) for _ in range(3)] for _ in range(2)]
        accfree = [None, None]
        recfree = [None]
        nc.vector.memset(KR[64:128, :], 0.0)
        t1 = kb.sig("dve", nc.vector.memset(ones[:, :], 1.0))
        cs = KB.DSem(kb, "kvl")
        kb.dma("sp", cs, tri[:, :], tri_d[:, :])
        npc = 4 if S >= 2048 else 1
        w = S // npc
        for p in range(npc):
            kb.dma("sp", cs, KT[:, p * w:(p + 1) * w], KnT[h, :, p * w:(p + 1) * w])
            tkv = kb.dma("sp", cs, VS[:, p * (w // 128):(p + 1) * (w // 128), :], Vh[h, :, p * (w // 128):(p + 1) * (w // 128), :])
        for (c_a, c_b, ap) in kr_src:
            tkv = kb.dma("sp", cs, KR[0:64, c_a:c_b], ap)
        kb.wait("pe", tkv)
        kb.wait("pe", t1)
        kb.wait("dve", tkv)
        qring = LoadRing(kb, "qb", 3, [([128, 512], BF16), ([128, 512], BF16)])
        for bufs_ in qring.bufs:
            t1 = kb.sig("dve", nc.vector.memset(bufs_[1][64:128, :], 0.0))
        kb.wait("pe", t1)
        pb = [kb.sb("pb", [128, 512], BF16) for _ in range(4)]
        pfree = [None] * 4
        sfree = [None] * 4
        o_st = StageRing(kb, "ao", 2, [128, 512], BF16)
        olfree = [None, None]
        gi = 0

        def qload(qb):
            return qring.load(lambda bufs: [(bufs[0][:, :], QnT[h, :, qb * 512:(qb + 1) * 512]),
                                            (bufs[1][0:64, :], QrT[h, :, qb * 512:(qb + 1) * 512])])
        nxt = qload(0)
        for qb in range(NQB):
            qslot, (qn, qr), tq = nxt
            if qb + 1 < NQB:
                nxt = qload(qb + 1)
            O = kb.banks[4 + 2 * (qb % 2)]
            L = kb.banks[5 + 2 * (qb % 2)]
            items = [(kt, None) for kt in range(4 * qb)] + [(4 * qb + j, j) for j in range(4)]
            n = len(items)
            ready = {}
            kb.wait("pe", tq)
            acc = accs[qb % 2]
            kb.wait("pool", accfree[qb % 2])
            nc.gpsimd.memset(acc[0][:, :], 0.0)
            nc.gpsimd.memset(acc[1][:, :], 0.0)
            tms = kb.sig("pool", nc.gpsimd.memset(acc[2][:, :], 0.0))
            acct = [tms, tms, tms]
            dcnt = [0]

            def emitS(i, g):
                kt, j = items[i]
                off = 0 if j is None else 128 * j
                sb_ = g % 4
                bank = kb.banks[sb_]
                kb.wait("pe", sfree[sb_])
                nc.tensor.matmul(bank[:, off:512], KT[:, kt * 128:(kt + 1) * 128], qn[:, off:512], start=True, stop=False)
                ts = kb.sig("pe", nc.tensor.matmul(bank[:, off:512], KR[:, kt * 128:(kt + 1) * 128], qr[:, off:512],
                                                   start=False, stop=True))
                kb.wait("act", ts)
                kb.wait("act", pfree[sb_])
                te = kb.sig("act", nc.scalar.activation(pb[sb_][:, off:512], bank[:, off:512], AF.Exp))
                sfree[sb_] = te
                if j is not None:
                    kb.wait("dve", te)
                    te = kb.sig("dve", nc.vector.tensor_tensor(pb[sb_][:, off:off + 128], pb[sb_][:, off:off + 128],
                                                               tri[:, :], op=ALU.mult))
                if j is None and POOL_SHARE and i % POOL_SHARE == POOL_SHARE - 1:
                    a_ = 2
                    kb.wait("pool", te)
                    kb.wait("pool", acct[a_])
                    acct[a_] = kb.sig("pool", nc.gpsimd.tensor_tensor(acc[a_][:, off:512], acc[a_][:, off:512], pb[sb_][:, off:512], op=ALU.add))
                else:
                    a_ = dcnt[0] % 2
                    dcnt[0] += 1
                    kb.wait("dve", te)
                    kb.wait("dve", acct[a_])
                    acct[a_] = kb.sig("dve", nc.vector.tensor_tensor(acc[a_][:, off:512], acc[a_][:, off:512], pb[sb_][:, off:512], op=ALU.add))
                ready[i] = (te, sb_, off, kt, acct[a_])

            def emitPV(i):
                te, sb_, off, kt, ta_ = ready[i]
                kb.wait("pe", te)
                if i == 0:
                    kb.wait("pe", olfree[qb % 2])
                tp = kb.sig("pe", nc.tensor.matmul(O[:, off:512], VS[:, kt, :], pb[sb_][:, off:512], start=(i == 0), stop=(i == n - 1)))
                pfree[sb_] = [tp, ta_]
                return tp
            LA = 2
            for i in range(min(LA, n)):
                emitS(i, gi + i)
            tp = None
            for i in range(n):
                if i + LA < n:
                    emitS(i + LA, gi + i + LA)
                tp = emitPV(i)
            gi += n
            qring.release(qslot, tp)
            for a_ in range(3):
                kb.wait("pe", acct[a_])
            nc.tensor.matmul(L[:, :], onesf[:, :], acc[0][:, :], start=True, stop=False)
            nc.tensor.matmul(L[:, :], onesf[:, :], acc[1][:, :], start=False, stop=False)
            tp = kb.sig("pe", nc.tensor.matmul(L[:, :], onesf[:, :], acc[2][:, :], start=False, stop=True))
            accfree[qb % 2] = tp
            kb.wait("act", tp)
            kb.wait("act", recfree[0])
            tr_ = kb.sig("act", nc.scalar.activation(rec[:, :], L[:, :], AF.Ln))
            kb.wait("act", tr_)
            tr_ = kb.sig("act", nc.scalar.activation(rec[:, :], rec[:, :], AF.Exp, scale=-1.0))
            kb.wait("dve", tr_)
            ob = o_st.next("dve")
            t = kb.sig("dve", nc.vector.tensor_tensor(ob[:, :], O[:, :], rec[:, :], op=ALU.mult))
            recfree[0] = t
            olfree[qb % 2] = t
            o_st.store([(o_dst(h, qb), ob[:, :])], t)
        kb.phase_end(o_st.finals())


def consts2(kb, cfg):
    C = consts(kb, cfg["D"])
    if cfg["QL"] != 1024:
        C["onesQ"] = kb.persist("onesQ", [128, 128], F32)
        C["t"] = kb.sig("dve", kb.nc.vector.memset(C["onesQ"][:, :], 1.0 / cfg["QL"]))
    return C


def ffn_inputs(nc, cfg, pfx):
    D, DFF = cfg["D"], cfg["DFF"]
    KC, NFC = D // 128, DFF // 128
    w_in_d = nc.dram_tensor(pfx + "w_in", [2 * NFC, 128, KC, 128], F32, kind="ExternalInput").ap()
    w_out_d = nc.dram_tensor(pfx + "w_out", [KC, 128, NFC, 128], F32, kind="ExternalInput").ap()
    cw_d = nc.dram_tensor(pfx + "cw", [128, 2 * NFC, 4], F32, kind="ExternalInput").ap()
    gam_d = nc.dram_tensor(pfx + "gam", [128, KC], F32, kind="ExternalInput").ap()
    return w_in_d, w_out_d, cw_d, gam_d


def build_A(cfg, debug=False):
    kb = KB()
    nc = kb.nc
    D, DFF, NT, QL = cfg["D"], cfg["DFF"], cfg["NT"], cfg["QL"]
    KC, NFC, NQ = D // 128, DFF // 128, QL // 128
    CG = KC // 4
    xT = nc.dram_tensor("xT", [D, HALO_A + NT], F32, kind="ExternalInput").ap()
    invc = nc.dram_tensor("invc", [4, 128, HALO_A + NT], F32, kind="ExternalInput").ap()
    w_pool = nc.dram_tensor("w_pool", [4 * CG, 128, CG, 128], F32, kind="ExternalInput").ap()
    a_gam = nc.dram_tensor("a_gam", [128, KC], F32, kind="ExternalInput").ap()
    a_asc = nc.dram_tensor("a_asc", [128, KC], F32, kind="ExternalInput").ap()
    fw = ffn_inputs(nc, cfg, "f0_")
    w_dkv = nc.dram_tensor("w_dkv", [6, 128, KC, 128], F32, kind="ExternalInput").ap()
    w_dq = nc.dram_tensor("w_dq", [NQ, 128, KC, 128], F32, kind="ExternalInput").ap()
    vec = nc.dram_tensor("vec", [128, 2 * KC + 4 + NQ], F32, kind="ExternalInput").ap()
    cosT = nc.dram_tensor("cosT", [64, NT], F32, kind="ExternalInput").ap()
    sinT = nc.dram_tensor("sinT", [64, NT], F32, kind="ExternalInput").ap()
    h0T = nc.dram_tensor("h0T", [D, NT], F32, kind="ExternalOutput").ap()
    ckvn = nc.dram_tensor("ckvn", [512, NT], BF16, kind="ExternalOutput").ap()
    krope = nc.dram_tensor("krope", [64, NT], BF16, kind="ExternalOutput").ap()
    cqn = nc.dram_tensor("cqn", [QL, NT], BF16, kind="ExternalOutput").ap()
    hmid = nc.dram_tensor("hmid", [D, 2 + NT], F32, kind="ExternalOutput" if debug else "Internal").ap()
    gscr = nc.dram_tensor("gscr", [NT // 512, 128, NFC, 512], BF16).ap()
    C = consts2(kb, cfg)
    rstd = kb.persist("rstd", [128, HALO_A + NT], F32)
    pool_phase(kb, cfg, C, xT, hmid, w_pool, a_gam, a_asc, invc, rstd)
    ffn_phase(kb, cfg, hmid, h0T, gscr, *fw, TS=min(1024, NT))
    latent_phase(kb, cfg, C, h0T, w_dkv, w_dq, vec, cosT, sinT, ckvn, krope, cqn, rstd, TS=min(1024, NT))
    kb.es.close()
    return nc


def build_B(cfg):
    kb = KB()
    nc = kb.nc
    S, NH, QL = cfg["S"], cfg["NH"], cfg["QL"]
    NQ = QL // 128
    cqnT = nc.dram_tensor("cqnT", [QL, S], BF16, kind="ExternalInput").ap()
    ckvnT = nc.dram_tensor("ckvnT", [512, S], BF16, kind="ExternalInput").ap()
    kropeT = nc.dram_tensor("kropeT", [64, S], BF16, kind="ExternalInput").ap()
    wq = nc.dram_tensor("wq", [128, NQ, NH * 256], F32, kind="ExternalInput").ap()
    wk = nc.dram_tensor("wk", [128, 4, NH * 128], F32, kind="ExternalInput").ap()
    wv = nc.dram_tensor("wv", [128, 4, NH * 128], F32, kind="ExternalInput").ap()
    cosT = nc.dram_tensor("cosT", [64, S], F32, kind="ExternalInput").ap()
    sinT = nc.dram_tensor("sinT", [64, S], F32, kind="ExternalInput").ap()
    tri = nc.dram_tensor("tri", [128, 128], BF16, kind="ExternalInput").ap()
    oT = nc.dram_tensor("oT", [NH * 128, S], BF16, kind="ExternalOutput").ap()
    QnT = nc.dram_tensor("QnT", [NH, 128, S], BF16).ap()
    QrT = nc.dram_tensor("QrT", [NH, 64, S], BF16).ap()
    KnT = nc.dram_tensor("KnT", [NH, 128, S], BF16).ap()
    Vh = nc.dram_tensor("Vh", [NH, 128, S // 128, 128], BF16).ap()
    attn_pre_phase(kb, cfg, lambda c0: cqnT[:, c0:c0 + 512], lambda c0: ckvnT[:, c0:c0 + 512], wq, wk, wv, cosT, sinT, QnT, QrT, KnT, Vh)
    attn_phase(kb, cfg, QnT, QrT, KnT, Vh, [(0, S, kropeT[:, :])], tri,
               lambda h, qb: oT[h * 128:(h + 1) * 128, qb * 512:(qb + 1) * 512])
    kb.es.close()
    return nc


def build_C(cfg):
    kb = KB()
    nc = kb.nc
    D, DFF, NT = cfg["D"], cfg["DFF"], cfg["NT"]
    KC, NFC = D // 128, DFF // 128
    oT = nc.dram_tensor("oT", [D, 2 + NT], BF16, kind="ExternalInput").ap()
    h0T = nc.dram_tensor("h0T", [D, 2 + NT], F32, kind="ExternalInput").ap()
    w_o = nc.dram_tensor("w_o", [KC, 128, KC, 128], F32, kind="ExternalInput").ap()
    fw = ffn_inputs(nc, cfg, "f1_")
    fgam = nc.dram_tensor("fgam", [128, KC], F32, kind="ExternalInput").ap()
    outT = nc.dram_tensor("outT", [D, NT], F32, kind="ExternalOutput").ap()
    h1T = nc.dram_tensor("h1T", [D, 2 + NT], F32).ap()
    h2T = nc.dram_tensor("h2T", [D, NT], F32).ap()
    gscr = nc.dram_tensor("gscr", [NT // 512, 128, NFC, 512], BF16).ap()
    C = consts(kb, D)
    rstd = kb.persist("rstd", [128, NT], F32)
    wo_phase(kb, cfg, lambda k0, k1, c_lo, ncol: oT[k0 * 128:k1 * 128, c_lo:c_lo + ncol], h0T, h1T, w_o, TS=min(1024, NT))
    ffn_phase(kb, cfg, h1T, h2T, gscr, *fw, TS=min(1024, NT))
    final_norm_phase(kb, cfg, C, h2T, outT, fgam, rstd)
    kb.es.close()
    return nc


def cc_allgather(kb, csem, src, dst, nranks):
    ins = kb.nc.gpsimd.collective_compute("AllGather", ALU.bypass, [list(range(nranks))], ins=[src], outs=[dst])
    ins.then_inc(csem.sem)
    csem.cnt += 1
    return ("d", csem, csem.cnt)


def build_F(cfg):
    kb = KB()
    nc = kb.nc
    D, DFF, NT, QL, S, NH = cfg["D"], cfg["DFF"], cfg["NT"], cfg["QL"], cfg["S"], cfg["NH"]
    KC, NFC, NQ = D // 128, DFF // 128, QL // 128
    CG = KC // 4
    NR = S // NT
    LR = QL + 576
    TS = min(1024, NT)
    ext = lambda name, shape, dt: nc.dram_tensor(name, shape, dt, kind="ExternalInput").ap()
    xT = ext("xT", [D, HALO_A + NT], F32)
    invc = ext("invc", [4, 128, HALO_A + NT], F32)
    w_pool = ext("w_pool", [4 * CG, 128, CG, 128], F32)
    a_gam = ext("a_gam", [128, KC], F32)
    a_asc = ext("a_asc", [128, KC], F32)
    fw0 = ffn_inputs(nc, cfg, "f0_")
    w_dkv = ext("w_dkv", [6, 128, KC, 128], F32)
    w_dq = ext("w_dq", [NQ, 128, KC, 128], F32)
    vec = ext("vec", [128, 2 * KC + 4 + NQ], F32)
    cosT = ext("cosT", [64, NT], F32)
    sinT = ext("sinT", [64, NT], F32)
    wq = ext("wq", [128, NQ, NH * 256], F32)
    wk = ext("wk", [128, 4, NH * 128], F32)
    wv = ext("wv", [128, 4, NH * 128], F32)
    cosF = ext("cosF", [64, S], F32)
    sinF = ext("sinF", [64, S], F32)
    tri = ext("tri", [128, 128], BF16)
    w_o = ext("w_o", [KC, 128, KC, 128], F32)
    fw1 = ffn_inputs(nc, cfg, "f1_")
    fgam = ext("fgam", [128, KC], F32)
    outT = nc.dram_tensor("outT", [D, NT], F32, kind="ExternalOutput").ap()
    it = lambda name, shape, dt: nc.dram_tensor(name, shape, dt).ap()
    hmid = it("hmid", [D, 2 + NT], F32)
    gscr = it("gscr", [NT // 512, 128, NFC, 512], BF16)
    h0x = it("h0x", [D, 2 + NT], F32)
    lat_local = it("lat_local", [LR, NT], BF16)
    lat_all = it("lat_all", [NR * LR, NT], BF16)
    hh_local = it("hh_local", [D, 2], F32)
    hh_all = it("hh_all", [NR * D, 2], F32)
    hh_pad = it("hh_pad", [(NR + 1) * D, 2], F32)
    o_local = it("o_local", [NH * 128, 2 + S], BF16)
    o_all = it("o_all", [NR * NH * 128, 2 + S], BF16)
    QnT = it("QnT", [NH, 128, S], BF16)
    QrT = it("QrT", [NH, 64, S], BF16)
    KnT = it("KnT", [NH, 128, S], BF16)
    Vh = it("Vh", [NH, 128, S // 128, 128], BF16)
    h1T = it("h1T", [D, 2 + NT], F32)
    h2T = it("h2T", [D, NT], F32)
    pid = nc.sync.partition_id()
    C = consts2(kb, cfg)
    rstd = kb.persist("rstd", [128, HALO_A + NT], F32)
    h0own = h0x[:, 2:2 + NT]
    pool_phase(kb, cfg, C, xT, hmid, w_pool, a_gam, a_asc, invc, rstd)
    ffn_phase(kb, cfg, hmid, h0own, gscr, *fw0, TS=TS)
    latent_phase(kb, cfg, C, h0own, w_dkv, w_dq, vec, cosT, sinT, lat_local[QL:QL + 512, :], lat_local[QL + 512:LR, :],
                 lat_local[0:QL, :], rstd, TS=TS)
    kb.phase_begin()
    z = kb.sb("zero", [128, max(2 * KC, 8)], F32)
    tz = kb.sig("dve", nc.vector.memset(z[:, :], 0.0))
    zb = kb.sb("zerob", [128, 8], BF16)
    tz = kb.sig("dve", nc.vector.memset(zb[:, :], 0.0))
    s1 = KB.DSem(kb, "x1")
    kb.wait("sp", tz)
    th = kb.dma("sp", s1, hh_local[:, :], h0x[:, NT:NT + 2])
    tzz = kb.dma("sp", s1, hh_pad[0:D, :].rearrange("(p k) c -> p (k c)", p=128), z[:, 0:2 * KC])
    for hq in range(NH):
        tzz = kb.dma("sp", s1, o_local[hq * 128:(hq + 1) * 128, 0:2], zb[:, 0:2])
    kb.wait("pool", th)
    ccs = KB.DSem(kb, "cc")
    cc_allgather(kb, ccs, lat_local[:, :], lat_all[:, :], NR)
    tcc = cc_allgather(kb, ccs, hh_local[:, :], hh_all[:, :], NR)
    kb.wait("sp", tcc)
    kb.wait("sp", tzz)
    t = kb.dma("sp", s1, hh_pad[D:(NR + 1) * D, :], hh_all[:, :])
    kb.wait("sp", t)
    t = kb.dma("sp", s1, h0x[:, 0:2], hh_pad[bass.ds(pid * D, D), :])
    kb.phase_end([t])
    cq_src = lambda c0: lat_all[(c0 // NT) * LR:(c0 // NT) * LR + QL, c0 % NT:c0 % NT + 512]
    ckv_src = lambda c0: lat_all[(c0 // NT) * LR + QL:(c0 // NT) * LR + QL + 512, c0 % NT:c0 % NT + 512]
    attn_pre_phase(kb, cfg, cq_src, ckv_src, wq, wk, wv, cosF, sinF, QnT, QrT, KnT, Vh)
    kr_src = [(r * NT, (r + 1) * NT, lat_all[r * LR + QL + 512:(r + 1) * LR, :]) for r in range(NR)]
    attn_phase(kb, cfg, QnT, QrT, KnT, Vh, kr_src, tri,
               lambda h, qb: o_local[h * 128:(h + 1) * 128, 2 + qb * 512:2 + (qb + 1) * 512])
    kb.phase_begin()
    ccs2 = KB.DSem(kb, "cc2")
    tcc = cc_allgather(kb, ccs2, o_local[:, :], o_all[:, :], NR)
    kb.phase_end([tcc])
    wo_phase(kb, cfg, lambda k0, k1, c_lo, ncol: o_all[k0 * 128:k1 * 128, bass.ds(pid * NT + c_lo, ncol)], h0x, h1T, w_o, TS=TS)
    ffn_phase(kb, cfg, h1T, h2T, gscr, *fw1, TS=TS)
    final_norm_phase(kb, cfg, C, h2T, outT, fgam, rstd)
    kb.es.close()
    return nc


import ml_dtypes
BF = ml_dtypes.bfloat16


def lay_slots(w):
    K, N = w.shape
    a = w.reshape(K // 128, 128, N // 128, 128)
    return np.ascontiguousarray(a.transpose(2, 1, 0, 3))


def lay_vec(g):
    return np.ascontiguousarray(np.asarray(g, np.float32).reshape(-1, 128).T)


def lay_ffn(w_in, w_out, conv_w, conv_b, gam, pfx):
    D, F2 = w_in.shape
    NFC = F2 // 256
    s = lay_slots(w_in)
    order = [fc + gv * NFC for fc in range(NFC) for gv in range(2)]
    cw = np.concatenate([conv_w, conv_b[None]], 0).reshape(4, 2 * NFC, 128)
    return {pfx + "w_in": np.ascontiguousarray(s[order]), pfx + "w_out": lay_slots(w_out),
            pfx + "cw": np.ascontiguousarray(cw.transpose(2, 1, 0)), pfx + "gam": lay_vec(gam)}


def rope_tables(pos):
    inv_freq = (np.float32(10000.0) ** (-(np.arange(0, 64, 2, dtype=np.float32)) / np.float32(64))).astype(np.float32)
    ang = (pos.astype(np.float32)[:, None] * inv_freq[None, :]).astype(np.float32)
    c, s = np.cos(ang).astype(np.float32), np.sin(ang).astype(np.float32)
    cosT = np.ascontiguousarray(np.concatenate([c, c], 1).T)
    sinT = np.ascontiguousarray(np.concatenate([-s, s], 1).T)
    return cosT, sinT


def halo_cols(fullT, c, NT, halo):
    F = fullT.shape[0]
    out = np.zeros((F, halo + NT), fullT.dtype)
    lo = c * NT - halo
    if lo < 0:
        out[:, -lo:] = fullT[:, 0:(c + 1) * NT]
    else:
        out[:] = fullT[:, lo:(c + 1) * NT]
    return out


def inputs_A(inp, c, NT):
    x = np.asarray(inp["x"])[0]
    D = x.shape[1]
    KC = D // 128
    CG = KC // 4
    m = {}
    m["xT"] = halo_cols(np.ascontiguousarray(x.T), c, NT, HALO_A)
    t = c * NT - HALO_A + np.arange(HALO_A + NT)
    iv = np.stack([np.where(t >= 0, 1.0 / np.minimum(np.maximum(t, 0) + 1, w), 1.0 / w) for w in POOL_WINDOWS]).astype(np.float32)
    m["invc"] = np.ascontiguousarray(np.broadcast_to(iv[:, None, :], (4, 128, HALO_A + NT)))
    m["w_pool"] = np.concatenate([lay_slots(np.asarray(inp["a_pool_w"])[0, g]) for g in range(4)], 0)
    m["a_gam"] = lay_vec(np.asarray(inp["a_norm"])[0])
    m["a_asc"] = lay_vec(np.asarray(inp["a_scale"])[0])
    m.update(lay_ffn(np.asarray(inp["ffn_w_in"])[0], np.asarray(inp["ffn_w_out"])[0], np.asarray(inp["ffn_conv_w"])[0],
                     np.asarray(inp["ffn_conv_b"])[0], np.asarray(inp["ffn_norm"])[0], "f0_"))
    wd = np.asarray(inp["w_dkv"])
    z = np.zeros((D, 64), np.float32)
    wd6 = np.concatenate([wd[:, :512], wd[:, 512:576], z, wd[:, 544:576], wd[:, 512:544], z], 1)
    m["w_dkv"] = lay_slots(wd6)
    m["w_dq"] = lay_slots(np.asarray(inp["w_dq"])[0])
    m["vec"] = np.ascontiguousarray(np.concatenate([lay_vec(inp["kv_norm"]), lay_vec(np.asarray(inp["b_norm"])[0]),
                                                    lay_vec(inp["kv_lat_norm"]), lay_vec(np.asarray(inp["q_lat_norm"])[0])], 1))
    m["cosT"], m["sinT"] = rope_tables(c * NT + np.arange(NT))
    return m


def inputs_B(inp, c, NH, cqnT, ckvnT, kropeT, S):
    m = {"cqnT": cqnT, "ckvnT": ckvnT, "kropeT": kropeT}
    wuq = np.asarray(inp["w_uq"])[0]
    QL = wuq.shape[0]
    hs = range(c * NH, (c + 1) * NH)
    cols = []
    for h in hs:
        cols += [wuq[:, h, 0:128], wuq[:, h, 128:192], wuq[:, h, 160:192], wuq[:, h, 128:160]]
    wq = np.concatenate(cols, 1)
    m["wq"] = np.ascontiguousarray(wq.reshape(QL // 128, 128, NH * 256).transpose(1, 0, 2))
    wukv = np.asarray(inp["w_ukv"])
    wk = np.concatenate([wukv[:, h, 0:128] for h in hs], 1)
    wv = np.concatenate([wukv[:, h, 128:256] for h in hs], 1)
    m["wk"] = np.ascontiguousarray(wk.reshape(4, 128, NH * 128).transpose(1, 0, 2))
    m["wv"] = np.ascontiguousarray(wv.reshape(4, 128, NH * 128).transpose(1, 0, 2))
    m["cosT"], m["sinT"] = rope_tables(np.arange(S))
    m["tri"] = (np.arange(128)[None, :] >= np.arange(128)[:, None]).astype(np.float32).astype(BF)
    return m


def inputs_C(inp, c, NT, oT_full, h0T_full):
    m = {"oT": halo_cols(oT_full, c, NT, 2), "h0T": halo_cols(h0T_full, c, NT, 2)}
    m["w_o"] = lay_slots(np.asarray(inp["w_o"])[0])
    m.update(lay_ffn(np.asarray(inp["ffn_w_in"])[1], np.asarray(inp["ffn_w_out"])[1], np.asarray(inp["ffn_conv_w"])[1],
                     np.asarray(inp["ffn_conv_b"])[1], np.asarray(inp["ffn_norm"])[1], "f1_"))
    m["fgam"] = lay_vec(inp["final_norm"])
    return m


from concourse.bass_utils import run_bass_kernel_spmd

CFG = dict(D=4096, DFF=11008, NT=2048, QL=1024, S=16384, NH=4)
NCORES = 8


def shared_A(inp):
    m = inputs_A(inp, 0, CFG["NT"])
    for k in ("xT", "invc", "cosT", "sinT"):
        m.pop(k)
    return m


def percore_A(inp, c, NT):
    x = np.asarray(inp["x"])[0]
    m = {}
    lo = max(0, c * NT - HALO_A)
    slab = np.ascontiguousarray(x[lo:(c + 1) * NT].T)
    xT = np.zeros((x.shape[1], HALO_A + NT), np.float32)
    xT[:, HALO_A + NT - slab.shape[1]:] = slab
    m["xT"] = xT
    t = c * NT - HALO_A + np.arange(HALO_A + NT)
    iv = np.stack([np.where(t >= 0, 1.0 / np.minimum(np.maximum(t, 0) + 1, w), 1.0 / w) for w in POOL_WINDOWS]).astype(np.float32)
    m["invc"] = np.ascontiguousarray(np.broadcast_to(iv[:, None, :], (4, 128, HALO_A + NT)))
    m["cosT"], m["sinT"] = rope_tables(c * NT + np.arange(NT))
    return m


def kernel(**inputs):
    cfg = CFG
    NT, S, NH = cfg["NT"], cfg["S"], cfg["NH"]
    cores = list(range(NCORES))
    shA = shared_A(inputs)
    mapsA = [dict(shA, **percore_A(inputs, c, NT)) for c in cores]
    resA = run_bass_kernel_spmd(build_A(cfg), mapsA, core_ids=cores).results
    del mapsA, shA
    h0T_full = np.concatenate([resA[c]["h0T"] for c in cores], 1)
    cqnT = np.ascontiguousarray(np.concatenate([resA[c]["cqn"] for c in cores], 1))
    ckvnT = np.ascontiguousarray(np.concatenate([resA[c]["ckvn"] for c in cores], 1))
    kropeT = np.ascontiguousarray(np.concatenate([resA[c]["krope"] for c in cores], 1))
    del resA
    mapsB = [inputs_B(inputs, c, NH, cqnT, ckvnT, kropeT, S) for c in cores]
    resB = run_bass_kernel_spmd(build_B(cfg), mapsB, core_ids=cores).results
    oT_full = np.concatenate([resB[c]["oT"] for c in cores], 0)
    del mapsB, resB
    shC = inputs_C(inputs, 0, NT, oT_full, h0T_full)
    mapsC = []
    for c in cores:
        m = dict(shC)
        m["oT"] = halo_cols(oT_full, c, NT, 2)
        m["h0T"] = halo_cols(h0T_full, c, NT, 2)
        mapsC.append(m)
    resC = run_bass_kernel_spmd(build_C(cfg), mapsC, core_ids=cores).results
    outT = np.concatenate([resC[c]["outT"] for c in cores], 1)
    return np.ascontiguousarray(outT.T)[None].astype(np.float32)
```

```python
from contextlib import ExitStack
import numpy as np
import concourse.bass as bass
import concourse.mybir as mybir

F32 = mybir.dt.float32
BF16 = mybir.dt.bfloat16
AF = mybir.ActivationFunctionType
ALU = mybir.AluOpType
EPS = 1e-6


class KB:
    def __init__(self):
        self.nc = bass.Bass("TRN2", target_bir_lowering=False)
        self.es = ExitStack()
        nc = self.nc
        self.engs = {"pe": nc.tensor, "act": nc.scalar, "dve": nc.vector, "pool": nc.gpsimd, "sp": nc.sync}
        self.sem = {}
        self.cnt = {}
        for e in self.engs:
            self.sem[e] = self.es.enter_context(nc.semaphore("s_" + e))
            self.cnt[e] = 0
        self.waited = {}
        self.nsem = 0
        self.banks = [nc.alloc_psum_tensor(f"bank{i}", [128, 512], F32) for i in range(8)]
        self.phase_es = None
        self.uid = 0
        self.sem_pool = []
        self.phase_sems = []

    def sig(self, e, ins):
        ins.then_inc(self.sem[e], 1)
        self.cnt[e] += 1
        return ("e", e, self.cnt[e])

    def wait(self, consumer, t):
        if t is None:
            return
        if isinstance(t, (list, tuple)) and t and isinstance(t[0], (list, tuple)):
            for x in t:
                self.wait(consumer, x)
            return
        kind, key, val = t
        k = (consumer, kind, key if kind == "e" else key.uid)
        if self.waited.get(k, 0) >= val:
            return
        s = self.sem[key] if kind == "e" else key.sem
        self.engs[consumer].wait_ge(s, val)
        self.waited[k] = val

    def newsem(self, name):
        self.nsem += 1
        return self.es.enter_context(self.nc.semaphore(f"{name}_{self.nsem}"))

    class DSem:
        def __init__(self, kb, name):
            if kb.sem_pool:
                self.sem, self.uid, self.cnt = kb.sem_pool.pop()
            else:
                self.sem = kb.newsem(name)
                self.uid = kb.nsem
                self.cnt = 0
            kb.phase_sems.append(self)

    def dma(self, q, dsem, out, in_):
        ins = self.engs[q].dma_start(out=out, in_=in_)
        ins.then_inc(dsem.sem, 16)
        dsem.cnt += 16
        return ("d", dsem, dsem.cnt)

    def phase_begin(self):
        self.phase_es = ExitStack()

    def sb(self, name, shape, dt):
        self.uid += 1
        return self.phase_es.enter_context(self.nc.sbuf_tensor(f"{name}_{self.uid}", list(shape), dt))

    def persist(self, name, shape, dt):
        self.uid += 1
        return self.es.enter_context(self.nc.sbuf_tensor(f"{name}_{self.uid}", list(shape), dt))

    def phase_end(self, final_tickets):
        for t in final_tickets:
            self.wait("sp", t)
        ins = self.nc.sync.sem_inc(self.sem["sp"], 1)
        self.cnt["sp"] += 1
        t = ("e", "sp", self.cnt["sp"])
        for e in ("pe", "act", "dve", "pool"):
            self.wait(e, t)
        self.phase_es.close()
        self.phase_es = None
        for d in self.phase_sems:
            self.sem_pool.append((d.sem, d.uid, d.cnt))
        self.phase_sems = []
        return t


class Ring:
    def __init__(self, bufs):
        self.bufs = bufs
        self.free = [None] * len(bufs)
        self.i = -1

    def next(self):
        self.i = (self.i + 1) % len(self.bufs)
        return self.i


class WStream:
    def __init__(self, kb, loads, ns=4, pf=3, q="pool"):
        self.kb = kb
        self.loads = loads
        self.ns, self.pf, self.q = ns, pf, q
        self.slots = [kb.sb("wslot", [128, 32, 128], BF16) for _ in range(ns)]
        self.dsem = [KB.DSem(kb, "wld") for _ in range(ns)]
        self.free = [None] * ns
        self.ticket = {}
        self.issued = 0
        for _ in range(min(pf, len(loads))):
            self._issue()

    def _issue(self):
        l = self.issued
        if l >= len(self.loads):
            return
        s = l % self.ns
        ap, nk = self.loads[l]
        self.kb.wait(self.q, self.free[s])
        self.ticket[l] = self.kb.dma(self.q, self.dsem[s], self.slots[s][:, 0:nk, :], ap)
        self.issued += 1

    def get(self, l):
        return self.slots[l % self.ns], self.ticket[l]

    def release(self, l, t):
        self.free[l % self.ns] = t
        self._issue()


class BankRing:
    def __init__(self, kb, idxs):
        self.kb = kb
        self.idxs = idxs
        self.free = {i: None for i in idxs}
        self.p = -1

    def next(self):
        self.p = (self.p + 1) % len(self.idxs)
        b = self.idxs[self.p]
        return b, self.kb.banks[b], self.free[b]

    def release(self, b, t):
        self.free[b] = t


def col_tiles(ncol, halo):
    tiles = []
    if halo:
        tiles.append((0, halo))
    c = halo
    while c < ncol:
        n = min(512, ncol - c)
        tiles.append((c, n))
        c += n
    return tiles


def rms_stats(kb, src_fn, nchunk, ncol, rstd, ones_f32, epsT, start_t=None):
    nc = kb.nc
    tiles = col_tiles(ncol, 0)
    assert len(tiles) <= 6
    xb = [kb.sb("rs_x", [128, ncol], F32) for _ in range(3)]
    xs = [KB.DSem(kb, "rs_xs") for _ in range(3)]
    xfree = [None] * 3
    sq = [kb.sb("rs_sq", [128, ncol], F32) for _ in range(2)]
    sqfree = [None] * 2
    lt = {}

    def load(k):
        i = k % 3
        kb.wait("sp", xfree[i])
        lt[k] = kb.dma("sp", xs[i], xb[i][:, :], src_fn(k))
    for k in range(min(2, nchunk)):
        load(k)
    last_pe = None
    for k in range(nchunk):
        if k + 2 < nchunk:
            load(k + 2)
        i, j = k % 3, k % 2
        kb.wait("act", lt[k])
        kb.wait("act", sqfree[j])
        if k == 0:
            kb.wait("act", start_t)
        ta = kb.sig("act", nc.scalar.activation(sq[j][:, :], xb[i][:, :], AF.Square))
        xfree[i] = ta
        kb.wait("pe", ta)
        if k == 0:
            kb.wait("pe", start_t)
        for ti, (c0, n) in enumerate(tiles):
            ins = nc.tensor.matmul(kb.banks[ti][:, 0:n], ones_f32[:, :], sq[j][:, c0:c0 + n],
                                   start=(k == 0), stop=(k == nchunk - 1))
        last_pe = kb.sig("pe", ins)
        sqfree[j] = last_pe
    kb.wait("act", last_pe)
    for ti, (c0, n) in enumerate(tiles):
        ins = nc.scalar.activation(rstd[:, c0:c0 + n], kb.banks[ti][:, 0:n], AF.Sqrt, bias=epsT[:, 0:1], scale=1.0)
    t1 = kb.sig("act", ins)
    kb.wait("dve", t1)
    t2 = kb.sig("dve", nc.vector.reciprocal(rstd[:, :], rstd[:, :]))
    return t2


def norm_apply(kb, src_fn, nchunk, ncol, gamma, rstd, rstd_t, dst_fn, dst_free=None, after=None):
    nc = kb.nc
    xb = [kb.sb("na_x", [128, ncol], F32) for _ in range(3)]
    xs = [KB.DSem(kb, "na_xs") for _ in range(3)]
    xfree = [None] * 3
    lt = {}

    def load(k):
        i = k % 3
        kb.wait("sp", xfree[i])
        lt[k] = kb.dma("sp", xs[i], xb[i][:, :], src_fn(k))
    for k in range(min(2, nchunk)):
        load(k)
    out_t = []
    kb.wait("dve", rstd_t)
    for k in range(nchunk):
        if k + 2 < nchunk:
            load(k + 2)
        i = k % 3
        kb.wait("dve", lt[k])
        if dst_free is not None:
            kb.wait("dve", dst_free(k))
        t = kb.sig("dve", nc.vector.scalar_tensor_tensor(dst_fn(k), xb[i][:, :], gamma[:, k:k + 1], rstd[:, :],
                                                          op0=ALU.mult, op1=ALU.mult))
        xfree[i] = t
        out_t.append(t)
        if after is not None:
            after(k, t)
    return out_t


def proj(kb, ws, x_fn, x_ready, douts, tiles, brng, evac, chunk_done=None, ms=None):
    nc = kb.nc
    first = True
    for di, parts in enumerate(douts):
        nk_tot = sum(p[1] for p in parts)
        m = 128 if ms is None else ms[di]
        last_t = None
        for ti, (c0, n) in enumerate(tiles):
            b, bank, bfree = brng.next()
            kb.wait("pe", bfree)
            if first:
                kb.wait("pe", x_ready)
                first = False
            kk = 0
            for (l, nk, k0) in parts:
                slot, lt = ws.get(l)
                kb.wait("pe", lt)
                for kc in range(nk):
                    ins = nc.tensor.matmul(bank[0:m, 0:n], slot[:, kc, 0:m], x_fn(k0 + kc, c0, n),
                                           start=(kk == 0), stop=(kk == nk_tot - 1))
                    kk += 1
            last_t = kb.sig("pe", ins)
            ft = evac(di, ti, c0, n, bank, last_t)
            brng.release(b, ft)
        for (l, nk, k0) in parts:
            ws.release(l, last_t)
        if chunk_done is not None:
            chunk_done(di)


def ffn_phase(kb, cfg, hin, hout, gscr, w_in_d, w_out_d, cw_d, gam_d, TS=1024):
    nc = kb.nc
    D, DFF, NT = cfg["D"], cfg["DFF"], cfg["NT"]
    KC, NFC = D // 128, DFF // 128
    final = []
    ucarry = kb.persist("ucarry", [128, 2 * NFC, 2], F32)
    for st in range(NT // TS):
        kb.phase_begin()
        ncol = 2 + TS
        cbase = st * TS
        ones = kb.sb("ones", [128, 128], F32)
        gam = kb.sb("gam", [128, KC], F32)
        cw = kb.sb("cw", [128, 2 * NFC, 4], F32)
        rstd = kb.sb("rstd", [128, ncol], F32)
        xnT = kb.sb("xnT", [128, KC, ncol], BF16)
        cs = KB.DSem(kb, "const")
        epsT = kb.sb("epsT", [128, 1], F32)
        nc.vector.memset(epsT[:, :], EPS)
        t0 = kb.sig("dve", nc.vector.memset(ones[:, :], 1.0 / D))
        kb.dma("sp", cs, gam[:, :], gam_d[:, :])
        tc = kb.dma("sp", cs, cw[:, :, :], cw_d[:, :, :])
        src = lambda k: hin[k * 128:(k + 1) * 128, cbase:cbase + ncol]
        rt = rms_stats(kb, src, KC, ncol, rstd, ones, epsT, start_t=t0)
        kb.wait("dve", tc)
        xt = norm_apply(kb, src, KC, ncol, gam, rstd, rt, lambda k: xnT[:, k, :])
        loads = [(w_in_d[i], KC) for i in range(2 * NFC)]
        ws = WStream(kb, loads)
        tiles = col_tiles(ncol, 2)
        if st > 0:
            tiles = tiles[1:]
        brng = BankRing(kb, list(range(8)))
        ub = [kb.sb("ubuf", [128, ncol], F32) for _ in range(2)]
        ubfree = [None, None]
        ab = {0: [kb.sb("ag", [128, TS], F32) for _ in range(2)], 1: [kb.sb("av", [128, TS], F32) for _ in range(2)]}
        abfree = {0: [None, None], 1: [None, None]}
        go = [kb.sb("gout", [128, TS], BF16) for _ in range(2)]
        gos = [KB.DSem(kb, "gst") for _ in range(2)]
        gofree = [None, None]
        state = {"evt": [], "conv": {}}

        def evac(di, ti, c0, n, bank, pt):
            u = di % 2
            kb.wait("act", pt)
            if ti == 0:
                kb.wait("act", ubfree[u])
                if st > 0:
                    nc.scalar.copy(ub[u][:, 0:2], ucarry[:, di, :])
            t = kb.sig("act", nc.scalar.copy(ub[u][:, c0:c0 + n], bank[:, 0:n]))
            if c0 + n == ncol and st + 1 < NT // TS:
                kb.wait("act", t)
                t2 = kb.sig("act", nc.scalar.copy(ucarry[:, di, :], ub[u][:, ncol - 2:ncol]))
                state["evt"] = t2
                return t
            state["evt"] = t
            return t

        def chunk_done(di):
            fc, gv = di // 2, di % 2
            u = di % 2
            r = fc % 2
            ci = fc if gv == 0 else NFC + fc
            a = ab[gv][r]
            kb.wait("dve", state["evt"])
            kb.wait("dve", abfree[gv][r])
            t = kb.sig("dve", nc.vector.tensor_scalar(a[:, :], ub[u][:, 2:2 + TS], cw[:, ci, 2:3], cw[:, ci, 3:4],
                                                      op0=ALU.mult, op1=ALU.add))
            kb.wait("dve", t)
            t = kb.sig("dve", nc.vector.scalar_tensor_tensor(a[:, :], ub[u][:, 1:1 + TS], cw[:, ci, 1:2], a[:, :],
                                                              op0=ALU.mult, op1=ALU.add))
            kb.wait("dve", t)
            t = kb.sig("dve", nc.vector.scalar_tensor_tensor(a[:, :], ub[u][:, 0:TS], cw[:, ci, 0:1], a[:, :],
                                                              op0=ALU.mult, op1=ALU.add))
            ubfree[u] = t
            if gv == 0:
                kb.wait("act", t)
                state["conv"][0] = kb.sig("act", nc.scalar.activation(a[:, :], a[:, :], AF.Silu))
            else:
                state["conv"][1] = t
                kb.wait("dve", state["conv"][0])
                kb.wait("dve", state["conv"][1])
                kb.wait("dve", gofree[r])
                tm = kb.sig("dve", nc.vector.tensor_tensor(go[r][:, :], ab[0][r][:, :], a[:, :], op=ALU.mult))
                abfree[0][r] = tm
                abfree[1][r] = tm
                kb.wait("sp", tm)
                for j in range(TS // 512):
                    tj = kb.dma("sp", gos[r], gscr[(cbase // 512) + j, :, fc, :], go[r][:, j * 512:(j + 1) * 512])
                gofree[r] = tj
                state["last_store"] = tj

        douts = [[(i, KC, 0)] for i in range(2 * NFC)]
        proj(kb, ws, lambda kc, c0, n: xnT[:, kc, c0:c0 + n], xt[-1], douts, tiles, brng, evac, chunk_done)
        kb.phase_end([gofree[0], gofree[1]])

        NTT = TS // 512
        kb.uid += 1
        ysc = nc.dram_tensor(f"ysc_{kb.uid}", [D, TS], F32).ap()
        halves = [(0, (NFC + 1) // 2), ((NFC + 1) // 2, NFC)] if NFC >= 2 else [(0, NFC)]
        for hi, (ka, kz) in enumerate(halves):
            lasth = (hi == len(halves) - 1)
            nkh = kz - ka
            kb.phase_begin()
            gT = kb.sb("gT", [128, nkh, TS], BF16)
            gs = KB.DSem(kb, "gld")
            for tt in range(NTT):
                tg = kb.dma("sp", gs, gT[:, :, tt * 512:(tt + 1) * 512], gscr[cbase // 512 + tt, :, ka:kz, :])
            parts_k = []
            k0 = 0
            while k0 < nkh:
                nk = min(32, nkh - k0)
                parts_k.append((k0, nk))
                k0 += nk
            loads = []
            douts = []
            for dc in range(KC):
                ps = []
                for (k0, nk) in parts_k:
                    ps.append((len(loads), nk, k0))
                    loads.append((w_out_d[dc, :, ka + k0:ka + k0 + nk, :], nk))
                douts.append(ps)
            ws = WStream(kb, loads)
            brng = BankRing(kb, list(range(8)))
            rb = [kb.sb("rbuf", [128, 512], F32) for _ in range(3)]
            rs = [KB.DSem(kb, "rld") for _ in range(3)]
            rfree = [None] * 3
            pbuf = [kb.sb("pbuf", [128, 512], F32) for _ in range(3)]
            pss = [KB.DSem(kb, "pld") for _ in range(3)]
            ob = [kb.sb("obuf", [128, 512], F32) for _ in range(2)]
            osm = [KB.DSem(kb, "ost") for _ in range(2)]
            ofree = [None, None]
            rt_ = {}
            units = [(dc, tt) for dc in range(KC) for tt in range(NTT)]

            def rload(ui):
                dc, tt = units[ui]
                i = ui % 3
                kb.wait("sp", rfree[i])
                cs0 = cbase + tt * 512
                if hi > 0:
                    kb.dma("sp", pss[i], pbuf[i][:, :], ysc[dc * 128:(dc + 1) * 128, tt * 512:(tt + 1) * 512])
                    rt_[ui] = [("d", pss[i], pss[i].cnt)]
                else:
                    rt_[ui] = []
                if lasth:
                    rt_[ui].append(kb.dma("sp", rs[i], rb[i][:, :], hin[dc * 128:(dc + 1) * 128, 2 + cs0:2 + cs0 + 512]))
            if lasth or hi > 0:
                rload(0)
                if len(units) > 1:
                    rload(1)

            def evac2(di, ti, c0, n, bank, pt):
                ui = di * NTT + ti
                i, o = ui % 3, ui % 2
                cs0 = cbase + ti * 512
                kb.wait("dve", pt)
                kb.wait("dve", ofree[o])
                if hi == 0 and not lasth:
                    t = kb.sig("dve", nc.vector.tensor_copy(ob[o][:, :], bank[:, 0:512]))
                    dst = ysc[di * 128:(di + 1) * 128, ti * 512:(ti + 1) * 512]
                else:
                    for tk in rt_[ui]:
                        kb.wait("dve", tk)
                    if hi > 0:
                        t = kb.sig("dve", nc.vector.tensor_tensor(ob[o][:, :], bank[:, 0:512], pbuf[i][:, :], op=ALU.add))
                        kb.wait("dve", t)
                        t = kb.sig("dve", nc.vector.tensor_tensor(ob[o][:, :], ob[o][:, :], rb[i][:, :], op=ALU.add))
                    else:
                        t = kb.sig("dve", nc.vector.tensor_tensor(ob[o][:, :], bank[:, 0:512], rb[i][:, :], op=ALU.add))
                    rfree[i] = t
                    if ui + 2 < len(units):
                        rload(ui + 2)
                    dst = hout[di * 128:(di + 1) * 128, cs0:cs0 + 512]
                ft = t if (hi == 0 and not lasth) else ("e", "dve", kb.cnt["dve"])
                kb.wait("sp", t)
                ofree[o] = kb.dma("sp", osm[o], dst, ob[o][:, :])
                return ft

            proj(kb, ws, lambda kc, c0, n: gT[:, kc, c0:c0 + n], tg, douts, [(tt * 512, 512) for tt in range(NTT)], brng, evac2)
            final = [ofree[0], ofree[1]]
            kb.phase_end(final)
    return final


POOL_WINDOWS = (2, 4, 8, 16)
HALO_A = 17


def consts(kb, D):
    nc = kb.nc
    c = {}
    c["epsT"] = kb.persist("epsT", [128, 1], F32)
    nc.vector.memset(c["epsT"][:, :], EPS)
    t = None
    for nm, val in (("onesD", 1.0 / D), ("ones512", 1.0 / 512), ("ones1024", 1.0 / 1024)):
        c[nm] = kb.persist(nm, [128, 128], F32)
        t = kb.sig("dve", nc.vector.memset(c[nm][:, :], val))
    c["t"] = t
    return c


def stats_phase(kb, src_fn, nchunk, ncol, rstd, ones, epsT, t0):
    kb.phase_begin()
    rt = rms_stats(kb, src_fn, nchunk, ncol, rstd, ones, epsT, start_t=t0)
    kb.phase_end([rt])


def pool_phase(kb, cfg, C, xT, hmid, w_pool_d, gam_d, asc_d, invc_d, rstd):
    nc = kb.nc
    D, NT = cfg["D"], cfg["NT"]
    KC = D // 128
    G = 4
    CG = KC // G
    ncol = HALO_A + NT
    nmid = 2 + NT
    OFF = HALO_A - 2
    stats_phase(kb, lambda k: xT[k * 128:(k + 1) * 128, :], KC, ncol, rstd, C["onesD"], C["epsT"], C["t"])
    kb.phase_begin()
    gam = kb.sb("gam", [128, KC], F32)
    asc = kb.sb("asc", [128, KC], F32)
    cs = KB.DSem(kb, "const")
    kb.dma("sp", cs, gam[:, :], gam_d[:, :])
    tc = kb.dma("sp", cs, asc[:, :], asc_d[:, :])
    pooled = kb.sb("pooled", [128, CG, nmid], BF16)
    pooled_free = None
    xl = [kb.sb("xl", [128, ncol], F32) for _ in range(2)]
    xls = [KB.DSem(kb, "xls") for _ in range(2)]
    xlfree = [None, None]
    xnb = kb.sb("xnb", [128, ncol], F32)
    tmp = [kb.sb("ptmp", [128, ncol], F32) for _ in range(2)]
    invc = kb.sb("invc", [128, ncol], F32)
    ivs = KB.DSem(kb, "ivs")
    loads = [(w_pool_d[i], CG) for i in range(G * CG)]
    ws = WStream(kb, loads)
    brng = BankRing(kb, list(range(8)))
    tiles = col_tiles(nmid, 2)
    rb = [kb.sb("presid", [128, nmid], F32) for _ in range(2)]
    rbs = [KB.DSem(kb, "prs") for _ in range(2)]
    rbfree = [None, None]
    ob = [kb.sb("pobuf", [128, nmid], F32) for _ in range(2)]
    obs = [KB.DSem(kb, "pos") for _ in range(2)]
    obfree = [None, None]
    kb.wait("dve", tc)
    last_dve = None
    nload = 0
    for g in range(G):
        kb.wait("sp", last_dve)
        ti = kb.dma("sp", ivs, invc[:, :], invc_d[g, :, :])
        for c in range(CG):
            k = g * CG + c
            i = nload % 2
            nload += 1
            kb.wait("sp", xlfree[i])
            tl = kb.dma("sp", xls[i], xl[i][:, :], xT[k * 128:(k + 1) * 128, :])
            kb.wait("dve", tl)
            kb.wait("dve", last_dve)
            t = kb.sig("dve", nc.vector.scalar_tensor_tensor(xnb[:, :], xl[i][:, :], gam[:, k:k + 1], rstd[:, 0:ncol],
                                                              op0=ALU.mult, op1=ALU.mult))
            xlfree[i] = t
            cur = xnb
            sh = 1
            for step in range(g + 1):
                dst = tmp[step % 2]
                kb.wait("dve", t)
                t = kb.sig("dve", nc.vector.tensor_tensor(dst[:, sh:ncol], cur[:, sh:ncol], cur[:, 0:ncol - sh], op=ALU.add))
                cur = dst
                sh *= 2
            other = tmp[(g + 1) % 2]
            kb.wait("dve", t)
            kb.wait("dve", ti)
            t = kb.sig("dve", nc.vector.tensor_tensor(other[:, OFF:ncol], cur[:, OFF:ncol], invc[:, OFF:ncol], op=ALU.mult))
            kb.wait("dve", t)
            if c == 0:
                kb.wait("dve", pooled_free)
            t = kb.sig("dve", nc.vector.tensor_tensor(pooled[:, c, :], other[:, OFF:ncol], xnb[:, OFF:ncol], op=ALU.subtract))
            last_dve = t
        st = {}

        def evac(di, ti_, c0, n, bank, pt, g=g):
            ko = g * CG + di
            r = ko % 2
            if ti_ == 0:
                kb.wait("sp", rbfree[r])
                st["rl"] = kb.dma("sp", rbs[r], rb[r][:, :], xT[ko * 128:(ko + 1) * 128, OFF:ncol])
                kb.wait("dve", obfree[r])
            kb.wait("dve", pt)
            kb.wait("dve", st["rl"])
            t = kb.sig("dve", nc.vector.scalar_tensor_tensor(ob[r][:, c0:c0 + n], bank[:, 0:n], asc[:, ko:ko + 1],
                                                              rb[r][:, c0:c0 + n], op0=ALU.mult, op1=ALU.add))
            st["t"] = t
            return t

        def chunk_done(di, g=g):
            ko = g * CG + di
            r = ko % 2
            rbfree[r] = st["t"]
            kb.wait("sp", st["t"])
            obfree[r] = kb.dma("sp", obs[r], hmid[ko * 128:(ko + 1) * 128, :], ob[r][:, :])

        douts = [[(g * CG + dc, CG, 0)] for dc in range(CG)]
        proj(kb, ws, lambda kc, c0, n: pooled[:, kc, c0:c0 + n], last_dve, douts, tiles, brng, evac, chunk_done)
        pooled_free = ("e", "pe", kb.cnt["pe"])
        last_dve = ("e", "dve", kb.cnt["dve"])
    kb.phase_end([obfree[0], obfree[1]])


def rms_sb(kb, buf, nch, ncol, ones, epsT, rstd, sq, buf_ready, banks=(0, 1)):
    nc = kb.nc
    tiles = col_tiles(ncol, 0)
    sqfree = [None, None]
    last = None
    for ch in range(nch):
        j = ch % 2
        kb.wait("act", buf_ready)
        kb.wait("act", sqfree[j])
        ta = kb.sig("act", nc.scalar.activation(sq[j][:, 0:ncol], buf[:, ch, :], AF.Square))
        kb.wait("pe", ta)
        for ti, (c0, n) in enumerate(tiles):
            ins = nc.tensor.matmul(kb.banks[banks[ti]][:, 0:n], ones[:, :], sq[j][:, c0:c0 + n],
                                   start=(ch == 0), stop=(ch == nch - 1))
        last = kb.sig("pe", ins)
        sqfree[j] = last
    kb.wait("act", last)
    for ti, (c0, n) in enumerate(tiles):
        ins = nc.scalar.activation(rstd[:, c0:c0 + n], kb.banks[banks[ti]][:, 0:n], AF.Sqrt, bias=epsT[:, 0:1], scale=1.0)
    t1 = kb.sig("act", ins)
    kb.wait("dve", t1)
    return kb.sig("dve", nc.vector.reciprocal(rstd[:, 0:ncol], rstd[:, 0:ncol]))


def latent_phase(kb, cfg, C, h0T, w_dkv_d, w_dq_d, vec_d, cosT_d, sinT_d, ckvn_o, krope_o, cqn_o, rstd, TS=1024):
    nc = kb.nc
    D, NT, QL = cfg["D"], cfg["NT"], cfg["QL"]
    KC, NQ = D // 128, QL // 128
    stats_phase(kb, lambda k: h0T[k * 128:(k + 1) * 128, :], KC, NT, rstd, C["onesD"], C["epsT"], C["t"])
    onesq = C["ones1024"] if QL == 1024 else C["onesQ"]
    for st in range(NT // TS):
        cb = st * TS
        kb.phase_begin()
        vec = kb.sb("vec", [128, 2 * KC + 4 + NQ], F32)
        cs = KB.DSem(kb, "const")
        tv = kb.dma("sp", cs, vec[:, :], vec_d[:, :])
        cosT = kb.sb("cosT", [64, TS], F32)
        sinT = kb.sb("sinT", [64, TS], F32)
        kb.dma("sp", cs, cosT[:, :], cosT_d[:, cb:cb + TS])
        tcs = kb.dma("sp", cs, sinT[:, :], sinT_d[:, cb:cb + TS])
        xnT = kb.sb("xnT", [128, KC, TS], BF16)
        ckv = kb.sb("ckv", [128, 6, TS], F32)
        sq = [kb.sb("lsq", [128, TS], F32) for _ in range(2)]
        rs2 = kb.sb("rs2", [128, TS], F32)
        ob = [kb.sb("lob", [128, TS], BF16) for _ in range(2)]
        obs = [KB.DSem(kb, "los") for _ in range(2)]
        obfree = [None, None]
        src = lambda k: h0T[k * 128:(k + 1) * 128, cb:cb + TS]
        kb.wait("dve", tv)
        ws = WStream(kb, [(w_dkv_d[i], KC) for i in range(6)])
        brng = BankRing(kb, [2, 3, 4, 5, 6, 7])
        tiles = col_tiles(TS, 0)
        xt = norm_apply(kb, src, KC, TS, vec[:, 0:KC], rstd[:, cb:cb + TS], None, lambda k: xnT[:, k, :])
        stt = {}

        def evac_kv(di, ti, c0, n, bank, pt):
            m = 128 if di < 4 else 64
            kb.wait("act", pt)
            t = kb.sig("act", nc.scalar.copy(ckv[0:m, di, c0:c0 + n], bank[0:m, 0:n]))
            stt["t"] = t
            return t
        proj(kb, ws, lambda kc, c0, n: xnT[:, kc, c0:c0 + n], xt[-1], [[(i, KC, 0)] for i in range(6)], tiles, brng,
             evac_kv, ms=[128] * 4 + [64, 64])
        pe_kv_done = ("e", "pe", kb.cnt["pe"])
        tr = rms_sb(kb, ckv, 4, TS, C["ones512"], C["epsT"], rs2, sq, stt["t"])
        nob = 0
        for ch in range(4):
            r = nob % 2
            nob += 1
            kb.wait("dve", tr)
            kb.wait("dve", obfree[r])
            t = kb.sig("dve", nc.vector.scalar_tensor_tensor(ob[r][:, :], ckv[:, ch, :], vec[:, 2 * KC + ch:2 * KC + ch + 1],
                                                              rs2[:, :], op0=ALU.mult, op1=ALU.mult))
            kb.wait("sp", t)
            obfree[r] = kb.dma("sp", obs[r], ckvn_o[ch * 128:(ch + 1) * 128, cb:cb + TS], ob[r][:, :])
        kb.wait("dve", tcs)
        kb.wait("dve", stt["t"])
        t = kb.sig("dve", nc.vector.tensor_tensor(ckv[0:64, 4, :], ckv[0:64, 4, :], cosT[:, :], op=ALU.mult))
        t = kb.sig("dve", nc.vector.tensor_tensor(ckv[0:64, 5, :], ckv[0:64, 5, :], sinT[:, :], op=ALU.mult))
        kb.wait("dve", t)
        r = nob % 2
        nob += 1
        kb.wait("dve", obfree[r])
        t = kb.sig("dve", nc.vector.tensor_tensor(ob[r][0:64, :], ckv[0:64, 4, :], ckv[0:64, 5, :], op=ALU.add))
        kb.wait("sp", t)
        obfree[r] = kb.dma("sp", obs[r], krope_o[:, cb:cb + TS], ob[r][0:64, :])
        kb.phase_end([obfree[0], obfree[1]])
        kb.phase_begin()
        vec = kb.sb("vec", [128, 2 * KC + 4 + NQ], F32)
        cs = KB.DSem(kb, "const")
        tv = kb.dma("sp", cs, vec[:, :], vec_d[:, :])
        xnT = kb.sb("xnT", [128, KC, TS], BF16)
        cq = kb.sb("cq", [128, NQ, TS], F32)
        sq = [kb.sb("lsq", [128, TS], F32) for _ in range(2)]
        rs2 = kb.sb("rs2", [128, TS], F32)
        ob = [kb.sb("lob", [128, TS], BF16) for _ in range(2)]
        obs = [KB.DSem(kb, "los") for _ in range(2)]
        obfree = [None, None]
        kb.wait("dve", tv)
        ws = WStream(kb, [(w_dq_d[i], KC) for i in range(NQ)])
        brng = BankRing(kb, [2, 3, 4, 5, 6, 7])
        xt = norm_apply(kb, src, KC, TS, vec[:, KC:2 * KC], rstd[:, cb:cb + TS], None, lambda k: xnT[:, k, :])

        def evac_q(di, ti, c0, n, bank, pt):
            kb.wait("act", pt)
            t = kb.sig("act", nc.scalar.copy(cq[:, di, c0:c0 + n], bank[:, 0:n]))
            stt["t"] = t
            return t
        proj(kb, ws, lambda kc, c0, n: xnT[:, kc, c0:c0 + n], xt[-1], [[(i, KC, 0)] for i in range(NQ)], tiles, brng, evac_q)
        tr = rms_sb(kb, cq, NQ, TS, onesq, C["epsT"], rs2, sq, stt["t"])
        for ch in range(NQ):
            r = nob % 2
            nob += 1
            kb.wait("dve", tr)
            kb.wait("dve", obfree[r])
            t = kb.sig("dve", nc.vector.scalar_tensor_tensor(ob[r][:, :], cq[:, ch, :], vec[:, 2 * KC + 4 + ch:2 * KC + 5 + ch],
                                                              rs2[:, :], op0=ALU.mult, op1=ALU.mult))
            kb.wait("sp", t)
            obfree[r] = kb.dma("sp", obs[r], cqn_o[ch * 128:(ch + 1) * 128, cb:cb + TS], ob[r][:, :])
        kb.phase_end([obfree[0], obfree[1]])


def wo_phase(kb, cfg, o_src, h0T, h1T, w_o_d, TS=1024):
    nc = kb.nc
    D, NT = cfg["D"], cfg["NT"]
    KC = D // 128
    for st in range(NT // TS):
        c_lo = 0 if st == 0 else 2 + st * TS
        ncol = TS + (2 if st == 0 else 0)
        kb.phase_begin()
        oS = kb.sb("oS", [128, KC, ncol], BF16)
        os_ = KB.DSem(kb, "old")
        for k0 in range(0, KC, 8):
            k1 = min(KC, k0 + 8)
            to = kb.dma("sp", os_, oS[:, k0:k1, :], o_src(k0, k1, c_lo, ncol).rearrange("(k p) c -> p k c", p=128))
        ws = WStream(kb, [(w_o_d[i], KC) for i in range(KC)])
        brng = BankRing(kb, list(range(8)))
        tiles = col_tiles(ncol, 2 if st == 0 else 0)
        rb = [kb.sb("wresid", [128, ncol], F32) for _ in range(2)]
        rbs = [KB.DSem(kb, "wrs") for _ in range(2)]
        rbfree = [None, None]
        ob = [kb.sb("wobuf", [128, ncol], F32) for _ in range(2)]
        obs = [KB.DSem(kb, "wos") for _ in range(2)]
        obfree = [None, None]
        stt = {}

        def evac(di, ti, c0, n, bank, pt):
            r = di % 2
            if ti == 0:
                kb.wait("sp", rbfree[r])
                stt["rl"] = kb.dma("sp", rbs[r], rb[r][:, :], h0T[di * 128:(di + 1) * 128, c_lo:c_lo + ncol])
                kb.wait("dve", obfree[r])
            kb.wait("dve", pt)
            kb.wait("dve", stt["rl"])
            t = kb.sig("dve", nc.vector.tensor_tensor(ob[r][:, c0:c0 + n], bank[:, 0:n], rb[r][:, c0:c0 + n], op=ALU.add))
            stt["t"] = t
            return t

        def chunk_done(di):
            r = di % 2
            rbfree[r] = stt["t"]
            kb.wait("sp", stt["t"])
            obfree[r] = kb.dma("sp", obs[r], h1T[di * 128:(di + 1) * 128, c_lo:c_lo + ncol], ob[r][:, :])
        proj(kb, ws, lambda kc, c0, n: oS[:, kc, c0:c0 + n], to, [[(i, KC, 0)] for i in range(KC)], tiles, brng, evac, chunk_done)
        kb.phase_end([obfree[0], obfree[1]])


def final_norm_phase(kb, cfg, C, hT, outT, gam_d, rstd):
    nc = kb.nc
    D, NT = cfg["D"], cfg["NT"]
    KC = D // 128
    stats_phase(kb, lambda k: hT[k * 128:(k + 1) * 128, :], KC, NT, rstd, C["onesD"], C["epsT"], C["t"])
    kb.phase_begin()
    gam = kb.sb("fgam", [128, KC], F32)
    cs = KB.DSem(kb, "const")
    tg = kb.dma("sp", cs, gam[:, :], gam_d[:, :])
    ob = [kb.sb("fob", [128, NT], F32) for _ in range(2)]
    obs = [KB.DSem(kb, "fos") for _ in range(2)]
    obfree = [None, None]
    kb.wait("dve", tg)
    def after(k, t):
        kb.wait("sp", t)
        obfree[k % 2] = kb.dma("sp", obs[k % 2], outT[k * 128:(k + 1) * 128, :], ob[k % 2][:, :])
    norm_apply(kb, lambda k: hT[k * 128:(k + 1) * 128, :], KC, NT, gam, rstd[:, 0:NT], None,
               lambda k: ob[k % 2][:, :], dst_free=lambda k: obfree[k % 2], after=after)
    kb.phase_end([obfree[0], obfree[1]])


class StageRing:
    def __init__(self, kb, name, n, shape, dt):
        self.kb, self.n = kb, n
        self.bufs = [kb.sb(name, shape, dt) for _ in range(n)]
        self.ds = [KB.DSem(kb, name + "s") for _ in range(n)]
        self.free = [None] * n
        self.i = -1

    def next(self, eng):
        self.i = (self.i + 1) % self.n
        self.kb.wait(eng, self.free[self.i])
        return self.bufs[self.i]

    def store(self, pairs, t):
        self.kb.wait("sp", t)
        for dram_ap, src_ap in pairs:
            self.free[self.i] = self.kb.dma("sp", self.ds[self.i], dram_ap, src_ap)

    def finals(self):
        return [f for f in self.free if f is not None]


class LoadRing:
    def __init__(self, kb, name, n, shapes):
        self.kb, self.n = kb, n
        self.bufs = [[kb.sb(name, sh, dt) for (sh, dt) in shapes] for _ in range(n)]
        self.ds = [KB.DSem(kb, name + "l") for _ in range(n)]
        self.free = [None] * n
        self.i = -1

    def load(self, fn):
        self.i = (self.i + 1) % self.n
        s = self.i
        self.kb.wait("sp", self.free[s])
        t = None
        for dst, src in fn(self.bufs[s]):
            t = self.kb.dma("sp", self.ds[s], dst, src)
        return s, self.bufs[s], t

    def release(self, s, t):
        self.free[s] = t


def attn_pre_phase(kb, cfg, cq_src, ckv_src, wq_d, wk_d, wv_d, cosT_d, sinT_d, QnT, QrT, KnT, Vh):
    nc = kb.nc
    S, NH, QL = cfg["S"], cfg["NH"], cfg["QL"]
    NQ = QL // 128
    sm = float(192 ** -0.5)
    kb.phase_begin()
    wq = kb.sb("wq", [128, NQ, NH * 256], BF16)
    wk = kb.sb("wk", [128, 4, NH * 128], BF16)
    wv = kb.sb("wv", [128, 4, NH * 128], BF16)
    cs = KB.DSem(kb, "wl")
    kb.dma("pool", cs, wq[:, :, :], wq_d[:, :, :])
    kb.dma("pool", cs, wk[:, :, :], wk_d[:, :, :])
    tw = kb.dma("pool", cs, wv[:, :, :], wv_d[:, :, :])
    inr = LoadRing(kb, "pin", 2, [([128, NQ, 512], BF16), ([128, 4, 512], BF16), ([64, 512], F32), ([64, 512], F32)])
    qn_o = StageRing(kb, "qno", 2, [128, 512], BF16)
    qr_o = StageRing(kb, "qro", 2, [64, 512], BF16)
    kn_o = StageRing(kb, "kno", 2, [128, 512], BF16)
    v_o = StageRing(kb, "vo", 2, [128, NH * 128], BF16)
    ta_ = [kb.sb("qra", [64, 512], F32) for _ in range(2)]
    tb_ = [kb.sb("qrb", [64, 512], F32) for _ in range(2)]
    tfree = [None, None]
    brng = BankRing(kb, list(range(8)))
    kb.wait("pe", tw)
    nqr = 0
    pending = None
    for it in range(S // 512):
        c0 = it * 512

        def ld(bufs, c0=c0):
            cq, ckv, co, si = bufs
            return [(cq[:, :, :], cq_src(c0).rearrange("(k p) c -> p k c", p=128)),
                    (ckv[:, :, :], ckv_src(c0).rearrange("(k p) c -> p k c", p=128)),
                    (co[:, :], cosT_d[:, c0:c0 + 512]), (si[:, :], sinT_d[:, c0:c0 + 512])]
        slot, (cq, ckv, co, si), tl = inr.load(ld)
        kb.wait("pe", tl)
        for h in range(NH):
            b, bank, bf = brng.next()
            kb.wait("pe", bf)
            for kc in range(NQ):
                ins = nc.tensor.matmul(bank[:, :], wq[:, kc, h * 256:h * 256 + 128], cq[:, kc, :], start=(kc == 0), stop=(kc == NQ - 1))
            tp = kb.sig("pe", ins)
            ob = qn_o.next("act")
            kb.wait("act", tp)
            t = kb.sig("act", nc.scalar.activation(ob[:, :], bank[:, :], AF.Copy, scale=sm))
            brng.release(b, t)
            qn_o.store([(QnT[h, :, c0:c0 + 512], ob[:, :])], t)
            r = nqr % 2
            nqr += 1
            tt = []
            for which, dst in ((0, ta_[r]), (1, tb_[r])):
                b, bank, bf = brng.next()
                kb.wait("pe", bf)
                o0 = h * 256 + 128 + 64 * which
                for kc in range(NQ):
                    ins = nc.tensor.matmul(bank[0:64, :], wq[:, kc, o0:o0 + 64], cq[:, kc, :], start=(kc == 0), stop=(kc == NQ - 1))
                tp = kb.sig("pe", ins)
                kb.wait("act", tp)
                kb.wait("act", tfree[r])
                t = kb.sig("act", nc.scalar.activation(dst[:, :], bank[0:64, :], AF.Copy, scale=sm))
                brng.release(b, t)
                tt.append(t)
            kb.wait("dve", tt[1])
            kb.wait("dve", tl)
            nc.vector.tensor_tensor(ta_[r][:, :], ta_[r][:, :], co[:, :], op=ALU.mult)
            t = kb.sig("dve", nc.vector.tensor_tensor(tb_[r][:, :], tb_[r][:, :], si[:, :], op=ALU.mult))
            kb.wait("dve", t)
            ob = qr_o.next("dve")
            t = kb.sig("dve", nc.vector.tensor_tensor(ob[:, :], ta_[r][:, :], tb_[r][:, :], op=ALU.add))
            tfree[r] = t
            qr_o.store([(QrT[h, :, c0:c0 + 512], ob[:, :])], t)
            last_dve = t
            b, bank, bf = brng.next()
            kb.wait("pe", bf)
            for kc in range(4):
                ins = nc.tensor.matmul(bank[:, :], wk[:, kc, h * 128:(h + 1) * 128], ckv[:, kc, :], start=(kc == 0), stop=(kc == 3))
            tp = kb.sig("pe", ins)
            ob = kn_o.next("act")
            kb.wait("act", tp)
            t = kb.sig("act", nc.scalar.copy(ob[:, :], bank[:, :]))
            brng.release(b, t)
            kn_o.store([(KnT[h, :, c0:c0 + 512], ob[:, :])], t)
        for j in range(4):
            b, bank, bf = brng.next()
            kb.wait("pe", bf)
            for kc in range(4):
                ins = nc.tensor.matmul(bank[:, 0:NH * 128], ckv[:, kc, j * 128:(j + 1) * 128], wv[:, kc, :], start=(kc == 0), stop=(kc == 3))
            tp = kb.sig("pe", ins)
            ob = v_o.next("act")
            kb.wait("act", tp)
            t = kb.sig("act", nc.scalar.copy(ob[:, :], bank[:, 0:NH * 128]))
            brng.release(b, t)
            v_o.store([(Vh[h, :, it * 4 + j, :], ob[:, h * 128:(h + 1) * 128]) for h in range(NH)], t)
        inr.release(slot, [("e", "pe", kb.cnt["pe"]), last_dve])
    kb.phase_end(qn_o.finals() + qr_o.finals() + kn_o.finals() + v_o.finals())


POOL_SHARE = 0


def attn_phase(kb, cfg, QnT, QrT, KnT, Vh, kr_src, tri_d, o_dst):
    nc = kb.nc
    S, NH = cfg["S"], cfg["NH"]
    NQB = S // 512
    for h in range(NH):
        kb.phase_begin()
        KT = kb.sb("KT", [128, S], BF16)
        KR = kb.sb("KR", [128, S], BF16)
        VS = kb.sb("VS", [128, S // 128, 128], BF16)
        ones = kb.sb("aones", [128, 128], BF16)
        tri = kb.sb("tri", [128, 128], BF16)
        rec = kb.sb("rec", [128, 512], F32)
        onesf = kb.sb("aonesf", [128, 128], F32)
        nc.vector.memset(onesf[:, :], 1.0)
        accs = [[kb.sb("lacc", [128, 512], F32) for _ in range(3)] for _ in range(2)]
        accfree = [None, None]
        recfree = [None]
        nc.vector.memset(KR[64:128, :], 0.0)
        t1 = kb.sig("dve", nc.vector.memset(ones[:, :], 1.0))
        cs = KB.DSem(kb, "kvl")
        kb.dma("sp", cs, tri[:, :], tri_d[:, :])
        npc = 4 if S >= 2048 else 1
        w = S // npc
        for p in range(npc):
            kb.dma("sp", cs, KT[:, p * w:(p + 1) * w], KnT[h, :, p * w:(p + 1) * w])
            tkv = kb.dma("sp", cs, VS[:, p * (w // 128):(p + 1) * (w // 128), :], Vh[h, :, p * (w // 128):(p + 1) * (w // 128), :])
        for (c_a, c_b, ap) in kr_src:
            tkv = kb.dma("sp", cs, KR[0:64, c_a:c_b], ap)
        kb.wait("pe", tkv)
        kb.wait("pe", t1)
        kb.wait("dve", tkv)
        qring = LoadRing(kb, "qb", 3, [([128, 512], BF16), ([128, 512], BF16)])
        for bufs_ in qring.bufs:
            t1 = kb.sig("dve", nc.vector.memset(bufs_[1][64:128, :], 0.0))
        kb.wait("pe", t1)
        pb = [kb.sb("pb", [128, 512], BF16) for _ in range(4)]
        pfree = [None] * 4
        sfree = [None] * 4
        o_st = StageRing(kb, "ao", 2, [128, 512], BF16)
        olfree = [None, None]
        gi = 0

        def qload(qb):
            return qring.load(lambda bufs: [(bufs[0][:, :], QnT[h, :, qb * 512:(qb + 1) * 512]),
                                            (bufs[1][0:64, :], QrT[h, :, qb * 512:(qb + 1) * 512])])
        nxt = qload(0)
        for qb in range(NQB):
            qslot, (qn, qr), tq = nxt
            if qb + 1 < NQB:
                nxt = qload(qb + 1)
            O = kb.banks[4 + 2 * (qb % 2)]
            L = kb.banks[5 + 2 * (qb % 2)]
            items = [(kt, None) for kt in range(4 * qb)] + [(4 * qb + j, j) for j in range(4)]
            n = len(items)
            ready = {}
            kb.wait("pe", tq)
            acc = accs[qb % 2]
            kb.wait("pool", accfree[qb % 2])
            nc.gpsimd.memset(acc[0][:, :], 0.0)
            nc.gpsimd.memset(acc[1][:, :], 0.0)
            tms = kb.sig("pool", nc.gpsimd.memset(acc[2][:, :], 0.0))
            acct = [tms, tms, tms]
            dcnt = [0]

            def emitS(i, g):
                kt, j = items[i]
                off = 0 if j is None else 128 * j
                sb_ = g % 4
                bank = kb.banks[sb_]
                kb.wait("pe", sfree[sb_])
                nc.tensor.matmul(bank[:, off:512], KT[:, kt * 128:(kt + 1) * 128], qn[:, off:512], start=True, stop=False)
                ts = kb.sig("pe", nc.tensor.matmul(bank[:, off:512], KR[:, kt * 128:(kt + 1) * 128], qr[:, off:512],
                                                   start=False, stop=True))
                kb.wait("act", ts)
                kb.wait("act", pfree[sb_])
                te = kb.sig("act", nc.scalar.activation(pb[sb_][:, off:512], bank[:, off:512], AF.Exp))
                sfree[sb_] = te
                if j is not None:
                    kb.wait("dve", te)
                    te = kb.sig("dve", nc.vector.tensor_tensor(pb[sb_][:, off:off + 128], pb[sb_][:, off:off + 128],
                                                               tri[:, :], op=ALU.mult))
                if j is None and POOL_SHARE and i % POOL_SHARE == POOL_SHARE - 1:
                    a_ = 2
                    kb.wait("pool", te)
                    kb.wait("pool", acct[a_])
                    acct[a_] = kb.sig("pool", nc.gpsimd.tensor_tensor(acc[a_][:, off:512], acc[a_][:, off:512], pb[sb_][:, off:512], op=ALU.add))
                else:
                    a_ = dcnt[0] % 2
                    dcnt[0] += 1
                    kb.wait("dve", te)
                    kb.wait("dve", acct[a_])
                    acct[a_] = kb.sig("dve", nc.vector.tensor_tensor(acc[a_][:, off:512], acc[a_][:, off:512], pb[sb_][:, off:512], op=ALU.add))
                ready[i] = (te, sb_, off, kt, acct[a_])

            def emitPV(i):
                te, sb_, off, kt, ta_ = ready[i]
                kb.wait("pe", te)
                if i == 0:
                    kb.wait("pe", olfree[qb % 2])
                tp = kb.sig("pe", nc.tensor.matmul(O[:, off:512], VS[:, kt, :], pb[sb_][:, off:512], start=(i == 0), stop=(i == n - 1)))
                pfree[sb_] = [tp, ta_]
                return tp
            LA = 2
            for i in range(min(LA, n)):
                emitS(i, gi + i)
            tp = None
            for i in range(n):
                if i + LA < n:
                    emitS(i + LA, gi + i + LA)
                tp = emitPV(i)
            gi += n
            qring.release(qslot, tp)
            for a_ in range(3):
                kb.wait("pe", acct[a_])
            nc.tensor.matmul(L[:, :], onesf[:, :], acc[0][:, :], start=True, stop=False)
            nc.tensor.matmul(L[:, :], onesf[:, :], acc[1][:, :], start=False, stop=False)
            tp = kb.sig("pe", nc.tensor.matmul(L[:, :], onesf[:, :], acc[2][:, :], start=False, stop=True))
            accfree[qb % 2] = tp
            kb.wait("act", tp)
            kb.wait("act", recfree[0])
            tr_ = kb.sig("act", nc.scalar.activation(rec[:, :], L[:, :], AF.Ln))
            kb.wait("act", tr_)
            tr_ = kb.sig("act", nc.scalar.activation(rec[:, :], rec[:, :], AF.Exp, scale=-1.0))
            kb.wait("dve", tr_)
            ob = o_st.next("dve")
            t = kb.sig("dve", nc.vector.tensor_tensor(ob[:, :], O[:, :], rec[:, :], op=ALU.mult))
            recfree[0] = t
            olfree[qb % 2] = t
            o_st.store([(o_dst(h, qb), ob[:, :])], t)
        kb.phase_end(o_st.finals())


def consts2(kb, cfg):
    C = consts(kb, cfg["D"])
    if cfg["QL"] != 1024:
        C["onesQ"] = kb.persist("onesQ", [128, 128], F32)
        C["t"] = kb.sig("dve", kb.nc.vector.memset(C["onesQ"][:, :], 1.0 / cfg["QL"]))
    return C


def ffn_inputs(nc, cfg, pfx):
    D, DFF = cfg["D"], cfg["DFF"]
    KC, NFC = D // 128, DFF // 128
    w_in_d = nc.dram_tensor(pfx + "w_in", [2 * NFC, 128, KC, 128], F32, kind="ExternalInput").ap()
    w_out_d = nc.dram_tensor(pfx + "w_out", [KC, 128, NFC, 128], F32, kind="ExternalInput").ap()
    cw_d = nc.dram_tensor(pfx + "cw", [128, 2 * NFC, 4], F32, kind="ExternalInput").ap()
    gam_d = nc.dram_tensor(pfx + "gam", [128, KC], F32, kind="ExternalInput").ap()
    return w_in_d, w_out_d, cw_d, gam_d


def build_A(cfg, debug=False):
    kb = KB()
    nc = kb.nc
    D, DFF, NT, QL = cfg["D"], cfg["DFF"], cfg["NT"], cfg["QL"]
    KC, NFC, NQ = D // 128, DFF // 128, QL // 128
    CG = KC // 4
    xT = nc.dram_tensor("xT", [D, HALO_A + NT], F32, kind="ExternalInput").ap()
    invc = nc.dram_tensor("invc", [4, 128, HALO_A + NT], F32, kind="ExternalInput").ap()
    w_pool = nc.dram_tensor("w_pool", [4 * CG, 128, CG, 128], F32, kind="ExternalInput").ap()
    a_gam = nc.dram_tensor("a_gam", [128, KC], F32, kind="ExternalInput").ap()
    a_asc = nc.dram_tensor("a_asc", [128, KC], F32, kind="ExternalInput").ap()
    fw = ffn_inputs(nc, cfg, "f0_")
    w_dkv = nc.dram_tensor("w_dkv", [6, 128, KC, 128], F32, kind="ExternalInput").ap()
    w_dq = nc.dram_tensor("w_dq", [NQ, 128, KC, 128], F32, kind="ExternalInput").ap()
    vec = nc.dram_tensor("vec", [128, 2 * KC + 4 + NQ], F32, kind="ExternalInput").ap()
    cosT = nc.dram_tensor("cosT", [64, NT], F32, kind="ExternalInput").ap()
    sinT = nc.dram_tensor("sinT", [64, NT], F32, kind="ExternalInput").ap()
    h0T = nc.dram_tensor("h0T", [D, NT], F32, kind="ExternalOutput").ap()
    ckvn = nc.dram_tensor("ckvn", [512, NT], BF16, kind="ExternalOutput").ap()
    krope = nc.dram_tensor("krope", [64, NT], BF16, kind="ExternalOutput").ap()
    cqn = nc.dram_tensor("cqn", [QL, NT], BF16, kind="ExternalOutput").ap()
    hmid = nc.dram_tensor("hmid", [D, 2 + NT], F32, kind="ExternalOutput" if debug else "Internal").ap()
    gscr = nc.dram_tensor("gscr", [NT // 512, 128, NFC, 512], BF16).ap()
    C = consts2(kb, cfg)
    rstd = kb.persist("rstd", [128, HALO_A + NT], F32)
    pool_phase(kb, cfg, C, xT, hmid, w_pool, a_gam, a_asc, invc, rstd)
    ffn_phase(kb, cfg, hmid, h0T, gscr, *fw, TS=min(1024, NT))
    latent_phase(kb, cfg, C, h0T, w_dkv, w_dq, vec, cosT, sinT, ckvn, krope, cqn, rstd, TS=min(1024, NT))
    kb.es.close()
    return nc


def build_B(cfg):
    kb = KB()
    nc = kb.nc
    S, NH, QL = cfg["S"], cfg["NH"], cfg["QL"]
    NQ = QL // 128
    cqnT = nc.dram_tensor("cqnT", [QL, S], BF16, kind="ExternalInput").ap()
    ckvnT = nc.dram_tensor("ckvnT", [512, S], BF16, kind="ExternalInput").ap()
    kropeT = nc.dram_tensor("kropeT", [64, S], BF16, kind="ExternalInput").ap()
    wq = nc.dram_tensor("wq", [128, NQ, NH * 256], F32, kind="ExternalInput").ap()
    wk = nc.dram_tensor("wk", [128, 4, NH * 128], F32, kind="ExternalInput").ap()
    wv = nc.dram_tensor("wv", [128, 4, NH * 128], F32, kind="ExternalInput").ap()
    cosT = nc.dram_tensor("cosT", [64, S], F32, kind="ExternalInput").ap()
    sinT = nc.dram_tensor("sinT", [64, S], F32, kind="ExternalInput").ap()
    tri = nc.dram_tensor("tri", [128, 128], BF16, kind="ExternalInput").ap()
    oT = nc.dram_tensor("oT", [NH * 128, S], BF16, kind="ExternalOutput").ap()
    QnT = nc.dram_tensor("QnT", [NH, 128, S], BF16).ap()
    QrT = nc.dram_tensor("QrT", [NH, 64, S], BF16).ap()
    KnT = nc.dram_tensor("KnT", [NH, 128, S], BF16).ap()
    Vh = nc.dram_tensor("Vh", [NH, 128, S // 128, 128], BF16).ap()
    attn_pre_phase(kb, cfg, lambda c0: cqnT[:, c0:c0 + 512], lambda c0: ckvnT[:, c0:c0 + 512], wq, wk, wv, cosT, sinT, QnT, QrT, KnT, Vh)
    attn_phase(kb, cfg, QnT, QrT, KnT, Vh, [(0, S, kropeT[:, :])], tri,
               lambda h, qb: oT[h * 128:(h + 1) * 128, qb * 512:(qb + 1) * 512])
    kb.es.close()
    return nc


def build_C(cfg):
    kb = KB()
    nc = kb.nc
    D, DFF, NT = cfg["D"], cfg["DFF"], cfg["NT"]
    KC, NFC = D // 128, DFF // 128
    oT = nc.dram_tensor("oT", [D, 2 + NT], BF16, kind="ExternalInput").ap()
    h0T = nc.dram_tensor("h0T", [D, 2 + NT], F32, kind="ExternalInput").ap()
    w_o = nc.dram_tensor("w_o", [KC, 128, KC, 128], F32, kind="ExternalInput").ap()
    fw = ffn_inputs(nc, cfg, "f1_")
    fgam = nc.dram_tensor("fgam", [128, KC], F32, kind="ExternalInput").ap()
    outT = nc.dram_tensor("outT", [D, NT], F32, kind="ExternalOutput").ap()
    h1T = nc.dram_tensor("h1T", [D, 2 + NT], F32).ap()
    h2T = nc.dram_tensor("h2T", [D, NT], F32).ap()
    gscr = nc.dram_tensor("gscr", [NT // 512, 128, NFC, 512], BF16).ap()
    C = consts(kb, D)
    rstd = kb.persist("rstd", [128, NT], F32)
    wo_phase(kb, cfg, lambda k0, k1, c_lo, ncol: oT[k0 * 128:k1 * 128, c_lo:c_lo + ncol], h0T, h1T, w_o, TS=min(1024, NT))
    ffn_phase(kb, cfg, h1T, h2T, gscr, *fw, TS=min(1024, NT))
    final_norm_phase(kb, cfg, C, h2T, outT, fgam, rstd)
    kb.es.close()
    return nc


def cc_allgather(kb, csem, src, dst, nranks):
    ins = kb.nc.gpsimd.collective_compute("AllGather", ALU.bypass, [list(range(nranks))], ins=[src], outs=[dst])
    ins.then_inc(csem.sem)
    csem.cnt += 1
    return ("d", csem, csem.cnt)


def build_F(cfg):
    kb = KB()
    nc = kb.nc
    D, DFF, NT, QL, S, NH = cfg["D"], cfg["DFF"], cfg["NT"], cfg["QL"], cfg["S"], cfg["NH"]
    KC, NFC, NQ = D // 128, DFF // 128, QL // 128
    CG = KC // 4
    NR = S // NT
    LR = QL + 576
    TS = min(1024, NT)
    ext = lambda name, shape, dt: nc.dram_tensor(name, shape, dt, kind="ExternalInput").ap()
    xT = ext("xT", [D, HALO_A + NT], F32)
    invc = ext("invc", [4, 128, HALO_A + NT], F32)
    w_pool = ext("w_pool", [4 * CG, 128, CG, 128], F32)
    a_gam = ext("a_gam", [128, KC], F32)
    a_asc = ext("a_asc", [128, KC], F32)
    fw0 = ffn_inputs(nc, cfg, "f0_")
    w_dkv = ext("w_dkv", [6, 128, KC, 128], F32)
    w_dq = ext("w_dq", [NQ, 128, KC, 128], F32)
    vec = ext("vec", [128, 2 * KC + 4 + NQ], F32)
    cosT = ext("cosT", [64, NT], F32)
    sinT = ext("sinT", [64, NT], F32)
    wq = ext("wq", [128, NQ, NH * 256], F32)
    wk = ext("wk", [128, 4, NH * 128], F32)
    wv = ext("wv", [128, 4, NH * 128], F32)
    cosF = ext("cosF", [64, S], F32)
    sinF = ext("sinF", [64, S], F32)
    tri = ext("tri", [128, 128], BF16)
    w_o = ext("w_o", [KC, 128, KC, 128], F32)
    fw1 = ffn_inputs(nc, cfg, "f1_")
    fgam = ext("fgam", [128, KC], F32)
    outT = nc.dram_tensor("outT", [D, NT], F32, kind="ExternalOutput").ap()
    it = lambda name, shape, dt: nc.dram_tensor(name, shape, dt).ap()
    hmid = it("hmid", [D, 2 + NT], F32)
    gscr = it("gscr", [NT // 512, 128, NFC, 512], BF16)
    h0x = it("h0x", [D, 2 + NT], F32)
    lat_local = it("lat_local", [LR, NT], BF16)
    lat_all = it("lat_all", [NR * LR, NT], BF16)
    hh_local = it("hh_local", [D, 2], F32)
    hh_all = it("hh_all", [NR * D, 2], F32)
    hh_pad = it("hh_pad", [(NR + 1) * D, 2], F32)
    o_local = it("o_local", [NH * 128, 2 + S], BF16)
    o_all = it("o_all", [NR * NH * 128, 2 + S], BF16)
    QnT = it("QnT", [NH, 128, S], BF16)
    QrT = it("QrT", [NH, 64, S], BF16)
    KnT = it("KnT", [NH, 128, S], BF16)
    Vh = it("Vh", [NH, 128, S // 128, 128], BF16)
    h1T = it("h1T", [D, 2 + NT], F32)
    h2T = it("h2T", [D, NT], F32)
    pid = nc.sync.partition_id()
    C = consts2(kb, cfg)
    rstd = kb.persist("rstd", [128, HALO_A + NT], F32)
    h0own = h0x[:, 2:2 + NT]
    pool_phase(kb, cfg, C, xT, hmid, w_pool, a_gam, a_asc, invc, rstd)
    ffn_phase(kb, cfg, hmid, h0own, gscr, *fw0, TS=TS)
    latent_phase(kb, cfg, C, h0own, w_dkv, w_dq, vec, cosT, sinT, lat_local[QL:QL + 512, :], lat_local[QL + 512:LR, :],
                 lat_local[0:QL, :], rstd, TS=TS)
    kb.phase_begin()
    z = kb.sb("zero", [128, max(2 * KC, 8)], F32)
    tz = kb.sig("dve", nc.vector.memset(z[:, :], 0.0))
    zb = kb.sb("zerob", [128, 8], BF16)
    tz = kb.sig("dve", nc.vector.memset(zb[:, :], 0.0))
    s1 = KB.DSem(kb, "x1")
    kb.wait("sp", tz)
    th = kb.dma("sp", s1, hh_local[:, :], h0x[:, NT:NT + 2])
    tzz = kb.dma("sp", s1, hh_pad[0:D, :].rearrange("(p k) c -> p (k c)", p=128), z[:, 0:2 * KC])
    for hq in range(NH):
        tzz = kb.dma("sp", s1, o_local[hq * 128:(hq + 1) * 128, 0:2], zb[:, 0:2])
    kb.wait("pool", th)
    ccs = KB.DSem(kb, "cc")
    cc_allgather(kb, ccs, lat_local[:, :], lat_all[:, :], NR)
    tcc = cc_allgather(kb, ccs, hh_local[:, :], hh_all[:, :], NR)
    kb.wait("sp", tcc)
    kb.wait("sp", tzz)
    t = kb.dma("sp", s1, hh_pad[D:(NR + 1) * D, :], hh_all[:, :])
    kb.wait("sp", t)
    t = kb.dma("sp", s1, h0x[:, 0:2], hh_pad[bass.ds(pid * D, D), :])
    kb.phase_end([t])
    cq_src = lambda c0: lat_all[(c0 // NT) * LR:(c0 // NT) * LR + QL, c0 % NT:c0 % NT + 512]
    ckv_src = lambda c0: lat_all[(c0 // NT) * LR + QL:(c0 // NT) * LR + QL + 512, c0 % NT:c0 % NT + 512]
    attn_pre_phase(kb, cfg, cq_src, ckv_src, wq, wk, wv, cosF, sinF, QnT, QrT, KnT, Vh)
    kr_src = [(r * NT, (r + 1) * NT, lat_all[r * LR + QL + 512:(r + 1) * LR, :]) for r in range(NR)]
    attn_phase(kb, cfg, QnT, QrT, KnT, Vh, kr_src, tri,
               lambda h, qb: o_local[h * 128:(h + 1) * 128, 2 + qb * 512:2 + (qb + 1) * 512])
    kb.phase_begin()
    ccs2 = KB.DSem(kb, "cc2")
    tcc = cc_allgather(kb, ccs2, o_local[:, :], o_all[:, :], NR)
    kb.phase_end([tcc])
    wo_phase(kb, cfg, lambda k0, k1, c_lo, ncol: o_all[k0 * 128:k1 * 128, bass.ds(pid * NT + c_lo, ncol)], h0x, h1T, w_o, TS=TS)
    ffn_phase(kb, cfg, h1T, h2T, gscr, *fw1, TS=TS)
    final_norm_phase(kb, cfg, C, h2T, outT, fgam, rstd)
    kb.es.close()
    return nc


import ml_dtypes
BF = ml_dtypes.bfloat16


def lay_slots(w):
    K, N = w.shape
    a = w.reshape(K // 128, 128, N // 128, 128)
    return np.ascontiguousarray(a.transpose(2, 1, 0, 3))


def lay_vec(g):
    return np.ascontiguousarray(np.asarray(g, np.float32).reshape(-1, 128).T)


def lay_ffn(w_in, w_out, conv_w, conv_b, gam, pfx):
    D, F2 = w_in.shape
    NFC = F2 // 256
    s = lay_slots(w_in)
    order = [fc + gv * NFC for fc in range(NFC) for gv in range(2)]
    cw = np.concatenate([conv_w, conv_b[None]], 0).reshape(4, 2 * NFC, 128)
    return {pfx + "w_in": np.ascontiguousarray(s[order]), pfx + "w_out": lay_slots(w_out),
            pfx + "cw": np.ascontiguousarray(cw.transpose(2, 1, 0)), pfx + "gam": lay_vec(gam)}


def rope_tables(pos):
    inv_freq = (np.float32(10000.0) ** (-(np.arange(0, 64, 2, dtype=np.float32)) / np.float32(64))).astype(np.float32)
    ang = (pos.astype(np.float32)[:, None] * inv_freq[None, :]).astype(np.float32)
    c, s = np.cos(ang).astype(np.float32), np.sin(ang).astype(np.float32)
    cosT = np.ascontiguousarray(np.concatenate([c, c], 1).T)
    sinT = np.ascontiguousarray(np.concatenate([-s, s], 1).T)
    return cosT, sinT


def halo_cols(fullT, c, NT, halo):
    F = fullT.shape[0]
    out = np.zeros((F, halo + NT), fullT.dtype)
    lo = c * NT - halo
    if lo < 0:
        out[:, -lo:] = fullT[:, 0:(c + 1) * NT]
    else:
        out[:] = fullT[:, lo:(c + 1) * NT]
    return out


def inputs_A(inp, c, NT):
    x = np.asarray(inp["x"])[0]
    D = x.shape[1]
    KC = D // 128
    CG = KC // 4
    m = {}
    m["xT"] = halo_cols(np.ascontiguousarray(x.T), c, NT, HALO_A)
    t = c * NT - HALO_A + np.arange(HALO_A + NT)
    iv = np.stack([np.where(t >= 0, 1.0 / np.minimum(np.maximum(t, 0) + 1, w), 1.0 / w) for w in POOL_WINDOWS]).astype(np.float32)
    m["invc"] = np.ascontiguousarray(np.broadcast_to(iv[:, None, :], (4, 128, HALO_A + NT)))
    m["w_pool"] = np.concatenate([lay_slots(np.asarray(inp["a_pool_w"])[0, g]) for g in range(4)], 0)
    m["a_gam"] = lay_vec(np.asarray(inp["a_norm"])[0])
    m["a_asc"] = lay_vec(np.asarray(inp["a_scale"])[0])
    m.update(lay_ffn(np.asarray(inp["ffn_w_in"])[0], np.asarray(inp["ffn_w_out"])[0], np.asarray(inp["ffn_conv_w"])[0],
                     np.asarray(inp["ffn_conv_b"])[0], np.asarray(inp["ffn_norm"])[0], "f0_"))
    wd = np.asarray(inp["w_dkv"])
    z = np.zeros((D, 64), np.float32)
    wd6 = np.concatenate([wd[:, :512], wd[:, 512:576], z, wd[:, 544:576], wd[:, 512:544], z], 1)
    m["w_dkv"] = lay_slots(wd6)
    m["w_dq"] = lay_slots(np.asarray(inp["w_dq"])[0])
    m["vec"] = np.ascontiguousarray(np.concatenate([lay_vec(inp["kv_norm"]), lay_vec(np.asarray(inp["b_norm"])[0]),
                                                    lay_vec(inp["kv_lat_norm"]), lay_vec(np.asarray(inp["q_lat_norm"])[0])], 1))
    m["cosT"], m["sinT"] = rope_tables(c * NT + np.arange(NT))
    return m


def inputs_B(inp, c, NH, cqnT, ckvnT, kropeT, S):
    m = {"cqnT": cqnT, "ckvnT": ckvnT, "kropeT": kropeT}
    wuq = np.asarray(inp["w_uq"])[0]
    QL = wuq.shape[0]
    hs = range(c * NH, (c + 1) * NH)
    cols = []
    for h in hs:
        cols += [wuq[:, h, 0:128], wuq[:, h, 128:192], wuq[:, h, 160:192], wuq[:, h, 128:160]]
    wq = np.concatenate(cols, 1)
    m["wq"] = np.ascontiguousarray(wq.reshape(QL // 128, 128, NH * 256).transpose(1, 0, 2))
    wukv = np.asarray(inp["w_ukv"])
    wk = np.concatenate([wukv[:, h, 0:128] for h in hs], 1)
    wv = np.concatenate([wukv[:, h, 128:256] for h in hs], 1)
    m["wk"] = np.ascontiguousarray(wk.reshape(4, 128, NH * 128).transpose(1, 0, 2))
    m["wv"] = np.ascontiguousarray(wv.reshape(4, 128, NH * 128).transpose(1, 0, 2))
    m["cosT"], m["sinT"] = rope_tables(np.arange(S))
    m["tri"] = (np.arange(128)[None, :] >= np.arange(128)[:, None]).astype(np.float32).astype(BF)
    return m


def inputs_C(inp, c, NT, oT_full, h0T_full):
    m = {"oT": halo_cols(oT_full, c, NT, 2), "h0T": halo_cols(h0T_full, c, NT, 2)}
    m["w_o"] = lay_slots(np.asarray(inp["w_o"])[0])
    m.update(lay_ffn(np.asarray(inp["ffn_w_in"])[1], np.asarray(inp["ffn_w_out"])[1], np.asarray(inp["ffn_conv_w"])[1],
                     np.asarray(inp["ffn_conv_b"])[1], np.asarray(inp["ffn_norm"])[1], "f1_"))
    m["fgam"] = lay_vec(inp["final_norm"])
    return m


from concourse.bass_utils import run_bass_kernel_spmd

CFG = dict(D=4096, DFF=11008, NT=2048, QL=1024, S=16384, NH=4)
NCORES = 8


def shared_A(inp):
    m = inputs_A(inp, 0, CFG["NT"])
    for k in ("xT", "invc", "cosT", "sinT"):
        m.pop(k)
    return m


def percore_A(inp, c, NT):
    x = np.asarray(inp["x"])[0]
    m = {}
    lo = max(0, c * NT - HALO_A)
    slab = np.ascontiguousarray(x[lo:(c + 1) * NT].T)
    xT = np.zeros((x.shape[1], HALO_A + NT), np.float32)
    xT[:, HALO_A + NT - slab.shape[1]:] = slab
    m["xT"] = xT
    t = c * NT - HALO_A + np.arange(HALO_A + NT)
    iv = np.stack([np.where(t >= 0, 1.0 / np.minimum(np.maximum(t, 0) + 1, w), 1.0 / w) for w in POOL_WINDOWS]).astype(np.float32)
    m["invc"] = np.ascontiguousarray(np.broadcast_to(iv[:, None, :], (4, 128, HALO_A + NT)))
    m["cosT"], m["sinT"] = rope_tables(c * NT + np.arange(NT))
    return m


def kernel(**inputs):
    cfg = CFG
    NT, S, NH = cfg["NT"], cfg["S"], cfg["NH"]
    cores = list(range(NCORES))
    shA = shared_A(inputs)
    mapsA = [dict(shA, **percore_A(inputs, c, NT)) for c in cores]
    resA = run_bass_kernel_spmd(build_A(cfg), mapsA, core_ids=cores).results
    del mapsA, shA
    h0T_full = np.concatenate([resA[c]["h0T"] for c in cores], 1)
    cqnT = np.ascontiguousarray(np.concatenate([resA[c]["cqn"] for c in cores], 1))
    ckvnT = np.ascontiguousarray(np.concatenate([resA[c]["ckvn"] for c in cores], 1))
    kropeT = np.ascontiguousarray(np.concatenate([resA[c]["krope"] for c in cores], 1))
    del resA
    mapsB = [inputs_B(inputs, c, NH, cqnT, ckvnT, kropeT, S) for c in cores]
    resB = run_bass_kernel_spmd(build_B(cfg), mapsB, core_ids=cores).results
    oT_full = np.concatenate([resB[c]["oT"] for c in cores], 0)
    del mapsB, resB
    shC = inputs_C(inputs, 0, NT, oT_full, h0T_full)
    mapsC = []
    for c in cores:
        m = dict(shC)
        m["oT"] = halo_cols(oT_full, c, NT, 2)
        m["h0T"] = halo_cols(h0T_full, c, NT, 2)
        mapsC.append(m)
    resC = run_bass_kernel_spmd(build_C(cfg), mapsC, core_ids=cores).results
    outT = np.concatenate([resC[c]["outT"] for c in cores], 1)
    return np.ascontiguousarray(outT.T)[None].astype(np.float32)
```
